# Optimizing a Trainium2 kernel written in Bass

```python
import math, functools
import jax, jax.numpy as jnp
from jax import lax
import numpy as np

D_MODEL = 1024
BATCH = 8
SEQ = 2048
DEPTH = 2
DEC_BATCH = 128
DEC_SEQ = 8
PAST_LEN = 16384
PAGE_SIZE = 128

N_META = 16
CHUNK = 64
EPS = 1e-6
CONV_W = 4
N_BRANCH = 3
ML_HEADS = 4
ML_DQK = D_MODEL // 8
ML_DV = D_MODEL // 4
ML_QK = ML_HEADS * ML_DQK
ML_V = ML_HEADS * ML_DV
LRU_W = D_MODEL
LRU_BLOCKS = 16
LRU_BD = LRU_W // LRU_BLOCKS
LRU_C = 8.0
SSD_HEADS = 16
SSD_P = D_MODEL // SSD_HEADS
SSD_INNER = SSD_HEADS * SSD_P
SSD_GROUPS = 4
SSD_N = 128
SSD_CONV_CH = SSD_INNER + 2 * SSD_GROUPS * SSD_N
IN_SPLITS = (ML_QK, ML_QK, ML_V, ML_HEADS, ML_HEADS, ML_V, ML_V,
             LRU_W, LRU_W,
             SSD_INNER, SSD_CONV_CH, SSD_HEADS,
             N_BRANCH * D_MODEL)
N_IN = 2 * ML_QK + 3 * ML_V + 2 * ML_HEADS + 2 * LRU_W + SSD_INNER + SSD_CONV_CH + SSD_HEADS + N_BRANCH * D_MODEL

kernel_name = 'hybrid_mlstm_rglru_ssd_decode_step'


def rmsnorm(x, g):
    xf = x.astype(jnp.float32)
    return xf * lax.rsqrt(jnp.mean(xf * xf, axis=-1, keepdims=True) + EPS) * g.astype(jnp.float32)


def split_cols(u):
    offs = np.cumsum((0,) + IN_SPLITS)
    return [u[..., int(offs[j]):int(offs[j + 1])] for j in range(len(IN_SPLITS))]


def _chunk(t):
    return math.gcd(t, CHUNK)


def causal_dwconv(x, buf, w, b):
    t = x.shape[1]
    xp = jnp.concatenate([buf, x], axis=1)
    y = b + xp[:, 0:t] * w[0]
    for j in range(1, CONV_W):
        y = y + xp[:, j:j + t] * w[j]
    return y, xp[:, -(CONV_W - 1):]


def _segmented(scan_fn, n_lead, seqs, state):
    if n_lead == 0:
        return scan_fn(*seqs, *state)
    out0, *state = scan_fn(*(s[:, :n_lead] for s in seqs), *state)
    out1, *state = scan_fn(*(s[:, n_lead:] for s in seqs), *state)
    return (jnp.concatenate([out0, out1], axis=1), *state)


def mlstm_scan(q, k, v, logi, logf, c0, n0, m0):
    bsz, t, h, _ = q.shape
    L = _chunk(t)
    nc = t // L

    def to_chunks(a):
        return jnp.moveaxis(a.reshape((bsz, nc, L) + a.shape[2:]), 1, 0)

    xs = tuple(to_chunks(a) for a in (q, k, v, logi, logf))
    causal = jnp.tril(jnp.ones((L, L), bool))[None, :, :, None]

    def step(carry, inp):
        c, n, m = carry
        qc, kc, vc, li, lf = inp
        b = jnp.cumsum(lf, axis=1)
        d = jnp.where(causal, b[:, :, None, :] - b[:, None, :, :] + li[:, None, :, :], -jnp.inf)
        inter = b + m[:, None, :]
        mt = jnp.maximum(inter, jnp.max(d, axis=2))
        s = jnp.einsum('bthd,bshd->btsh', qc, kc) * jnp.exp(d - mt[:, :, None, :])
        w_inter = jnp.exp(inter - mt)
        num = jnp.einsum('btsh,bshv->bthv', s, vc) + w_inter[..., None] * jnp.einsum('bthd,bhdv->bthv', qc, c)
        den = jnp.sum(s, axis=2) + w_inter * jnp.einsum('bthd,bhd->bth', qc, n)
        hout = num / jnp.maximum(jnp.abs(den), jnp.exp(-mt))[..., None]
        bl = b[:, -1]
        g = bl[:, None, :] - b + li
        m_new = jnp.maximum(bl + m, jnp.max(g, axis=1))
        ws = jnp.exp(g - m_new[:, None, :])
        wc = jnp.exp(bl + m - m_new)
        c_new = wc[..., None, None] * c + jnp.einsum('bsh,bshd,bshv->bhdv', ws, kc, vc)
        n_new = wc[..., None] * n + jnp.einsum('bsh,bshd->bhd', ws, kc)
        return (c_new, n_new, m_new), hout

    (c, n, m), hs = lax.scan(step, (c0, n0, m0), xs)
    hout = jnp.moveaxis(hs, 0, 1).reshape(bsz, t, h, v.shape[-1])
    return hout, c, n, m


def ssd_scan(xh, dt, bm, cm, s0, a):
    bsz, t, h, p = xh.shape
    g = bm.shape[2]
    e = h // g
    L = _chunk(t)
    nc = t // L

    def to_chunks(v):
        return jnp.moveaxis(v.reshape((bsz, nc, L) + v.shape[2:]), 1, 0)

    xs = tuple(to_chunks(v) for v in (xh.reshape(bsz, t, g, e, p), dt.reshape(bsz, t, g, e), bm, cm))
    causal = jnp.tril(jnp.ones((L, L), bool))[None, :, :, None, None]
    ag = a.reshape(g, e)

    def step(s, inp):
        xc, dtc, bc, cc = inp
        cum = jnp.cumsum(dtc * ag, axis=1)
        seg = jnp.where(causal, cum[:, :, None] - cum[:, None, :], -jnp.inf)
        w = jnp.einsum('btgn,bsgn->btsg', cc, bc)[..., None] * jnp.exp(seg) * dtc[:, None]
        y = jnp.einsum('btsge,bsgep->btgep', w, xc)
        y = y + jnp.einsum('btgn,bgepn->btgep', cc, s) * jnp.exp(cum)[..., None]
        cl = cum[:, -1]
        ws = jnp.exp(cl[:, None] - cum) * dtc
        s_new = jnp.exp(cl)[..., None, None] * s + jnp.einsum('bsge,bsgep,bsgn->bgepn', ws, xc, bc)
        return s_new, y

    s, ys = lax.scan(step, s0.reshape(bsz, g, e, p, -1), xs)
    y = jnp.moveaxis(ys, 0, 1).reshape(bsz, t, h, p)
    return y, s.reshape(bsz, h, p, -1)


def rglru(x, h0, w_a, b_a, w_x, b_x, lam):
    bsz, t, w = x.shape
    xb = x.reshape(bsz, t, LRU_BLOCKS, LRU_BD)
    r = jax.nn.sigmoid(jnp.einsum('btnd,nde->btne', xb, w_a).reshape(bsz, t, w) + b_a)
    i = jax.nn.sigmoid(jnp.einsum('btnd,nde->btne', xb, w_x).reshape(bsz, t, w) + b_x)
    log_a = -LRU_C * r * jax.nn.softplus(-lam)
    a = jnp.exp(log_a)
    u = jnp.sqrt(-jnp.expm1(2.0 * log_a)) * (i * x)
    u = u.at[:, 0].add(a[:, 0] * h0)

    def comb(l, rr):
        a1, b1 = l
        a2, b2 = rr
        return a1 * a2, a2 * b1 + b2

    _, hs = lax.associative_scan(comb, (a, u), axis=1)
    return hs, hs[:, -1]


def mixer_layer(x, n_lead, st, w_in, norm_g, ml_f_bias, ml_norm_g, lru_conv_w, lru_conv_b,
                lru_w_a, lru_b_a, lru_w_x, lru_b_x, lru_lambda, ssd_conv_w, ssd_conv_b,
                ssd_dt_bias, ssd_a_log, ssd_d, ssd_norm_g, w_br_ml, w_br_lru, w_br_ssd, w_out):
    c0, n0, m0, hl0, cl0, hs0, cs0 = (s.astype(jnp.float32) for s in st)
    bsz, t, _ = x.shape
    u = jnp.einsum('btd,de->bte', rmsnorm(x, norm_g), w_in)
    (ml_q, ml_k, ml_v, ml_i, ml_f, ml_o, ml_z, lru_x, lru_z,
     ssd_z, ssd_xbc, ssd_dt, gates) = split_cols(u)

    q = ml_q.reshape(bsz, t, ML_HEADS, ML_DQK)
    k = ml_k.reshape(bsz, t, ML_HEADS, ML_DQK) * (ML_DQK ** -0.5)
    v = ml_v.reshape(bsz, t, ML_HEADS, ML_DV)
    logf = jax.nn.log_sigmoid(ml_f + ml_f_bias)
    h_ml, c1, n1, m1 = _segmented(mlstm_scan, n_lead, (q, k, v, ml_i, logf), (c0, n0, m0))
    y_ml = rmsnorm(h_ml, ml_norm_g).reshape(bsz, t, ML_V) * jax.nn.sigmoid(ml_o) * jax.nn.silu(ml_z)

    xc, cl1 = causal_dwconv(lru_x, cl0, lru_conv_w, lru_conv_b)
    h_lru, hl1 = rglru(xc, hl0, lru_w_a, lru_b_a, lru_w_x, lru_b_x, lru_lambda)
    y_lru = h_lru * jax.nn.silu(lru_z)

    xbc, cs1 = causal_dwconv(ssd_xbc, cs0, ssd_conv_w, ssd_conv_b)
    xbc = jax.nn.silu(xbc)
    xs = xbc[..., :SSD_INNER].reshape(bsz, t, SSD_HEADS, SSD_P)
    bm = xbc[..., SSD_INNER:SSD_INNER + SSD_GROUPS * SSD_N].reshape(bsz, t, SSD_GROUPS, SSD_N)
    cm = xbc[..., SSD_INNER + SSD_GROUPS * SSD_N:].reshape(bsz, t, SSD_GROUPS, SSD_N)
    dt = jax.nn.softplus(ssd_dt + ssd_dt_bias)
    a = -jnp.exp(ssd_a_log.astype(jnp.float32))
    y_s, hs1 = _segmented(functools.partial(ssd_scan, a=a), n_lead, (xs, dt, bm, cm), (hs0,))
    y_s = (y_s + ssd_d[:, None] * xs).reshape(bsz, t, SSD_INNER) * jax.nn.silu(ssd_z)
    y_ssd = rmsnorm(y_s, ssd_norm_g)

    gt = jax.nn.sigmoid(gates).reshape(bsz, t, N_BRANCH, D_MODEL)
    merged = (gt[:, :, 0] * jnp.einsum('btw,wd->btd', y_ml, w_br_ml)
              + gt[:, :, 1] * jnp.einsum('btw,wd->btd', y_lru, w_br_lru)
              + gt[:, :, 2] * jnp.einsum('btw,wd->btd', y_ssd, w_br_ssd))
    out = jnp.einsum('btd,de->bte', merged, w_out)
    x_new = (x.astype(jnp.float32) + out).astype(x.dtype)
    return x_new, (c1, n1, m1, hl1, cl1, hs1, cs1)


def run_trunk(x, n_lead, states, layer_params, final_norm_g):
    new = []
    for l in range(DEPTH):
        x, st = mixer_layer(x, n_lead, tuple(s[l] for s in states), *(p[l] for p in layer_params))
        new.append(st)
    y = rmsnorm(x, final_norm_g).astype(x.dtype)
    stacked = tuple(jnp.stack([new[l][j] for l in range(DEPTH)]) for j in range(len(states)))
    return y, stacked


def setup_inputs(seed: int = 0) -> dict:
    key = jax.random.key(seed)
    ks = iter(jax.random.split(key, 48))

    def nrm(shape, scale):
        return scale * jax.random.normal(next(ks), shape, jnp.float32)

    def uni(shape, lo, hi):
        return jax.random.uniform(next(ks), shape, jnp.float32, lo, hi)

    x_prompt = nrm((BATCH, SEQ, D_MODEL), 1.0)
    x_sample = nrm((DEC_BATCH, DEC_SEQ, D_MODEL), 1.0)
    state_mlstm_c = nrm((DEPTH, DEC_BATCH, ML_HEADS, ML_DQK, ML_DV), 0.1)
    state_mlstm_n = nrm((DEPTH, DEC_BATCH, ML_HEADS, ML_DQK), 0.1)
    state_mlstm_m = nrm((DEPTH, DEC_BATCH, ML_HEADS), 0.5)
    state_rglru_h = nrm((DEPTH, DEC_BATCH, LRU_W), 0.5)
    state_rglru_conv = nrm((DEPTH, DEC_BATCH, CONV_W - 1, LRU_W), 1.0)
    state_ssd_h = nrm((DEPTH, DEC_BATCH, SSD_HEADS, SSD_P, SSD_N), 0.1)
    state_ssd_conv = nrm((DEPTH, DEC_BATCH, CONV_W - 1, SSD_CONV_CH), 1.0)
    meta_tokens = nrm((N_META, D_MODEL), 1.0)
    w_in = nrm((DEPTH, D_MODEL, N_IN), D_MODEL ** -0.5)
    norm_g = 1.0 + nrm((DEPTH, D_MODEL), 0.02)
    ml_f_bias = jnp.linspace(3.0, 6.0, ML_HEADS, dtype=jnp.float32)[None] + nrm((DEPTH, ML_HEADS), 0.1)
    ml_norm_g = 1.0 + nrm((DEPTH, ML_HEADS, ML_DV), 0.02)
    lru_conv_w = nrm((DEPTH, CONV_W, LRU_W), CONV_W ** -0.5)
    lru_conv_b = nrm((DEPTH, LRU_W), 0.02)
    lru_w_a = nrm((DEPTH, LRU_BLOCKS, LRU_BD, LRU_BD), LRU_BD ** -0.5)
    lru_b_a = nrm((DEPTH, LRU_W), 0.02)
    lru_w_x = nrm((DEPTH, LRU_BLOCKS, LRU_BD, LRU_BD), LRU_BD ** -0.5)
    lru_b_x = nrm((DEPTH, LRU_W), 0.02)
    a_c = uni((DEPTH, LRU_W), 0.9, 0.999) ** (1.0 / LRU_C)
    lru_lambda = jnp.log(a_c) - jnp.log1p(-a_c)
    ssd_conv_w = nrm((DEPTH, CONV_W, SSD_CONV_CH), CONV_W ** -0.5)
    ssd_conv_b = nrm((DEPTH, SSD_CONV_CH), 0.02)
    dt0 = jnp.exp(uni((DEPTH, SSD_HEADS), math.log(1e-3), math.log(1e-1)))
    ssd_dt_bias = dt0 + jnp.log(-jnp.expm1(-dt0))
    ssd_a_log = jnp.log(uni((DEPTH, SSD_HEADS), 1.0, 16.0))
    ssd_d = 1.0 + nrm((DEPTH, SSD_HEADS), 0.1)
    ssd_norm_g = 1.0 + nrm((DEPTH, SSD_INNER), 0.02)
    w_br_ml = nrm((DEPTH, ML_V, D_MODEL), ML_V ** -0.5)
    w_br_lru = nrm((DEPTH, LRU_W, D_MODEL), LRU_W ** -0.5)
    w_br_ssd = nrm((DEPTH, SSD_INNER, D_MODEL), SSD_INNER ** -0.5)
    w_out = nrm((DEPTH, D_MODEL, D_MODEL), D_MODEL ** -0.5)
    final_norm_g = 1.0 + nrm((D_MODEL,), 0.02)
    return {'x_prompt': x_prompt, 'x_sample': x_sample,
            'state_mlstm_c': state_mlstm_c, 'state_mlstm_n': state_mlstm_n, 'state_mlstm_m': state_mlstm_m,
            'state_rglru_h': state_rglru_h, 'state_rglru_conv': state_rglru_conv,
            'state_ssd_h': state_ssd_h, 'state_ssd_conv': state_ssd_conv,
            'meta_tokens': meta_tokens, 'w_in': w_in, 'norm_g': norm_g, 'ml_f_bias': ml_f_bias,
            'ml_norm_g': ml_norm_g, 'lru_conv_w': lru_conv_w, 'lru_conv_b': lru_conv_b,
            'lru_w_a': lru_w_a, 'lru_b_a': lru_b_a, 'lru_w_x': lru_w_x, 'lru_b_x': lru_b_x,
            'lru_lambda': lru_lambda, 'ssd_conv_w': ssd_conv_w, 'ssd_conv_b': ssd_conv_b,
            'ssd_dt_bias': ssd_dt_bias, 'ssd_a_log': ssd_a_log, 'ssd_d': ssd_d, 'ssd_norm_g': ssd_norm_g,
            'w_br_ml': w_br_ml, 'w_br_lru': w_br_lru, 'w_br_ssd': w_br_ssd, 'w_out': w_out,
            'final_norm_g': final_norm_g}


def reference(x_prompt, x_sample, state_mlstm_c, state_mlstm_n, state_mlstm_m, state_rglru_h,
              state_rglru_conv, state_ssd_h, state_ssd_conv, meta_tokens, w_in, norm_g, ml_f_bias,
              ml_norm_g, lru_conv_w, lru_conv_b, lru_w_a, lru_b_a, lru_w_x, lru_b_x, lru_lambda,
              ssd_conv_w, ssd_conv_b, ssd_dt_bias, ssd_a_log, ssd_d, ssd_norm_g,
              w_br_ml, w_br_lru, w_br_ssd, w_out, final_norm_g):
    layer_params = (w_in, norm_g, ml_f_bias, ml_norm_g, lru_conv_w, lru_conv_b, lru_w_a, lru_b_a,
                    lru_w_x, lru_b_x, lru_lambda, ssd_conv_w, ssd_conv_b, ssd_dt_bias, ssd_a_log,
                    ssd_d, ssd_norm_g, w_br_ml, w_br_lru, w_br_ssd, w_out)
    sample_states = (state_mlstm_c, state_mlstm_n, state_mlstm_m, state_rglru_h, state_rglru_conv,
                     state_ssd_h, state_ssd_conv)
    bp = x_prompt.shape[0]
    meta = jnp.broadcast_to(meta_tokens.astype(x_prompt.dtype)[None], (bp, N_META, D_MODEL))
    xp = jnp.concatenate([meta, x_prompt], axis=1)
    zero_states = tuple(jnp.zeros((DEPTH, bp) + s.shape[2:], jnp.float32) for s in sample_states)
    yp, (p_c, p_n, p_m, p_hl, p_cl, p_hs, p_cs) = run_trunk(xp, N_META, zero_states, layer_params, final_norm_g)
    y_prompt = yp[:, N_META:]
    y_sample, (s_c, s_n, s_m, s_hl, s_cl, s_hs, s_cs) = run_trunk(x_sample, 0, sample_states, layer_params, final_norm_g)
    return (y_prompt, y_sample, p_c, p_n, p_m, p_hl, p_cl, p_hs, p_cs, s_c, s_n, s_m, s_hl, s_cl, s_hs, s_cs)
```

```python
import numpy as np
import ml_dtypes
from contextlib import ExitStack
import concourse.bass as bass
import concourse.mybir as mybir
from concourse.bass_utils import run_bass_kernel_spmd

F32 = mybir.dt.float32
BF16 = mybir.dt.bfloat16
AF = mybir.ActivationFunctionType
ALU = mybir.AluOpType
AX = mybir.AxisListType

EPS = 1e-6
NIN = 12312
OQ, OK_, OV, OI, OO, OZ = 0, 512, 1024, 2048, 2056, 3080
OLX, OLZ = 4104, 5128
OSZ, OXBC, ODT, OG = 6152, 7176, 9224, 9240
TILES = [(i * 128, 128) for i in range(16)] + [(2048, 16), (2064, 128)]
NTOK = 2192
SAMPLE = 17
NSLOT = 12
P_NG, P_MG, P_SG, P_LCW, P_LCB, P_LAM, P_SCW, P_SCB, P_LBA, P_LBX, NPP = 0, 8, 16, 24, 56, 64, 72, 136, 152, 160, 168
R_FB, R_DTB, R_ALOG, R_DD, NPR = 0, 4, 20, 36, 52
C_IDF, C_TRIP, C_SGTP, C_ONES, C_TRIS, C_SGTS, C_SAMES, C_ROWSEL, C_HALF0, C_HALF1, C_EPS, NCF = 0, 128, 256, 384, 512, 640, 768, 896, 912, 1040, 1168, 1172
B_IDB, B_COLSEL, NCB = 0, 128, 128 + 2048


class Sem:
    def __init__(self, h, name):
        self.h = h
        self.name = name
        self.val = 0


class Buf:
    __slots__ = ("name", "excl", "w", "rd", "sem")

    def __init__(self, name, excl=False, sem=None):
        self.name = name
        self.excl = excl
        self.w = None
        self.rd = {}
        self.sem = sem


class Eng:
    def __init__(self, name, h, sem, is_pe=False):
        self.name = name
        self.h = h
        self.sem = sem
        self.is_pe = is_pe
        self.waited = {}
        self.nwait = 0
        self.nins = 0


class K:
    def __init__(self, nc, stack):
        self.nc = nc
        self.stack = stack
        self.nsem = 0
        self.pe = Eng("pe", nc.tensor, self.new_sem("s_pe"), is_pe=True)
        self.act = Eng("act", nc.scalar, self.new_sem("s_act"))
        self.dve = Eng("dve", nc.vector, self.new_sem("s_dve"))
        self.pool = Eng("pool", nc.gpsimd, self.new_sem("s_pool"))
        self.sp = Eng("sp", nc.sync, self.new_sem("s_sp"))
        self.dram_out_sems = {}

    def new_sem(self, name):
        self.nsem += 1
        return Sem(self.stack.enter_context(self.nc.semaphore(name)), name)

    def _deps(self, engid, is_pe, R, W):
        deps = []
        for b in R:
            if b.w is not None:
                e, s, v = b.w
                if not (e == engid and is_pe):
                    deps.append((s, v))
            if b.excl:
                for (e, s), v in b.rd.items():
                    if e != engid:
                        deps.append((s, v))
        for b in W:
            if b.w is not None:
                e, s, v = b.w
                if e != engid or engid == "dma":
                    deps.append((s, v))
            for (e, s), v in b.rd.items():
                if e != engid or engid == "dma":
                    deps.append((s, v))
        return deps

    def _wait(self, eng, deps):
        best = {}
        for s, v in deps:
            if v > best.get(s, 0):
                best[s] = v
        for s, v in best.items():
            if eng.waited.get(s, 0) >= v:
                continue
            if v > s.val:
                raise RuntimeError(f"wait on un-emitted milestone {s.name} {v}>{s.val} from {eng.name}")
            eng.h.wait_ge(s.h, v)
            eng.waited[s] = v
            eng.nwait += 1

    def _mark(self, engid, ev_sem, ev_val, R, W):
        for b in R:
            b.rd[(engid, ev_sem)] = ev_val
        for b in W:
            b.w = (engid, ev_sem, ev_val)
            b.rd = {}

    def op(self, eng, fn, R=(), W=(), inc=True):
        deps = self._deps(eng.name, eng.is_pe, R, W)
        self._wait(eng, deps)
        ins = fn(eng.h)
        eng.nins += 1
        if inc:
            eng.sem.val += 1
            ins.then_inc(eng.sem.h, 1)
            val = eng.sem.val
        else:
            val = eng.sem.val + 1
        self._mark(eng.name, eng.sem, val, R, W)
        return ins

    def dma(self, eng, out, in_, R=(), W=(), sem=None, is_out=False, **kw):
        deps = self._deps("dma", False, R, W)
        self._wait(eng, deps)
        ins = eng.h.dma_start(out=out, in_=in_, **kw)
        eng.nins += 1
        sem.val += 16
        ins.then_inc(sem.h, 16)
        self._mark("dma", sem, sem.val, R, W)
        if is_out:
            self.dram_out_sems[sem] = sem.val
        return ins

    def barrier(self, engs):
        for e in engs:
            deps = [(o.sem, o.sem.val) for o in engs if o is not e]
            self._wait(e, deps)


def ap(t, p0, npart, f0, dims):
    F = 1
    for s in t.shape[1:]:
        F *= s
    return bass.AP(t, p0 * F + f0, [[F, npart]] + [list(d) for d in dims])


def dap(t, off, dims):
    return bass.AP(t, off, [list(d) for d in dims])


def build_program(cfg=None):
    cfg = cfg or {}
    nlayers = cfg.get("nlayers", 2)
    nc = bass.Bass("TRN2", target_bir_lowering=False)
    di = lambda n, s, dt=F32: nc.dram_tensor(n, list(s), dt, kind="ExternalInput")
    do = lambda n, s: nc.dram_tensor(n, list(s), F32, kind="ExternalOutput")
    dint = lambda n, s, dt=F32: nc.dram_tensor(n, list(s), dt, kind="Internal")
    xp = di("xp", [2048, 1024]); meta = di("meta", [16, 1024]); xs = di("xs", [128, 1024])
    st_c = di("st_c", [2, 16, 4, 128, 256]); st_n = di("st_n", [2, 64, 128]); st_m = di("st_m", [2, 16, 4])
    st_hl = di("st_hl", [2, 16, 1024]); st_cl = di("st_cl", [2, 48, 1024])
    st_hs = di("st_hs", [2, 16, 1024, 128]); st_cs = di("st_cs", [2, 48, 2048])
    w_in = di("w_in", [2, 1024, NIN]); w_br = di("w_br", [2, 3, 1024, 1024]); w_out = di("w_out", [2, 1024, 1024])
    wbd_d = di("wbd", [2, 2, 8, 128, 128])
    pp_d = di("pp", [2, 128, NPP]); pr_d = di("pr", [2, 128, NPR]); fg_d = di("fg", [128, 1024])
    cf_d = di("cf", [128, NCF]); cb_d = di("cb", [128, NCB], BF16)
    y_p = do("y_p", [2048, 1024]); y_s = do("y_s", [128, 1024])
    p_c = do("p_c", [2, 4, 128, 256]); p_n = do("p_n", [2, 4, 128]); p_m = do("p_m", [2, 4])
    p_hl = do("p_hl", [2, 1024]); p_cl = do("p_cl", [2, 3, 1024]); p_hs = do("p_hs", [2, 1024, 128]); p_cs = do("p_cs", [2, 3, 2048])
    s_c = do("s_c", [2, 16, 4, 128, 256]); s_n = do("s_n", [2, 64, 128]); s_m = do("s_m", [2, 16, 4])
    s_hl = do("s_hl", [2, 16, 1024]); s_cl = do("s_cl", [2, 16, 3, 1024]); s_hs = do("s_hs", [2, 16, 1024, 128]); s_cs = do("s_cs", [2, 16, 3, 2048])
    xscr = dint("xscr", [NTOK, 1024]); mscr = dint("mscr", [NTOK, 1024]); xnscr = dint("xnscr", [18, 128, 8, 128], BF16)

    with ExitStack() as st:
        k = K(nc, st)
        pe, act, dve, pool, sp = k.pe, k.act, k.dve, k.pool, k.sp

        def sbt(name, shape, dt=F32, sem=False, stack=st):
            t = stack.enter_context(nc.sbuf_tensor("sb_" + name, list(shape), dt))
            b = Buf(name, sem=(k.new_sem("d_" + name) if sem else None))
            return t, b

        slots = [sbt(f"slot{i}", [128, 8, 512], BF16, sem=True) for i in range(NSLOT)]
        XT = [sbt(f"xt{i}", [128, 1024], F32, sem=True) for i in range(4)]
        XN = [sbt(f"xn{i}", [128, 8, 128], BF16, sem=True) for i in range(3)]
        cf, cfB = sbt("cf", [128, NCF], F32, sem=True)
        cb, cbB = sbt("cb", [128, 128], BF16, sem=True)
        fg, fgB = sbt("fg", [128, 1024], F32, sem=True)
        ppt, ppB = sbt("ppt", [128, 2, NPP], F32, sem=True)
        prt, prB = sbt("prt", [128, 2, NPR], F32, sem=True)
        wif, wifB = sbt("wif", [128, 8, 8], BF16, sem=True)
        wdt, wdtB = sbt("wdt", [128, 8, 16], BF16, sem=True)
        ps = []
        for i in range(8):
            t = st.enter_context(nc.psum_tensor(f"ps{i}", [128, 512], F32))
            ps.append((t, t.bitcast(BF16), Buf(f"ps{i}", excl=True)))
        bank_free = {"F": [0, 1, 2], "B": [3, 4, 5, 6, 7]}

        def bk(pool="B"):
            if not bank_free[pool]:
                raise RuntimeError("out of PSUM banks in pool " + pool)
            return ps[bank_free[pool].pop(0)]

        def rel(*bs):
            for b in bs:
                i = [x[2] for x in ps].index(b[2])
                p = "F" if i < 3 else "B"
                assert i not in bank_free[p]
                bank_free[p].append(i)

        def run(gens):
            if cfg.get("seq"):
                for g in gens:
                    for _ in g:
                        pass
                return
            act_ = list(gens)
            wts = cfg.get("wts", [2, 1])
            while act_:
                for gi, g in enumerate(list(act_)):
                    for _ in range(wts[gi] if len(act_) > 1 and gi < len(wts) else 1):
                        try:
                            next(g)
                        except StopIteration:
                            if g in act_:
                                act_.remove(g)
                            break

        XS = [Buf(f"xscr{i}") for i in range(18)]
        MS = [Buf(f"mscr{i}") for i in range(18)]
        XNS = [Buf(f"xnscr{i}") for i in range(18)]

        def mm(out, lhsT, rhs, start, stop, R, W, inc):
            k.op(pe, lambda e: e.matmul(out, lhsT=lhsT, rhs=rhs, start=start, stop=stop), R, W, inc)

        def tr(out, in_, ident, R, W, inc):
            k.op(pe, lambda e: e.transpose(out=out, in_=in_, identity=ident), R, W, inc)

        def A(out, in_, func, R, W, scale=None, bias=None, accum=None):
            kw = {}
            if scale is not None:
                kw["scale"] = scale
            if bias is not None:
                kw["bias"] = bias
            if accum is not None:
                kw["accum_out"] = accum
            k.op(act, lambda e: e.activation(out=out, in_=in_, func=func, **kw), R, W)

        def tt(out, in0, in1, op, R, W, eng=None):
            k.op(eng or dve, lambda e: e.tensor_tensor(out=out, in0=in0, in1=in1, op=op), R, W)

        def ts(out, in0, s1, s2, op0, op1, R, W, eng=None):
            if op1 is None:
                k.op(eng or dve, lambda e: e.tensor_scalar(out=out, in0=in0, scalar1=s1, scalar2=None, op0=op0), R, W)
            else:
                k.op(eng or dve, lambda e: e.tensor_scalar(out=out, in0=in0, scalar1=s1, scalar2=s2, op0=op0, op1=op1), R, W)

        def stt(out, in0, scalar, in1, op0, op1, R, W):
            k.op(dve, lambda e: e.scalar_tensor_tensor(out=out, in0=in0, scalar=scalar, in1=in1, op0=op0, op1=op1), R, W)

        def cp(out, in_, R, W, eng=None):
            k.op(eng or dve, lambda e: e.tensor_copy(out=out, in_=in_), R, W)

        def rcp(out, in_, R, W):
            k.op(dve, lambda e: e.reciprocal(out=out, in_=in_), R, W)

        def mset(t_ap, val, W, eng=None):
            k.op(eng or dve, lambda e: e.memset(t_ap, val), (), W)

        def dma_in(dst_ap, src_ap, dstbuf, R=(), eng=None, **kw):
            k.dma(eng or sp, dst_ap, src_ap, R=R, W=[dstbuf], sem=dstbuf.sem, **kw)

        def dma_out(dst_ap, src_ap, srcbuf, W=(), final=False, eng=None, **kw):
            k.dma(eng or sp, dst_ap, src_ap, R=[srcbuf], W=W, sem=srcbuf.sem, is_out=final, **kw)

        identf = cf[:, C_IDF:C_IDF + 128]
        identb = cb[:, B_IDB:B_IDB + 128]

        def cfm(c0, L):
            return cf[:L, c0:c0 + L]

        slot_free = list(range(NSLOT))
        loaded = {}
        loaded_done = {}

        def wsrc(spec):
            kind = spec[0]
            if kind == "in":
                _, l, c0 = spec
                return dap(w_in, l * 1024 * NIN + c0, [[NIN, 128], [128 * NIN, 8], [1, 512]])
            if kind == "br":
                _, l, b, h = spec
                return dap(w_br, (l * 3 + b) * 1024 * 1024 + h * 512, [[1024, 128], [128 * 1024, 8], [1, 512]])
            _, l, h = spec
            return dap(w_out, l * 1024 * 1024 + h * 512, [[1024, 128], [128 * 1024, 8], [1, 512]])

        def phase_set(l, ph):
            if ph == "A":
                d = {"q": ("in", l, OQ), "k": ("in", l, OK_), "v0": ("in", l, OV), "v1": ("in", l, OV + 512),
                     "o0": ("in", l, OO), "o1": ("in", l, OO + 512), "z0": ("in", l, OZ), "z1": ("in", l, OZ + 512),
                     "g0": ("in", l, OG), "g1": ("in", l, OG + 512), "br0": ("br", l, 0, 0), "br1": ("br", l, 0, 1)}
            elif ph == "B":
                d = {"lx0": ("in", l, OLX), "lx1": ("in", l, OLX + 512), "lz0": ("in", l, OLZ), "lz1": ("in", l, OLZ + 512),
                     "g0": ("in", l, OG + 1024), "g1": ("in", l, OG + 1536), "br0": ("br", l, 1, 0), "br1": ("br", l, 1, 1)}
            else:
                d = {"sz0": ("in", l, OSZ), "sz1": ("in", l, OSZ + 512),
                     "xb0": ("in", l, OXBC), "xb1": ("in", l, OXBC + 512), "xb2": ("in", l, OXBC + 1024), "xb3": ("in", l, OXBC + 1536),
                     "g0": ("in", l, OG + 2048), "g1": ("in", l, OG + 2560), "br0": ("br", l, 2, 0), "br1": ("br", l, 2, 1),
                     "wo0": ("out", l, 0), "wo1": ("out", l, 1)}
            return d

        def load_weights(l, ph, only_free=True):
            key = (l, ph)
            d = phase_set(l, ph)
            have = loaded.setdefault(key, {})
            done = loaded_done.setdefault(key, set())
            for name, spec in d.items():
                if name in done:
                    continue
                if not slot_free:
                    return False
                si = slot_free.pop(0)
                t, b = slots[si]
                k.dma(pool, t[:], wsrc(spec), W=[b], sem=b.sem)
                have[name] = si
                done.add(name)
            return True

        def release_weights(l, ph, names=None):
            d = loaded[(l, ph)]
            for name in list(d.keys()):
                if names is None or name in names:
                    slot_free.append(d.pop(name))
            if names is None:
                loaded.pop((l, ph))

        def Wt(l, ph, name):
            si = loaded[(l, ph)][name]
            return slots[si]

        with nc.Block() as block:
            dma_in(cf[:], cf_d[:, :], cfB)
            dma_in(cb[:], cb_d[:, 0:128], cbB)
            dma_in(fg[:], fg_d[:, :], fgB)
            for l_ in range(2):
                dma_in(ppt[:, l_, :], pp_d[l_, :, :], ppB)
                dma_in(prt[:, l_, :], pr_d[l_, :, :], prB)
            phases = [(l, ph) for l in range(nlayers) for ph in "ABC" if cfg.get("ph" + ph, True)]
            if phases:
                load_weights(*phases[0])

            for l in range(nlayers):
                last_layer = (l == nlayers - 1)
                k.dma(pool, wif[:], dap(w_in, l * 1024 * NIN + OI, [[NIN, 128], [128 * NIN, 8], [1, 8]]), W=[wifB], sem=wifB.sem)
                k.dma(pool, wdt[:], dap(w_in, l * 1024 * NIN + ODT, [[NIN, 128], [128 * NIN, 8], [1, 16]]), W=[wdtB], sem=wdtB.sem)

                def ppc(c0, n=1):
                    return ppt[:, l, c0:c0 + n]

                def ppbc(c0, n, L):
                    return ap(ppt, 0, 128, l * NPP + c0, [[1, n], [0, L]])

                def prc(c0, n, L):
                    return prt[:L, l, c0:c0 + n]

                with ExitStack() as pst:
                    P = lambda n, s, dt=F32, sem=False: sbt(f"p0_{n}_{l}", s, dt, sem=sem, stack=pst)
                    xnb2 = [P(f"xnb{i_}", [128, 1024], BF16) for i_ in range(2)]
                    jk2 = [P(f"jk{i_}", [128, 1024], BF16) for i_ in range(2)]
                    sm2 = [P(f"sm{i_}", [128, 4], F32) for i_ in range(2)]

                    def p0_load(i):
                        t0, L = TILES[i]
                        xt, xb = XT[i % 4]
                        if l == 0:
                            if i == 0:
                                dma_in(xt[0:16, :], meta[:, :], xb)
                                dma_in(xt[16:128, :], xp[0:112, :], xb)
                            elif i < 16:
                                dma_in(xt[:, :], xp[128 * i - 16:128 * i + 112, :], xb)
                            elif i == 16:
                                dma_in(xt[0:16, :], xp[2032:2048, :], xb)
                            else:
                                dma_in(xt[:, :], xs[:, :], xb)
                        else:
                            dma_in(xt[:L, :], xscr[t0:t0 + L, :], xb, R=[XS[i]])

                    def p0(i):
                        t0, L = TILES[i]
                        xt, xb = XT[i % 4]
                        xnb, xnbB = xnb2[i % 2]; jk, jkB = jk2[i % 2]; sm, smB = sm2[i % 2]
                        if i + 2 < 18:
                            p0_load(i + 2)
                        if l == 0:
                            dma_out(xscr[t0:t0 + L, :], xt[:L, :], xb, W=[XS[i]])
                        A(jk[:L, :], xt[:L, :], AF.Square, [xb], [jkB, smB], accum=sm[:L, 0:1])
                        yield
                        A(sm[:L, 1:2], sm[:L, 0:1], AF.Ln, [smB, cfB], [smB], scale=1.0 / 1024, bias=cf[:L, C_EPS:C_EPS + 1])
                        A(sm[:L, 3:4], sm[:L, 1:2], AF.Exp, [smB], [smB], scale=-0.5)
                        A(xnb[:L, :], xt[:L, :], AF.Identity, [xb, smB], [xnbB], scale=sm[:L, 3:4])
                        yield
                        pt = bk("F" if i % 2 else "B")
                        for j in range(8):
                            tr(pt[1][:, j * 128:j * 128 + L], xnb[:L, j * 128:(j + 1) * 128], identb[:L, :L], [xnbB, cbB], [pt[2]], j == 7)
                        xn, xnB = XN[i % 3]
                        tt(xn[:, :, :L], ap(pt[1], 0, 128, 0, [[128, 8], [1, L]]), ppbc(P_NG, 8, L), ALU.mult, [pt[2], ppB], [xnB])
                        rel(pt)
                        dma_out(xnscr[i, :, :, :L], xn[:, :, :L], xnB, W=[XNS[i]])
                        yield

                    p0_load(0)
                    p0_load(1)
                    for i in range(0, 18, 2):
                        run([p0(i), p0(i + 1)])
                    k.barrier([pe, act, dve])

                def tail_prefetch(i, mode):
                    t0, L = TILES[i]
                    if mode != "first":
                        mt_, mb_ = XT[i % 2]
                        dma_in(mt_[:L, :], mscr[t0:t0 + L, :], mb_, R=[MS[i]])
                    if mode == "last":
                        xt_, xb_ = XT[2 + i % 2]
                        dma_in(xt_[:L, :], xscr[t0:t0 + L, :], xb_, R=[XS[i]])

                def xn_load(i):
                    xn, xnB = XN[i % 3]
                    L = TILES[i][1]
                    dma_in(xn[:, :, :L], xnscr[i, :, :, :L], xnB, R=[XNS[i]])

                def tail(i, ph, mode, yT, yTB, xn, xnB, T):
                    t0, L = TILES[i]
                    sg, sgB = T["sg"]
                    mt_, mb_ = XT[i % 2]
                    zb = [bk(), bk()]
                    for n in range(2):
                        wt_, wB = Wt(l, ph, f"br{n}")
                        for kk in range(8):
                            mm(zb[n][0][:L, :], yT[:, kk, :L], wt_[:, kk, :], kk == 0, kk == 7, [yTB, wB], [zb[n][2]], kk == 7)
                        yield
                    gb = [bk(), bk()]
                    for n in range(2):
                        wt_, wB = Wt(l, ph, f"g{n}")
                        for kk in range(8):
                            mm(gb[n][0][:L, :], xn[:, kk, :L], wt_[:, kk, :], kk == 0, kk == 7, [xnB, wB], [gb[n][2]], kk == 7)
                        yield
                    for n in range(2):
                        A(sg[:L, n * 512:(n + 1) * 512], gb[n][0][:L, :], AF.Tanh, [gb[n][2]], [sgB], scale=0.5)
                    rel(*gb)
                    if mode == "first":
                        for n in range(2):
                            stt(mt_[:L, n * 512:(n + 1) * 512], sg[:L, n * 512:(n + 1) * 512], 1.0, zb[n][0][:L, :], ALU.add, ALU.mult, [zb[n][2], sgB], [mb_])
                        rel(*zb)
                        dma_out(mscr[t0:t0 + L, :], mt_[:L, :], mb_, W=[MS[i]])
                        return
                    for n in range(2):
                        stt(sg[:L, n * 512:(n + 1) * 512], sg[:L, n * 512:(n + 1) * 512], 1.0, zb[n][0][:L, :], ALU.add, ALU.mult, [zb[n][2], sgB], [sgB])
                    rel(*zb)
                    if mode == "mid":
                        tt(mt_[:L, :], mt_[:L, :], sg[:L, :], ALU.add, [mb_, sgB], [mb_])
                        dma_out(mscr[t0:t0 + L, :], mt_[:L, :], mb_, W=[MS[i]])
                        return
                    mrg, mrgB = T["mrg"]
                    mT, mTB = T["mT"]
                    tt(mrg[:L, :], mt_[:L, :], sg[:L, :], ALU.add, [mb_, sgB], [mrgB])
                    pt = bk()
                    for j in range(8):
                        tr(pt[1][:, j * 128:j * 128 + L], mrg[:L, j * 128:(j + 1) * 128], identb[:L, :L], [mrgB, cbB], [pt[2]], j == 7)
                    cp(mT[:, :, :L], ap(pt[1], 0, 128, 0, [[128, 8], [1, L]]), [pt[2]], [mTB])
                    rel(pt)
                    yield
                    ob = [bk(), bk()]
                    for n in range(2):
                        wt_, wB = Wt(l, ph, f"wo{n}")
                        for kk in range(8):
                            mm(ob[n][0][:L, :], mT[:, kk, :L], wt_[:, kk, :], kk == 0, kk == 7, [mTB, wB], [ob[n][2]], kk == 7)
                        yield
                    xt_, xb_ = XT[2 + i % 2]
                    for n in range(2):
                        stt(xt_[:L, n * 512:(n + 1) * 512], ob[n][0][:L, :], 0.5, xt_[:L, n * 512:(n + 1) * 512], ALU.mult, ALU.add, [xb_, ob[n][2]], [xb_])
                    rel(*ob)
                    if not last_layer:
                        dma_out(xscr[t0:t0 + L, :], xt_[:L, :], xb_, W=[XS[i]])
                        return
                    sm, smB = T["fsm"]
                    A(mrg[:L, :], xt_[:L, :], AF.Square, [xb_], [mrgB, smB], accum=sm[:L, 0:1])
                    A(sm[:L, 1:2], sm[:L, 0:1], AF.Ln, [smB, cfB], [smB], scale=1.0 / 1024, bias=cf[:L, C_EPS:C_EPS + 1])
                    A(sm[:L, 3:4], sm[:L, 1:2], AF.Exp, [smB], [smB], scale=-0.5)
                    stt(mt_[:L, :], xt_[:L, :], sm[:L, 3:4], fg[:L, :], ALU.mult, ALU.mult, [xb_, smB, fgB], [mb_])
                    if i == 0:
                        dma_out(y_p[0:112, :], mt_[16:128, :], mb_, final=True)
                    elif i < 16:
                        dma_out(y_p[128 * i - 16:128 * i + 112, :], mt_[:, :], mb_, final=True)
                    elif i == 16:
                        dma_out(y_p[2032:2048, :], mt_[0:16, :], mb_, final=True)
                    else:
                        dma_out(y_s[:, :], mt_[:, :], mb_, final=True)

                def repl4(val_ap, ncol, dstR, dstRB, out_t, out_B):
                    tt(ap(dstR, 0, 4, 0, [[4, ncol], [1, 4]]), ap(val_ap[0], 0, 4, val_ap[1], [[1, ncol], [0, 4]]),
                       ap(cf, 0, 4, C_IDF, [[0, ncol], [1, 4]]), ALU.mult, [val_ap[2], cfB], [dstRB])
                    b = bk()
                    mm(b[0][:, 0:ncol * 4], cf[0:4, C_ONES:C_ONES + 128], dstR[0:4, 0:ncol * 4], True, True, [cfB, dstRB], [b[2]], True)
                    cp(out_t[:, 0:ncol * 4], b[0][:, 0:ncol * 4], [b[2]], [out_B])
                    rel(b)

                def drive(front, back, ph, front_only, nxt_phase):
                    xn_load(0)
                    run([front(0)])
                    for i in range(18):
                        gs = [back(i)]
                        if i + 1 < 18:
                            gs.append(front(i + 1))
                        run(gs)
                        if i + 1 == 17:
                            release_weights(l, ph, front_only)
                            if nxt_phase is not None:
                                load_weights(*nxt_phase)
                    k.barrier([pe, act, dve])

                def phaseA(nxt_phase):
                    with ExitStack() as pst:
                        P = lambda n, s, dt=F32, sem=False: sbt(f"a_{n}_{l}", s, dt, sem=sem, stack=pst)
                        P2 = lambda n, s, dt=F32: [P(f"{n}{i_}", s, dt) for i_ in range(2)]
                        qT2 = P2("qT", [128, 4, 128], BF16); kT, kTB = P("kT", [128, 4, 128], BF16)
                        ktok2 = P2("ktok", [128, 512], BF16)
                        ve2 = P2("ve", [128, 4, 260], BF16); pmT2 = P2("pmT", [128, 4, 128], BF16)
                        og2 = P2("og", [128, 1024]); sgo, sgoB = P("sgo", [128, 1024]); sg, sgB = P("sg", [128, 1024])
                        yml, ymlB = P("yml", [128, 1024], BF16); yT, yTB = P("yT", [128, 8, 128], BF16)
                        CN, CNB = P("CN", [128, 4, 260]); CNb, CNbB = P("CNb", [128, 4, 260], BF16)
                        jk, jkB = P("jk", [128, 256], BF16)
                        gif, gifB = P("gif", [128, 8]); gx, gxB = P("gx", [128, 8]); gnlf, gnlfB = P("gnlf", [128, 4])
                        ga2 = P2("ga", [128, 4]); ge, geB = P("ge", [128, 4]); gfl2 = P2("gfl", [128, 4])
                        gebl2 = P2("gebl", [128, 4]); gnbl2 = P2("gnbl", [128, 4])
                        gden, gdenB = P("gden", [128, 8]); gss, gssB = P("gss", [128, 4]); gt, gtB = P("gt", [128, 12]); gsc, gscB = P("gsc", [128, 4])
                        mst, mstB = P("mst", [128, 96], F32, sem=True); rr, rrB = P("rr", [128, 64]); emr, emrB = P("emr", [128, 64])
                        co = [P(f"co{i_}", [128, 4, 260], F32, sem=True) for i_ in range(2)]
                        T = {"sg": (sg, sgB)}
                        DQS = float(128 ** -0.5)
                        csel, cselB = P("csel", [128, 2048], BF16, sem=True)
                        dma_in(csel[:], cb_d[:, B_COLSEL:B_COLSEL + 2048], cselB)
                        mset(CN[:], 0.0, [CNB]); mset(CNb[:], 0.0, [CNbB]); mset(mst[:], 0.0, [mstB])

                        def front(i):
                            t0, L = TILES[i]
                            smp = (i == SAMPLE)
                            par = i % 2
                            xn, xnB = XN[i % 3]
                            if i + 1 < 18:
                                xn_load(i + 1)
                            qT, qTB = qT2[par]; ktok, ktokB = ktok2[par]; ve, veB = ve2[par]; pmT, pmTB = pmT2[par]; og, ogB = og2[par]
                            ga, gaB = ga2[par]; gfl, gflB = gfl2[par]; gebl, geblB = gebl2[par]; gnbl, gnblB = gnbl2[par]
                            C_TRI = C_TRIS if smp else C_TRIP
                            C_ONE = C_SAMES if smp else C_ONES
                            for (wn, dst, dstB, scl) in (("q", qT, qTB, 1.0), ("k", kT, kTB, DQS)):
                                wt_, wB = Wt(l, "A", wn)
                                b = bk("F")
                                for h in range(4):
                                    for kk in range(8):
                                        mm(b[0][:, h * 128:h * 128 + L], wt_[:, kk, h * 128:(h + 1) * 128], xn[:, kk, :L], kk == 0, kk == 7,
                                           [wB, xnB], [b[2]], kk == 7 and h == 3)
                                A(dst[:, :, :L], ap(b[0], 0, 128, 0, [[128, 4], [1, L]]), AF.Identity, [b[2]], [dstB], scale=scl)
                                rel(b)
                                yield
                            wt_, wB = Wt(l, "A", "k")
                            b = bk("F")
                            for kk in range(8):
                                mm(b[0][:L, :], xn[:, kk, :L], wt_[:, kk, :], kk == 0, kk == 7, [xnB, wB], [b[2]], kk == 7)
                            A(ktok[:L, :], b[0][:L, :], AF.Identity, [b[2]], [ktokB], scale=DQS)
                            rel(b)
                            bg = bk("F")
                            for kk in range(8):
                                mm(bg[0][:L, 0:8], xn[:, kk, :L], wif[:, kk, :], kk == 0, kk == 7, [xnB, wifB], [bg[2]], kk == 7)
                            cp(gif[:L, :], bg[0][:L, 0:8], [bg[2]], [gifB])
                            tt(gx[:L, 0:4], gif[:L, 4:8], prc(R_FB, 4, L), ALU.add, [gifB, prB], [gxB])
                            A(gx[:L, 4:8], gx[:L, 0:4], AF.Exp, [gxB], [gxB], scale=-1.0)
                            A(gnlf[:L, :], gx[:L, 4:8], AF.Ln, [gxB, cfB], [gnlfB], bias=cf[:L, C_ONES:C_ONES + 1])
                            yield
                            bv = [bk("F"), bk("F")]
                            for n in range(2):
                                wt_, wB = Wt(l, "A", f"v{n}")
                                for kk in range(8):
                                    mm(bv[n][0][:L, :], xn[:, kk, :L], wt_[:, kk, :], kk == 0, kk == 7, [xnB, wB], [bv[n][2]], kk == 7)
                            yield
                            mm(bg[0][:L, 8:12], cfm(C_TRI, L), gnlf[:L, :], True, True, [cfB, gnlfB], [bg[2]], False)
                            mm(bg[0][:, 12:16], cf[:L, C_ONE:C_ONE + 128], gnlf[:L, :], True, True, [cfB, gnlfB], [bg[2]], True)
                            tt(ga[:L, :], gif[:L, 0:4], bg[0][:L, 8:12], ALU.add, [gifB, bg[2]], [gaB])
                            A(ge[:L, :], ga[:L, :], AF.Exp, [gaB], [geB])
                            A(gfl[:L, :], bg[0][:L, 8:12], AF.Exp, [bg[2]], [gflB])
                            A(gebl[:, :], bg[0][:, 12:16], AF.Exp, [bg[2]], [geblB], scale=-1.0)
                            cp(gnbl[:, :], bg[0][:, 12:16], [bg[2]], [gnblB])
                            rel(bg)
                            yield
                            for n in range(2):
                                tt(ve[:L, 2 * n:2 * n + 2, 0:256], ap(bv[n][0], 0, L, 0, [[256, 2], [1, 256]]),
                                   ap(ge, 0, L, 2 * n, [[1, 2], [0, 256]]), ALU.mult, [bv[n][2], geB], [veB])
                            rel(*bv)
                            cp(ap(ve, 0, L, 256, [[260, 4], [1, 1]]), ap(ge, 0, L, 0, [[1, 4], [1, 1]]), [geB], [veB])
                            bs = bk("F")
                            for h in range(4):
                                mm(bs[0][:L, h * 128:h * 128 + L], kT[:, h, :L], qT[:, h, :L], True, True, [kTB, qTB], [bs[2]], h == 3)
                            tt(pmT[:L, :, :L], ap(bs[0], 0, L, 0, [[128, 4], [1, L]]), ap(cf, 0, L, C_TRI, [[0, 4], [1, L]]), ALU.mult, [bs[2], cfB], [pmTB])
                            rel(bs)
                            yield
                            for (wn, dst, dstB, fn, fsc) in (("o", sgo, sgoB, AF.Tanh, 0.5), ("z", og, ogB, AF.Silu, 1.0)):
                                bb = [bk("F"), bk("F")]
                                for n in range(2):
                                    wt_, wB = Wt(l, "A", f"{wn}{n}")
                                    for kk in range(8):
                                        mm(bb[n][0][:L, :], xn[:, kk, :L], wt_[:, kk, :], kk == 0, kk == 7, [xnB, wB], [bb[n][2]], kk == 7)
                                    yield
                                for n in range(2):
                                    A(dst[:L, n * 512:(n + 1) * 512], bb[n][0][:L, :], fn, [bb[n][2]], [dstB], scale=fsc)
                                rel(*bb)
                            stt(og[:L, :], sgo[:L, :], 1.0, og[:L, :], ALU.add, ALU.mult, [ogB, sgoB], [ogB])
                            yield

                        def back(i):
                            t0, L = TILES[i]
                            smp = (i == SAMPLE)
                            par = i % 2
                            xn, xnB = XN[i % 3]
                            qT, qTB = qT2[par]; ktok, ktokB = ktok2[par]; ve, veB = ve2[par]; pmT, pmTB = pmT2[par]; og, ogB = og2[par]
                            ga, gaB = ga2[par]; gfl, gflB = gfl2[par]; gebl, geblB = gebl2[par]; gnbl, gnblB = gnbl2[par]
                            if not smp:
                                bn = [bk() for _ in range(4)]
                                for h in range(4):
                                    mm(bn[h][0][:L, 0:257], pmT[:L, h, :L], ve[:L, h, 0:257], True, False, [pmTB, veB], [bn[h][2]], False)
                                    mm(bn[h][0][:L, 0:257], qT[:, h, :L], CNb[:, h, 0:257], False, True, [qTB, CNbB], [bn[h][2]], True)
                                yield
                                for h in range(4):
                                    b = bk()
                                    mm(b[0][:, 0:257], ktok[:L, h * 128:(h + 1) * 128], ve[:L, h, 0:257], True, True, [ktokB, veB], [b[2]], True)
                                    tt(CN[:, h, 0:257], b[0][:, 0:257], CN[:, h, 0:257], ALU.add, [b[2], CNB], [CNB])
                                    ts(CN[:, h, 0:257], CN[:, h, 0:257], gebl[:, h:h + 1], None, ALU.mult, None, [CNB, geblB], [CNB])
                                    rel(b)
                                    if h % 2 == 1:
                                        yield
                                A(CNb[:, :, 0:257], CN[:, :, 0:257], AF.Copy, [CNB], [CNbB])
                                bm = bk()
                                tr(bm[0][0:4, 0:L], ga[:L, 0:4], identf[:L, :L], [gaB, cfB], [bm[2]], False)
                                tr(bm[0][0:4, 128:256], gnbl[:, 0:4], identf[:, :], [gnblB, cfB], [bm[2]], True)
                                k.op(dve, lambda e: e.reduce_max(out=mst[0:4, 1:2], in_=bm[0][0:4, 0:L], axis=AX.X), [bm[2]], [mstB])
                                tt(mst[0:4, 2:3], mst[0:4, 0:1], mst[0:4, 1:2], ALU.max, [mstB], [mstB])
                                tt(mst[0:4, 0:1], mst[0:4, 2:3], bm[0][0:4, 128:129], ALU.subtract, [mstB, bm[2]], [mstB])
                                rel(bm)
                                yield
                            else:
                                n0t, n0tB = XT[3]
                                dma_in(n0t[0:64, 0:128], st_n[l, :, :], n0tB)
                                dma_in(ap(mst, 0, 4, 16, [[1, 16]]), dap(st_m, l * 64, [[1, 4], [4, 16]]), mstB, allow_slow_non_contiguous=True)
                                b = bk()
                                tr(b[0][:, 0:64], n0t[0:64, 0:128], identf[0:64, 0:64], [n0tB, cfB], [b[2]], True)
                                n0T, n0TB = P("n0T", [128, 64])
                                cp(n0T[:, :], b[0][:, 0:64], [b[2]], [n0TB])
                                rel(b)
                                nout, noutB = P("nout", [128, 64])
                                A(mst[0:4, 32:48], mst[0:4, 16:32], AF.Exp, [mstB], [mstB])
                                rr5, rr5B = P("rr5", [128, 512])
                                tt(ap(rr5, 0, 4, 0, [[128, 4], [8, 16], [1, 8]]), ap(cf, 0, 4, C_IDF, [[1, 4], [0, 16], [0, 8]]), ap(mst, 0, 4, 32, [[0, 4], [1, 16], [0, 8]]),
                                   ALU.mult, [cfB, mstB], [rr5B])
                                b = bk()
                                mm(b[0][:, 0:512], cf[0:4, C_ONES:C_ONES + 128], rr5[0:4, 0:512], True, True, [cfB, rr5B], [b[2]], True)
                                qs, qsB = P("qs", [128, 4, 128], BF16)
                                tt(qs[:].rearrange("p a b -> p (a b)"), qT[:].rearrange("p a b -> p (a b)"), b[0][:, 0:512], ALU.mult, [qTB, b[2]], [qsB])
                                rel(b)
                                yield
                                bm = bk()
                                tr(bm[0][0:4, 0:128], ga[:, 0:4], identf[:, :], [gaB, cfB], [bm[2]], False)
                                tr(bm[0][0:4, 128:256], gnbl[:, 0:4], identf[:, :], [gnblB, cfB], [bm[2]], True)
                                k.op(dve, lambda e: e.tensor_reduce(out=mst[0:4, 64:80], in_=ap(bm[0], 0, 4, 0, [[8, 16], [1, 8]]), axis=AX.X, op=ALU.max), [bm[2]], [mstB])
                                tt(mst[0:4, 64:80], mst[0:4, 64:80], mst[0:4, 16:32], ALU.max, [mstB], [mstB])
                                nblv = ap(bm[0], 0, 4, 128, [[8, 16]])
                                tt(mst[0:4, 48:64], mst[0:4, 64:80], nblv, ALU.subtract, [mstB, bm[2]], [mstB])
                                tt(mst[0:4, 64:80], mst[0:4, 16:32], mst[0:4, 48:64], ALU.subtract, [mstB], [mstB])
                                tt(mst[0:4, 64:80], mst[0:4, 64:80], nblv, ALU.subtract, [mstB, bm[2]], [mstB])
                                A(mst[0:4, 64:80], mst[0:4, 64:80], AF.Exp, [mstB], [mstB])
                                w1r, w1rB = P("w1r", [128, 64]); w2r, w2rB = P("w2r", [128, 64])
                                repl4((mst, 64, mstB), 16, rr, rrB, w1r, w1rB)
                                ts(mst[0:4, 80:96], mst[0:4, 48:64], -1.0, None, ALU.mult, None, [mstB], [mstB])
                                tt(mst[0:4, 80:96], mst[0:4, 80:96], nblv, ALU.subtract, [mstB, bm[2]], [mstB])
                                A(mst[0:4, 80:96], mst[0:4, 80:96], AF.Exp, [mstB], [mstB])
                                repl4((mst, 80, mstB), 16, rr, rrB, w2r, w2rB)
                                rel(bm)
                                dma_out(dap(s_m, l * 64, [[1, 4], [4, 16]]), ap(mst, 0, 4, 48, [[1, 16]]), mstB, final=True, allow_slow_non_contiguous=True)
                                yield
                                bn = [bk() for _ in range(4)]
                                for h in range(4):
                                    mm(bn[h][0][:, 0:257], pmT[:, h, :], ve[:, h, 0:257], True, False, [pmTB, veB], [bn[h][2]], False)
                                qz = [P(f"qz{i_}", [128, 4, 128], BF16) for i_ in range(2)]
                                vej = [P(f"vej{i_}", [128, 4, 260], BF16) for i_ in range(2)]
                                CNb2 = [(CNb, CNbB), P("CNb2", [128, 4, 260], BF16)]
                                tmpc, tmpcB = P("tmpc", [128, 4, 256])
                                ndl, ndlB = P("ndl", [128, 64])
                                stg = [XT[0], XT[2], XT[3]]

                                def ld(j):
                                    cs_, csB = stg[j % 3]
                                    dma_in(ap(cs_, 0, 128, 0, [[256, 4], [1, 256]]), dap(st_c, ((l * 16 + j) * 4) * 128 * 256, [[256, 128], [128 * 256, 4], [1, 256]]), csB)

                                ld(0)
                                ld(1)
                                for j in range(16):
                                    if j + 2 < 16:
                                        ld(j + 2)
                                    cs_, csB = stg[j % 3]
                                    cn, cnB = CNb2[j % 2]
                                    A(cn[:, :, 0:256], ap(cs_, 0, 128, 0, [[256, 4], [1, 256]]), AF.Copy, [csB], [cnB])
                                    A(ap(cn, 0, 128, 256, [[260, 4], [1, 1]]), ap(n0T, 0, 128, j * 4, [[1, 4], [1, 1]]), AF.Copy, [n0TB], [cnB])
                                    qz_, qzB = qz[j % 2]
                                    tt(qz_[:, :, :], qs[:, :, :], ap(csel, 0, 128, j * 128, [[0, 4], [1, 128]]), ALU.mult, [qsB, cselB], [qzB])
                                    for h in range(4):
                                        mm(bn[h][0][:, 0:257], qz_[:, h, :], cn[:, h, 0:257], False, j == 15, [qzB, cnB], [bn[h][2]], True)
                                    vj, vjB = vej[j % 2]
                                    A(vj[:].rearrange("p a b -> p (a b)"), ve[:].rearrange("p a b -> p (a b)"), AF.Identity, [veB, cfB], [vjB], scale=cf[:, C_ROWSEL + j:C_ROWSEL + j + 1])
                                    co_, coB = co[j % 2]
                                    for h in range(4):
                                        b = bk("F")
                                        col = j * 4 + h
                                        mm(b[0][:, 0:257], ktok[:, h * 128:(h + 1) * 128], vj[:, h, 0:257], True, True, [ktokB, vjB], [b[2]], True)
                                        A(tmpc[:, h, :], b[0][:, 0:256], AF.Identity, [b[2], w2rB], [tmpcB], scale=w2r[:, col:col + 1])
                                        A(ndl[:, col:col + 1], b[0][:, 256:257], AF.Copy, [b[2]], [ndlB])
                                        rel(b)
                                        stt(co_[:, h, 0:256], ap(cs_, 0, 128, h * 256, [[1, 256]]), w1r[:, col:col + 1], tmpc[:, h, :], ALU.mult, ALU.add, [csB, w1rB, tmpcB], [coB])
                                    dma_out(dap(s_c, ((l * 16 + j) * 4) * 128 * 256, [[256, 128], [128 * 256, 4], [1, 256]]), co_[:, :, 0:256], coB, final=True)
                                    yield
                                tt(nout[:, :], ndl[:, :], w2r[:, :], ALU.mult, [ndlB, w2rB], [noutB])
                                tt(ndl[:, :], n0T[:, :], w1r[:, :], ALU.mult, [n0TB, w1rB], [ndlB])
                                tt(nout[:, :], nout[:, :], ndl[:, :], ALU.add, [noutB, ndlB], [noutB])
                                b = bk()
                                tr(b[0][0:64, 0:128], nout[:, 0:64], identf[:, :], [noutB, cfB], [b[2]], True)
                                no2, no2B = P("no2", [64, 128], F32, sem=True)
                                cp(no2[:, :], b[0][0:64, 0:128], [b[2]], [no2B])
                                rel(b)
                                dma_out(s_n[l, :, :], no2[:, :], no2B, final=True)
                            for h in range(4):
                                cp(gden[:L, h:h + 1], bn[h][0][:L, 256:257], [bn[h][2]], [gdenB])
                            A(gden[:L, 4:8], gden[:L, 0:4], AF.Abs, [gdenB], [gdenB])
                            tt(gden[:L, 4:8], gden[:L, 4:8], gfl[:L, :], ALU.max, [gdenB, gflB], [gdenB])
                            rcp(gt[:L, 0:4], gden[:L, 4:8], [gdenB], [gtB])
                            for h in range(4):
                                A(jk[:L, :], bn[h][0][:L, 0:256], AF.Square, [bn[h][2]], [jkB, gssB], accum=gss[:L, h:h + 1])
                            yield
                            tt(gt[:L, 4:8], gt[:L, 0:4], gt[:L, 0:4], ALU.mult, [gtB], [gtB])
                            tt(gt[:L, 4:8], gt[:L, 4:8], gss[:L, :], ALU.mult, [gtB, gssB], [gtB])
                            A(gt[:L, 4:8], gt[:L, 4:8], AF.Ln, [gtB, cfB], [gtB], scale=1.0 / 256, bias=cf[:L, C_EPS:C_EPS + 1])
                            A(gt[:L, 8:12], gt[:L, 4:8], AF.Exp, [gtB], [gtB], scale=-0.5)
                            stt(gsc[:L, :], gt[:L, 0:4], 0.5, gt[:L, 8:12], ALU.mult, ALU.mult, [gtB], [gscB])
                            for h in range(4):
                                stt(yml[:L, h * 256:(h + 1) * 256], bn[h][0][:L, 0:256], gsc[:L, h:h + 1], og[:L, h * 256:(h + 1) * 256], ALU.mult, ALU.mult,
                                    [bn[h][2], gscB, ogB], [ymlB])
                            rel(*bn)
                            yield
                            pt = bk()
                            for j in range(8):
                                tr(pt[1][:, j * 128:j * 128 + L], yml[:L, j * 128:(j + 1) * 128], identb[:L, :L], [ymlB, cbB], [pt[2]], j == 7)
                            tt(yT[:, :, :L], ap(pt[1], 0, 128, 0, [[128, 8], [1, L]]), ppbc(P_MG, 8, L), ALU.mult, [pt[2], ppB], [yTB])
                            rel(pt)
                            yield
                            yield from tail(i, "A", "first", yT, yTB, xn, xnB, T)
                            if i == 16:
                                A(mst[0:4, 3:4], mst[0:4, 0:1], AF.Exp, [mstB], [mstB], scale=-1.0)
                                repl4((mst, 3, mstB), 1, rr, rrB, emr, emrB)
                                co_, coB = co[0]
                                for h in range(4):
                                    ts(co_[:, h, 0:257], CN[:, h, 0:257], emr[:, h:h + 1], None, ALU.mult, None, [CNB, emrB], [coB])
                                dma_out(dap(p_c, l * 4 * 128 * 256, [[256, 128], [128 * 256, 4], [1, 256]]), co_[:, :, 0:256], coB, final=True)
                                dma_out(dap(p_n, l * 512, [[1, 128], [128, 4], [1, 1]]), ap(co_, 0, 128, 256, [[260, 4], [1, 1]]), coB, final=True, allow_slow_non_contiguous=True)
                                dma_out(dap(p_m, l * 4, [[1, 4], [1, 1]]), mst[0:4, 0:1], mstB, final=True)

                        drive(front, back, "A", ["q", "k", "v0", "v1", "o0", "o1", "z0", "z1"], nxt_phase)

                def phaseB(nxt_phase):
                    with ExitStack() as pst:
                        P = lambda n, s, dt=F32, sem=False: sbt(f"b_{n}_{l}", s, dt, sem=sem, stack=pst)
                        xf, xfB = P("xf", [128, 1408], BF16)
                        wbd, wbdB = P("wbd", [128, 2, 8, 128], BF16, sem=True)
                        for a_ in range(2):
                            k.dma(pool, wbd[:, a_, :, :], dap(wbd_d, (l * 2 + a_) * 8 * 128 * 128, [[128, 128], [128 * 128, 8], [1, 128]]), W=[wbdB], sem=wbdB.sem)
                        dgl, dglB = P("dgl", [128, 32, 128], BF16)
                        tt(dgl[:, :, :], ap(cb, 0, 128, 0, [[0, 32], [1, 128]]), ap(ppt, 0, 128, l * NPP + P_LCW, [[1, 32], [0, 128]]), ALU.mult, [cbB, ppB], [dglB])
                        xc, xcB = P("xc", [128, 8, 128]); xcb, xcbB = P("xcb", [128, 8, 128], BF16)
                        Rr, RrB = P("R", [128, 8, 128]); Ii, IiB = P("I", [128, 8, 128]); Tt, TtB = P("T", [128, 8, 128]); Hh, HhB = P("H", [128, 8, 128])
                        yT2 = [P(f"yT{i_}", [128, 8, 128], BF16) for i_ in range(2)]
                        sg, sgB = P("sg", [128, 1024])
                        hc, hcB = P("hc", [128, 8]); cA, cAB = P("cA", [128, 8]); tq, tqB = P("tq", [128, 8, 16])
                        xcBk = [Buf(f"xc{kk}_{l}") for kk in range(8)]
                        ctmp = [P(f"ctmp{i_}", [128, 128]) for i_ in range(2)]
                        ho, hoB = P("ho", [128, 1024], F32, sem=True)
                        T = {"sg": (sg, sgB)}
                        A(cA[:, :], ppc(P_LAM, 8), AF.Exp, [ppB], [cAB], scale=-1.0)
                        A(cA[:, :], cA[:, :], AF.Ln, [cAB, cfB], [cAB], bias=cf[:, C_ONES:C_ONES + 1])
                        ts(cA[:, :], cA[:, :], -4.0, None, ALU.mult, None, [cAB], [cAB])
                        hb, hbB = P("hb", [128, 16])
                        ts(hb[:, :], ppc(P_LBA, 16), 0.5, None, ALU.mult, None, [ppB], [hbB])
                        mset(xf[:], 0.0, [xfB]); mset(hc[:], 0.0, [hcB])

                        def front(i):
                            t0, L = TILES[i]
                            smp = (i == SAMPLE)
                            xn, xnB = XN[i % 3]
                            yT, yTB = yT2[i % 2]
                            if i + 1 < 18:
                                xn_load(i + 1)
                            if smp:
                                xwin = lambda kk, j: ap(xf, 0, 128, kk * 176 + j, [[11, 16], [1, 8]])
                                xcv = lambda kk: ap(xc, 0, 128, kk * 128, [[8, 16], [1, 8]])
                                s48, s48B = XT[3]
                                dma_in(s48[0:48, :], st_cl[l, :, :], s48B)
                                b = bk("F")
                                for kk in range(8):
                                    tr(b[0][:, kk * 48:(kk + 1) * 48], s48[0:48, kk * 128:(kk + 1) * 128], identf[0:48, 0:48], [s48B, cfB], [b[2]], kk == 7)
                                cp(ap(xf, 0, 128, 0, [[176, 8], [11, 16], [1, 3]]), ap(b[0], 0, 128, 0, [[48, 8], [3, 16], [1, 3]]), [b[2]], [xfB])
                                rel(b)
                            else:
                                xwin = lambda kk, j: ap(xf, 0, 128, kk * 131 + j, [[1, L]])
                                xcv = lambda kk: xc[:, kk, :L]
                            for n in range(2):
                                wt_, wB = Wt(l, "B", f"lx{n}")
                                b = bk("F")
                                for m in range(4):
                                    for kk in range(8):
                                        mm(b[0][:, m * 128:m * 128 + L], wt_[:, kk, m * 128:(m + 1) * 128], xn[:, kk, :L], kk == 0, kk == 7, [wB, xnB], [b[2]], kk == 7 and m == 3)
                                if smp:
                                    A(ap(xf, 0, 128, n * 4 * 176 + 3, [[176, 4], [11, 16], [1, 8]]), ap(b[0], 0, 128, 0, [[128, 4], [8, 16], [1, 8]]), AF.Copy, [b[2]], [xfB])
                                else:
                                    A(ap(xf, 0, 128, n * 4 * 131 + 3, [[131, 4], [1, L]]), ap(b[0], 0, 128, 0, [[128, 4], [1, L]]), AF.Copy, [b[2]], [xfB])
                                rel(b)
                                yield
                            if smp or i == 16:
                                lt, ltB = XT[3]
                                bt_ = [bk("F"), bk("F")]
                                for n in range(2):
                                    wt_, wB = Wt(l, "B", f"lx{n}")
                                    for kk in range(8):
                                        mm(bt_[n][0][:L, :], xn[:, kk, :L], wt_[:, kk, :], kk == 0, kk == 7, [xnB, wB], [bt_[n][2]], kk == 7)
                                for n in range(2):
                                    A(lt[:L, n * 512:(n + 1) * 512], bt_[n][0][:L, :], AF.Copy, [bt_[n][2]], [ltB])
                                rel(*bt_)
                                if smp:
                                    for r in range(3):
                                        dma_out(dap(s_cl, l * 16 * 3 * 1024 + r * 1024, [[3 * 1024, 16], [1, 1024]]), bass.AP(lt, (5 + r) * 1024, [[8 * 1024, 16], [1, 1024]]), ltB, final=True)
                                else:
                                    dma_out(p_cl[l, :, :], lt[13:16, :], ltB, final=True)
                                yield
                            for g4 in range(2):
                                b = bk("F")
                                ov = (lambda q: ap(b[0], 0, 128, q * 128, [[8, 16], [1, 8]])) if smp else (lambda q: b[0][:, q * 128:q * 128 + L])
                                for q in range(4):
                                    kk = 4 * g4 + q
                                    for j in range(4):
                                        mm(ov(q), dgl[:, kk * 4 + j, :], xwin(kk, j), j == 0, j == 3, [dglB, xfB], [b[2]], j == 3 and q == 3)
                                for q in range(4):
                                    kk = 4 * g4 + q
                                    A(xcv(kk), ov(q), AF.Identity, [b[2], ppB], [xcBk[kk]], bias=ppc(P_LCB + kk))
                                rel(b)
                                yield
                            if not smp:
                                cp(ap(xf, 0, 128, 0, [[131, 8], [1, 3]]), ap(xf, 0, 128, L, [[131, 8], [1, 3]]), [xfB], [xfB])
                            A(xcb[:, :, :L], xc[:, :, :L], AF.Copy, xcBk, [xcbB])
                            for (a_, dst, dstB, bcol) in ((0, Rr, RrB, P_LBA), (1, Ii, IiB, P_LBX)):
                                bb = [bk("F"), bk("F")]
                                for kk in range(8):
                                    mm(bb[kk // 4][0][:, (kk % 4) * 128:(kk % 4) * 128 + L], wbd[:, a_, kk, :], xcb[:, kk, :L], True, True, [wbdB, xcbB], [bb[kk // 4][2]], kk % 4 == 3)
                                for kk in range(8):
                                    A(dst[:, kk, :L], bb[kk // 4][0][:, (kk % 4) * 128:(kk % 4) * 128 + L], AF.Tanh, [bb[kk // 4][2], hbB], [dstB], scale=0.5, bias=hb[:, bcol - P_LBA + kk:bcol - P_LBA + kk + 1])
                                rel(*bb)
                                yield
                            bz_ = [bk("F"), bk("F")]
                            for n in range(2):
                                wt_, wB = Wt(l, "B", f"lz{n}")
                                for m in range(4):
                                    for kk in range(8):
                                        mm(bz_[n][0][:, m * 128:m * 128 + L], wt_[:, kk, m * 128:(m + 1) * 128], xn[:, kk, :L], kk == 0, kk == 7, [wB, xnB], [bz_[n][2]], kk == 7 and m == 3)
                                yield
                            stt(Rr[:, :, :L], Rr[:, :, :L], 1.0, ap(cA, 0, 128, 0, [[1, 8], [0, L]]), ALU.add, ALU.mult, [RrB, cAB], [RrB])
                            A(Rr[:, :, :L], Rr[:, :, :L], AF.Exp, [RrB], [RrB])
                            tt(Tt[:, :, :L], Rr[:, :, :L], Rr[:, :, :L], ALU.mult, [RrB], [TtB])
                            A(Tt[:, :, :L], Tt[:, :, :L], AF.Ln, [TtB, cfB], [TtB], scale=-1.0, bias=cf[:, C_ONES:C_ONES + 1])
                            A(Tt[:, :, :L], Tt[:, :, :L], AF.Exp, [TtB], [TtB], scale=0.5)
                            yield
                            stt(Ii[:, :, :L], Ii[:, :, :L], 1.0, xc[:, :, :L], ALU.add, ALU.mult, [IiB] + xcBk, [IiB])
                            stt(Ii[:, :, :L], Ii[:, :, :L], 0.5, Tt[:, :, :L], ALU.mult, ALU.mult, [IiB, TtB], [IiB])
                            yield
                            if smp:
                                h0, h0B = XT[2]
                                dma_in(h0[0:16, :], st_hl[l, :, :], h0B)
                                b = bk("F")
                                for kk in range(8):
                                    tr(b[0][:, kk * 16:(kk + 1) * 16], h0[0:16, kk * 128:(kk + 1) * 128], identf[0:16, 0:16], [h0B, cfB], [b[2]], kk == 7)
                                a0 = ap(Rr, 0, 128, 0, [[128, 8], [8, 16]])
                                u0 = ap(Ii, 0, 128, 0, [[128, 8], [8, 16]])
                                tt(tq[:, :, :], a0, ap(b[0], 0, 128, 0, [[16, 8], [1, 16]]), ALU.mult, [RrB, b[2]], [tqB])
                                rel(b)
                                tt(u0, u0, tq[:, :, :], ALU.add, [IiB, tqB], [IiB])
                                mset(a0, 0.0, [RrB])
                                for kk in range(8):
                                    k.op(dve, lambda e: e.tensor_tensor_scan(out=Hh[:, kk, :], data0=Rr[:, kk, :], data1=Ii[:, kk, :], initial=0.0, op0=ALU.mult, op1=ALU.add),
                                         [RrB, IiB], [HhB])
                                cp(tq[:, :, :], ap(Hh, 0, 128, 7, [[128, 8], [8, 16]]), [HhB], [tqB])
                                for n in range(2):
                                    b2 = bk("F")
                                    for kk in range(4):
                                        tr(b2[0][0:16, kk * 128:(kk + 1) * 128], tq[:, n * 4 + kk, :], identf[:, :], [tqB, cfB], [b2[2]], kk == 3)
                                    cp(ho[0:16, n * 512:(n + 1) * 512], b2[0][0:16, :], [b2[2]], [hoB])
                                    rel(b2)
                                dma_out(s_hl[l, :, :], ho[0:16, :], hoB, final=True)
                            else:
                                for kk in range(8):
                                    k.op(dve, lambda e: e.tensor_tensor_scan(out=Hh[:, kk, :L], data0=Rr[:, kk, :L], data1=Ii[:, kk, :L], initial=hc[:, kk:kk + 1], op0=ALU.mult, op1=ALU.add),
                                         [RrB, IiB, hcB], [HhB])
                                cp(hc[:, :], ap(Hh, 0, 128, L - 1, [[128, 8]]), [HhB], [hcB])
                                if i == 16:
                                    b = bk("F")
                                    tr(b[0][0:8, 0:128], hc[:, 0:8], identf[:, :], [hcB, cfB], [b[2]], True)
                                    cp(ho[0:8, 0:128], b[0][0:8, 0:128], [b[2]], [hoB])
                                    rel(b)
                                    dma_out(dap(p_hl, l * 1024, [[128, 8], [1, 128]]), ho[0:8, 0:128], hoB, final=True)
                            yield
                            for n in range(2):
                                A(Tt[:, 4 * n:4 * n + 4, :L], ap(bz_[n][0], 0, 128, 0, [[128, 4], [1, L]]), AF.Silu, [bz_[n][2]], [TtB])
                            rel(*bz_)
                            tt(yT[:, :, :L], Hh[:, :, :L], Tt[:, :, :L], ALU.mult, [HhB, TtB], [yTB])
                            yield

                        def back(i):
                            xn, xnB = XN[i % 3]
                            yT, yTB = yT2[i % 2]
                            if i + 1 < 18:
                                tail_prefetch(i + 1, "mid")
                            yield from tail(i, "B", "mid", yT, yTB, xn, xnB, T)

                        tail_prefetch(0, "mid")
                        drive(front, back, "B", ["lx0", "lx1", "lz0", "lz1"], nxt_phase)

                def phaseC(nxt_phase):
                    with ExitStack() as pst:
                        P = lambda n, s, dt=F32, sem=False: sbt(f"c_{n}_{l}", s, dt, sem=sem, stack=pst)
                        P2 = lambda n, s, dt=F32: [P(f"{n}{i_}", s, dt) for i_ in range(2)]
                        xf, xfB = P("xf", [128, 2096], BF16)
                        dg, dgB = P("dg", [128, 64, 128], BF16)
                        tt(dg[:, :, :], ap(cb, 0, 128, 0, [[0, 64], [1, 128]]), ap(ppt, 0, 128, l * NPP + P_SCW, [[1, 64], [0, 128]]), ALU.mult, [cbB, ppB], [dgB])
                        xbcb2 = P2("xbcb", [128, 16, 128], BF16)
                        xdt2 = P2("xdt", [128, 1024], BF16); xw2 = P2("xw", [128, 1024], BF16); xsD2 = P2("xsD", [128, 1024], BF16)
                        btok2 = P2("btok", [128, 512], BF16)
                        Xq, XqB = P("Xq", [128, 4, 128]); Eq, EqB = P("Eq", [128, 4, 128])
                        Wq = [P(f"Wq{i_}", [128, 4, 128], BF16) for i_ in range(2)]
                        Gm2 = P2("Gm", [128, 4, 128], BF16)
                        ST, STB = P("ST", [128, 1024], F32, sem=True); STb, STbB = P("STb", [128, 1024], BF16)
                        yc, ycB = P("yc", [128, 1024], F32, sem=True); sg, sgB = P("sg", [128, 1024])
                        ltb, ltbB = P("ltb", [128, 1024], F32, sem=True)
                        ytok, ytokB = P("ytok", [128, 1024], BF16); yT, yTB = P("yT", [128, 8, 128], BF16)
                        mT, mTB = P("mT", [128, 8, 128], BF16); mT2 = mT.reshape([128, 1024])
                        d0, d0B = P("d0", [128, 32]); dtt, dttB = P("dt", [128, 16]); dta2 = P2("dta", [128, 16])
                        cum, cumB = P("cum", [128, 16]); ecum2 = P2("ecum", [128, 16]); wsx, wsxB = P("wsx", [128, 16]); ecl2 = P2("ecl", [128, 16])
                        aneg, anegB = P("aneg", [128, 16]); ss, ssB = P("ss", [128, 4]); fsm, fsmB = P("fsm", [128, 4])
                        T = {"sg": (sg, sgB), "mrg": (ytok, ytokB), "mT": (mT, mTB), "fsm": (fsm, fsmB)}
                        A(aneg[:, :], prt[:, l, R_ALOG:R_ALOG + 16], AF.Exp, [prB], [anegB])
                        ts(aneg[:, :], aneg[:, :], -1.0, None, ALU.mult, None, [anegB], [anegB])
                        mset(xf[:], 0.0, [xfB]); mset(ST[:], 0.0, [STB]); mset(STb[:], 0.0, [STbB])

                        def front(i):
                            t0, L = TILES[i]
                            smp = (i == SAMPLE)
                            par = i % 2
                            xn, xnB = XN[i % 3]
                            if i + 1 < 18:
                                xn_load(i + 1)
                            xbcb, xbcbB = xbcb2[par]; xdt, xdtB = xdt2[par]; xw, xwB = xw2[par]; xsD, xsDB = xsD2[par]
                            btok, btokB = btok2[par]; Gm, GmB = Gm2[par]; dta, dtaB = dta2[par]; ecum, ecumB = ecum2[par]; ecl, eclB = ecl2[par]
                            C_TRI = C_TRIS if smp else C_TRIP
                            C_ONE = C_SAMES if smp else C_ONES
                            def conv_chunks(ms, xwin, accv):
                                ms = list(ms)
                                for g0 in range(0, len(ms), 4):
                                    grp = ms[g0:g0 + 4]
                                    b = bk("F")
                                    ov = (lambda q: ap(b[0], 0, 128, q * 128, [[8, 16], [1, 8]])) if smp else (lambda q: b[0][:, q * 128:q * 128 + L])
                                    for q, m in enumerate(grp):
                                        for j in range(4):
                                            mm(ov(q), dg[:, m * 4 + j, :], xwin(m, j), j == 0, j == 3, [dgB, xfB], [b[2]], j == 3 and q == len(grp) - 1)
                                    for q, m in enumerate(grp):
                                        A(xbcb[:, m, :L], b[0][:, q * 128:q * 128 + L], AF.Silu, [b[2], ppB], [xbcbB], bias=ppc(P_SCB + m))
                                    rel(b)
                                    yield

                            if smp:
                                accv = lambda a_: ap(a_, 0, 128, 0, [[8, 16], [1, 8]])
                                s48, s48B = ltb, ltbB
                                for hf in range(2):
                                    xwin = lambda m, j, hf=hf: ap(xf, 0, 128, (m - 8 * hf) * 176 + j, [[11, 16], [1, 8]])
                                    dma_in(s48[0:48, :], st_cs[l, :, hf * 1024:(hf + 1) * 1024], s48B)
                                    b = bk("F")
                                    for kk in range(8):
                                        tr(b[0][:, kk * 48:(kk + 1) * 48], s48[0:48, kk * 128:(kk + 1) * 128], identf[0:48, 0:48], [s48B, cfB], [b[2]], kk == 7)
                                    cp(ap(xf, 0, 128, 0, [[176, 8], [11, 16], [1, 3]]), ap(b[0], 0, 128, 0, [[48, 8], [3, 16], [1, 3]]), [b[2]], [xfB])
                                    rel(b)
                                    for n in range(2):
                                        wt_, wB = Wt(l, "C", f"xb{2 * hf + n}")
                                        b = bk("F")
                                        for m in range(4):
                                            for kk in range(8):
                                                mm(b[0][:, m * 128:m * 128 + L], wt_[:, kk, m * 128:(m + 1) * 128], xn[:, kk, :L], kk == 0, kk == 7, [wB, xnB], [b[2]], kk == 7 and m == 3)
                                        A(ap(xf, 0, 128, n * 4 * 176 + 3, [[176, 4], [11, 16], [1, 8]]), ap(b[0], 0, 128, 0, [[128, 4], [8, 16], [1, 8]]), AF.Copy, [b[2]], [xfB])
                                        rel(b)
                                        yield
                                    yield from conv_chunks(range(8 * hf, 8 * hf + 8), xwin, accv)
                            else:
                                xwin = lambda m, j: ap(xf, 0, 128, m * 131 + j, [[1, L]])
                                accv = lambda a_: a_[:, :L]
                                for n in range(4):
                                    wt_, wB = Wt(l, "C", f"xb{n}")
                                    b = bk("F")
                                    for m in range(4):
                                        for kk in range(8):
                                            mm(b[0][:, m * 128:m * 128 + L], wt_[:, kk, m * 128:(m + 1) * 128], xn[:, kk, :L], kk == 0, kk == 7, [wB, xnB], [b[2]], kk == 7 and m == 3)
                                    A(ap(xf, 0, 128, n * 4 * 131 + 3, [[131, 4], [1, L]]), ap(b[0], 0, 128, 0, [[128, 4], [1, L]]), AF.Copy, [b[2]], [xfB])
                                    rel(b)
                                    yield
                            if smp or i == 16:
                                for hf in range(2):
                                    bt_ = [bk("F"), bk("F")]
                                    for n in range(2):
                                        wt_, wB = Wt(l, "C", f"xb{hf * 2 + n}")
                                        for kk in range(8):
                                            mm(bt_[n][0][:L, :], xn[:, kk, :L], wt_[:, kk, :], kk == 0, kk == 7, [xnB, wB], [bt_[n][2]], kk == 7)
                                    for n in range(2):
                                        A(ltb[:L, n * 512:(n + 1) * 512], bt_[n][0][:L, :], AF.Copy, [bt_[n][2]], [ltbB])
                                    rel(*bt_)
                                    if smp:
                                        for r in range(3):
                                            dma_out(dap(s_cs, l * 16 * 3 * 2048 + r * 2048 + hf * 1024, [[3 * 2048, 16], [1, 1024]]), bass.AP(ltb, (5 + r) * 1024, [[8 * 1024, 16], [1, 1024]]), ltbB, final=True)
                                    else:
                                        dma_out(p_cs[l, :, hf * 1024:(hf + 1) * 1024], ltb[13:16, :], ltbB, final=True)
                                    yield
                            bd = bk("F")
                            for kk in range(8):
                                mm(bd[0][:L, 0:16], xn[:, kk, :L], wdt[:, kk, :], kk == 0, kk == 7, [xnB, wdtB], [bd[2]], kk == 7)
                            tt(d0[:L, 0:16], bd[0][:L, 0:16], prc(R_DTB, 16, L), ALU.add, [bd[2], prB], [d0B])
                            A(d0[:L, 16:32], d0[:L, 0:16], AF.Exp, [d0B], [d0B])
                            A(dtt[:L, :], d0[:L, 16:32], AF.Ln, [d0B, cfB], [dttB], bias=cf[:L, C_ONES:C_ONES + 1])
                            tt(dta[:L, :], dtt[:L, :], aneg[:L, :], ALU.mult, [dttB, anegB], [dtaB])
                            yield
                            if not smp:
                                yield from conv_chunks(range(16), xwin, accv)
                                cp(ap(xf, 0, 128, 0, [[131, 16], [1, 3]]), ap(xf, 0, 128, L, [[131, 16], [1, 3]]), [xfB], [xfB])
                            mm(bd[0][:L, 16:32], cfm(C_TRI, L), dta[:L, :], True, True, [cfB, dtaB], [bd[2]], False)
                            mm(bd[0][:, 32:48], cf[:L, C_ONE:C_ONE + 128], dta[:L, :], True, True, [cfB, dtaB], [bd[2]], True)
                            cp(cum[:L, :], bd[0][:L, 16:32], [bd[2]], [cumB])
                            A(ecum[:L, :], cum[:L, :], AF.Exp, [cumB], [ecumB])
                            tt(wsx[:L, :], bd[0][:L, 32:48], cum[:L, :], ALU.subtract, [bd[2], cumB], [wsxB])
                            A(wsx[:L, :], wsx[:L, :], AF.Exp, [wsxB], [wsxB])
                            tt(wsx[:L, :], wsx[:L, :], dtt[:L, :], ALU.mult, [wsxB, dttB], [wsxB])
                            A(ecl[:, :], bd[0][:, 32:48], AF.Exp, [bd[2]], [eclB])
                            rel(bd)
                            yield
                            pxs = bk("F")
                            for m in range(8):
                                tr(pxs[1][:L, m * 128:(m + 1) * 128], xbcb[:, m, :L], identb[:, :], [xbcbB, cbB], [pxs[2]], m == 7)
                            pB_ = bk("F")
                            for g in range(4):
                                tr(pB_[1][:L, g * 128:(g + 1) * 128], xbcb[:, 8 + g, :L], identb[:, :], [xbcbB, cbB], [pB_[2]], g == 3)
                            cp(btok[:L, :], pB_[1][:L, 0:512], [pB_[2]], [btokB])
                            rel(pB_)
                            xsv = ap(pxs[1], 0, L, 0, [[64, 16], [1, 64]])
                            o3 = lambda t_: ap(t_, 0, L, 0, [[64, 16], [1, 64]])
                            tt(o3(xdt), xsv, ap(dtt, 0, L, 0, [[1, 16], [0, 64]]), ALU.mult, [pxs[2], dttB], [xdtB])
                            yield
                            tt(o3(xw), xsv, ap(wsx, 0, L, 0, [[1, 16], [0, 64]]), ALU.mult, [pxs[2], wsxB], [xwB])
                            tt(o3(xsD), xsv, ap(prt, 0, L, l * NPR + R_DD, [[1, 16], [0, 64]]), ALU.mult, [pxs[2], prB], [xsDB])
                            rel(pxs)
                            bgm = bk("F")
                            for g in range(4):
                                mm(bgm[0][:L, g * 128:g * 128 + L], xbcb[:, 8 + g, :L], xbcb[:, 12 + g, :L], True, True, [xbcbB], [bgm[2]], g == 3)
                            tt(Gm[:L, :, :L], ap(bgm[0], 0, L, 0, [[128, 4], [1, L]]), ap(cf, 0, L, C_TRI, [[0, 4], [1, L]]), ALU.mult, [bgm[2], cfB], [GmB])
                            rel(bgm)
                            yield

                        def back(i):
                            t0, L = TILES[i]
                            smp = (i == SAMPLE)
                            par = i % 2
                            xn, xnB = XN[i % 3]
                            if i + 1 < 18:
                                tail_prefetch(i + 1, "last")
                            xbcb, xbcbB = xbcb2[par]; xdt, xdtB = xdt2[par]; xw, xwB = xw2[par]; xsD, xsDB = xsD2[par]
                            btok, btokB = btok2[par]; Gm, GmB = Gm2[par]; dta, dtaB = dta2[par]; ecum, ecumB = ecum2[par]; ecl, eclB = ecl2[par]
                            C_TRI = C_TRIS if smp else C_TRIP
                            C_SGT = C_SGTS if smp else C_SGTP
                            if not smp:
                                bi = [bk(), bk()]
                                for g in range(4):
                                    mm(bi[g // 2][0][:L, (g % 2) * 256:(g % 2 + 1) * 256], xbcb[:, 12 + g, :L], STb[:, g * 256:(g + 1) * 256], True, True,
                                       [xbcbB, STbB], [bi[g // 2][2]], g % 2 == 1)
                            else:
                                rb = [P(f"rb{b_}", [128, 16, 8]) for b_ in range(2)]
                                eclP, eclPB = P("eclP", [128, 128])
                                for b_ in range(2):
                                    tt(rb[b_][0][:, :, :], ap(cf, 0, 128, C_ROWSEL, [[1, 16], [0, 8]]), ap(dta, 0, 128, b_, [[0, 16], [2, 8]]), ALU.mult, [cfB, dtaB], [rb[b_][1]])
                                b = bk()
                                mm(b[0][:, 0:128], cf[:, C_HALF0:C_HALF0 + 128], rb[0][0][:].rearrange("p a b -> p (a b)"), True, False, [cfB, rb[0][1]], [b[2]], False)
                                mm(b[0][:, 0:128], cf[:, C_HALF1:C_HALF1 + 128], rb[1][0][:].rearrange("p a b -> p (a b)"), False, True, [cfB, rb[1][1]], [b[2]], True)
                                A(eclP[:, :], b[0][:, 0:128], AF.Exp, [b[2]], [eclPB])
                                rel(b)
                                xwj = [(ytok, ytokB), (mT2, mTB)]
                                so = [(ST, STB), (yc, ycB)]
                                byT = [bk(), bk()]
                                stg = [XT[0], XT[2], (ltb, ltbB)]

                                def ld(j):
                                    s_, sB_ = stg[j % 3]
                                    dma_in(ap(s_, 0, 128, 0, [[128, 8], [1, 128]]), dap(st_hs, (l * 16 + j) * 1024 * 128, [[128, 128], [128 * 128, 8], [1, 128]]), sB_)

                                ld(0)
                                ld(1)
                                for j in range(16):
                                    if j + 2 < 16:
                                        ld(j + 2)
                                    sg_, sgB_ = stg[j % 3]
                                    b2 = [bk("F"), bk("F")]
                                    for c in range(8):
                                        tr(b2[c // 4][0][:, (c % 4) * 128:(c % 4 + 1) * 128], sg_[:, c * 128:(c + 1) * 128], identf[:, :], [sgB_, cfB], [b2[c // 4][2]], c % 4 == 3)
                                    for n in range(2):
                                        A(STb[:, n * 512:(n + 1) * 512], b2[n][0][:, :], AF.Copy, [b2[n][2]], [STbB])
                                    rel(*b2)
                                    for c in range(8):
                                        mm(byT[c // 4][0][:, (c % 4) * 128 + 8 * j:(c % 4) * 128 + 8 * j + 8], STb[:, c * 128:(c + 1) * 128], xbcb[:, 12 + c // 2, 8 * j:8 * j + 8],
                                           True, True, [STbB, xbcbB], [byT[c // 4][2]], c % 4 == 3)
                                    xj, xjB = xwj[j % 2]
                                    A(xj[:, :], xw[:, :], AF.Identity, [xwB, cfB], [xjB], scale=cf[:, C_ROWSEL + j:C_ROWSEL + j + 1])
                                    b3 = [bk(), bk()]
                                    for c in range(8):
                                        g = c // 2
                                        mm(b3[c // 4][0][:, (c % 4) * 128:(c % 4 + 1) * 128], xj[:, c * 128:(c + 1) * 128], btok[:, g * 128:(g + 1) * 128], True, True,
                                           [xjB, btokB], [b3[c // 4][2]], c % 4 == 3)
                                    so_, soB = so[j % 2]
                                    tt(ap(so_, 0, 128, 0, [[128, 8], [1, 128]]), ap(sg_, 0, 128, 0, [[128, 8], [1, 128]]), ap(eclP, 0, 128, j * 8, [[1, 8], [0, 128]]), ALU.mult,
                                       [sgB_, eclPB], [soB])
                                    for n in range(2):
                                        tt(so_[:, n * 512:(n + 1) * 512], so_[:, n * 512:(n + 1) * 512], b3[n][0][:, :], ALU.add, [soB, b3[n][2]], [soB])
                                    rel(*b3)
                                    dma_out(dap(s_hs, (l * 16 + j) * 1024 * 128, [[128, 128], [128 * 128, 8], [1, 128]]), ap(so_, 0, 128, 0, [[128, 8], [1, 128]]), soB, final=True)
                                    yield
                                for n in range(2):
                                    A(yc[:, n * 512:(n + 1) * 512], byT[n][0][:, :], AF.Copy, [byT[n][2]], [ycB])
                                rel(*byT)
                                bi = [bk(), bk()]
                                for c in range(8):
                                    tr(bi[c // 4][0][:, (c % 4) * 128:(c % 4 + 1) * 128], yc[:, c * 128:(c + 1) * 128], identf[:, :], [ycB, cfB], [bi[c // 4][2]], c % 4 == 3)
                            for n in range(2):
                                tt(ap(yc, 0, L, n * 512, [[64, 8], [1, 64]]), ap(bi[n][0], 0, L, 0, [[64, 8], [1, 64]]), ap(ecum, 0, L, n * 8, [[1, 8], [0, 64]]), ALU.mult,
                                   [bi[n][2], ecumB], [ycB])
                            rel(*bi)
                            yield
                            by = [bk(), bk()]
                            for q in range(4):
                                tt(Xq[:L, :, :L], ap(cf, 0, L, C_TRI, [[0, 4], [1, L]]), ap(dta, 0, L, 4 * q, [[1, 4], [0, L]]), ALU.mult, [cfB, dtaB], [XqB])
                                bsg = bk()
                                mm(ap(bsg[0], 0, L, 0, [[128, 4], [1, L]]), cfm(C_SGT, L), Xq[:L, :, :L], True, True, [cfB, XqB], [bsg[2]], True)
                                A(Eq[:L, :, :L], ap(bsg[0], 0, L, 0, [[128, 4], [1, L]]), AF.Exp, [bsg[2]], [EqB])
                                rel(bsg)
                                wq_, wqB = Wq[q % 2]
                                tt(wq_[:L, :, :L], Eq[:L, :, :L], ap(Gm, 0, L, q * 128, [[0, 4], [1, L]]), ALU.mult, [EqB, GmB], [wqB])
                                for e_ in range(4):
                                    h = 4 * q + e_
                                    o_ = by[h // 8][0][:L, (h % 8) * 64:(h % 8 + 1) * 64]
                                    mm(o_, wq_[:L, e_, :L], xdt[:L, h * 64:(h + 1) * 64], True, False, [wqB, xdtB], [by[h // 8][2]], False)
                                    mm(o_, identb[:L, :L], xsD[:L, h * 64:(h + 1) * 64], False, True, [cbB, xsDB], [by[h // 8][2]], True)
                                yield
                            for n in range(2):
                                tt(yc[:L, n * 512:(n + 1) * 512], yc[:L, n * 512:(n + 1) * 512], by[n][0][:L, :], ALU.add, [ycB, by[n][2]], [ycB])
                            rel(*by)
                            bz = [bk(), bk()]
                            for n in range(2):
                                wt_, wB = Wt(l, "C", f"sz{n}")
                                for kk in range(8):
                                    mm(bz[n][0][:L, :], xn[:, kk, :L], wt_[:, kk, :], kk == 0, kk == 7, [xnB, wB], [bz[n][2]], kk == 7)
                            for n in range(2):
                                A(sg[:L, n * 512:(n + 1) * 512], bz[n][0][:L, :], AF.Silu, [bz[n][2]], [sgB])
                            rel(*bz)
                            yield
                            tt(yc[:L, :], yc[:L, :], sg[:L, :], ALU.mult, [ycB, sgB], [ycB])
                            A(ytok[:L, :], yc[:L, :], AF.Square, [ycB], [ytokB, ssB], accum=ss[:L, 0:1])
                            A(ss[:L, 1:2], ss[:L, 0:1], AF.Ln, [ssB, cfB], [ssB], scale=1.0 / 1024, bias=cf[:L, C_EPS:C_EPS + 1])
                            A(ss[:L, 3:4], ss[:L, 1:2], AF.Exp, [ssB], [ssB], scale=-0.5)
                            A(ytok[:L, :], yc[:L, :], AF.Identity, [ycB, ssB], [ytokB], scale=ss[:L, 3:4])
                            yield
                            pt = bk()
                            for j in range(8):
                                tr(pt[1][:, j * 128:j * 128 + L], ytok[:L, j * 128:(j + 1) * 128], identb[:L, :L], [ytokB, cbB], [pt[2]], j == 7)
                            tt(yT[:, :, :L], ap(pt[1], 0, 128, 0, [[128, 8], [1, L]]), ppbc(P_SG, 8, L), ALU.mult, [pt[2], ppB], [yTB])
                            rel(pt)
                            if not smp:
                                bu = [bk(), bk()]
                                for g in range(4):
                                    mm(bu[g // 2][0][:, (g % 2) * 256:(g % 2 + 1) * 256], btok[:L, g * 128:(g + 1) * 128], xw[:L, g * 256:(g + 1) * 256], True, True,
                                       [btokB, xwB], [bu[g // 2][2]], g % 2 == 1)
                                tt(ap(ST, 0, 128, 0, [[64, 16], [1, 64]]), ap(ST, 0, 128, 0, [[64, 16], [1, 64]]), ap(ecl, 0, 128, 0, [[1, 16], [0, 64]]), ALU.mult, [STB, eclB], [STB])
                                for n in range(2):
                                    tt(ST[:, n * 512:(n + 1) * 512], ST[:, n * 512:(n + 1) * 512], bu[n][0][:, :], ALU.add, [STB, bu[n][2]], [STB])
                                rel(*bu)
                                A(STb[:, :], ST[:, :], AF.Copy, [STB], [STbB])
                            yield
                            yield from tail(i, "C", "last", yT, yTB, xn, xnB, T)
                            if i == 16:
                                so_, soB = yc, ycB
                                b2 = [bk(), bk()]
                                for c in range(8):
                                    tr(b2[c // 4][0][:, (c % 4) * 128:(c % 4 + 1) * 128], ST[:, c * 128:(c + 1) * 128], identf[:, :], [STB, cfB], [b2[c // 4][2]], c % 4 == 3)
                                for n in range(2):
                                    cp(so_[:, n * 512:(n + 1) * 512], b2[n][0][:, :], [b2[n][2]], [soB])
                                rel(*b2)
                                dma_out(dap(p_hs, l * 1024 * 128, [[128, 128], [128 * 128, 8], [1, 128]]), ap(so_, 0, 128, 0, [[128, 8], [1, 128]]), soB, final=True)

                        tail_prefetch(0, "last")
                        drive(front, back, "C", ["xb0", "xb1", "xb2", "xb3"], nxt_phase)

                PH = {"A": phaseA, "B": phaseB, "C": phaseC}
                for ph in "ABC":
                    if (l, ph) not in phases:
                        continue
                    ok = load_weights(l, ph)
                    assert ok, "not enough weight slots"
                    nxt = phases.index((l, ph)) + 1
                    if nxt < len(phases):
                        load_weights(*phases[nxt])
                    PH[ph](phases[nxt] if nxt < len(phases) else None)
                    release_weights(l, ph)

            for s in k.dram_out_sems:
                sp.h.wait_ge(s.h, s.val)
        print("instr counts", {e.name: (e.nins, e.nwait) for e in (pe, act, dve, pool, sp)}, "sems", k.nsem)
    return nc


def _consts():
    r = np.arange(128)
    seq = r // 8
    same = (seq[:, None] == seq[None, :]).astype(np.float32)
    cfa = np.zeros((128, NCF), np.float32)
    cfa[:, C_IDF:C_IDF + 128] = np.eye(128)
    cfa[:, C_TRIP:C_TRIP + 128] = (r[:, None] <= r[None, :])
    cfa[:, C_SGTP:C_SGTP + 128] = (r[:, None] > r[None, :])
    cfa[:, C_ONES:C_ONES + 128] = 1.0
    cfa[:, C_TRIS:C_TRIS + 128] = (r[:, None] <= r[None, :]) * same
    cfa[:, C_SGTS:C_SGTS + 128] = (r[:, None] > r[None, :]) * same
    cfa[:, C_SAMES:C_SAMES + 128] = same
    cfa[:, C_ROWSEL:C_ROWSEL + 16] = (seq[:, None] == np.arange(16)[None, :])
    cfa[:, C_HALF0:C_HALF0 + 128] = (r[None, :] // 64 == 0)
    cfa[:, C_HALF1:C_HALF1 + 128] = (r[None, :] // 64 == 1)
    cfa[:, C_EPS] = EPS
    cba = np.zeros((128, NCB), np.float32)
    cba[:, B_IDB:B_IDB + 128] = np.eye(128)
    colsel = (np.arange(16)[:, None] == seq[None, :]).astype(np.float32).reshape(1, 2048)
    cba[:, B_COLSEL:B_COLSEL + 2048] = colsel
    return cfa, cba.astype(ml_dtypes.bfloat16)


_PROG = {}


def kernel(x_prompt, x_sample, state_mlstm_c, state_mlstm_n, state_mlstm_m, state_rglru_h,
           state_rglru_conv, state_ssd_h, state_ssd_conv, meta_tokens, w_in, norm_g, ml_f_bias,
           ml_norm_g, lru_conv_w, lru_conv_b, lru_w_a, lru_b_a, lru_w_x, lru_b_x, lru_lambda,
           ssd_conv_w, ssd_conv_b, ssd_dt_bias, ssd_a_log, ssd_d, ssd_norm_g,
           w_br_ml, w_br_lru, w_br_ssd, w_out, final_norm_g, _cfg=None):
    f = lambda a: np.ascontiguousarray(np.asarray(a, dtype=np.float32))
    x_prompt, x_sample = f(x_prompt), f(x_sample)
    w_in_, w_out_ = f(w_in), f(w_out)
    w_br_ = np.ascontiguousarray(np.stack([f(w_br_ml), f(w_br_lru), f(w_br_ssd)], axis=1))
    wbd = np.zeros((2, 2, 8, 128, 128), np.float32)
    for a_, w_ in enumerate((f(lru_w_a), f(lru_w_x))):
        for n in range(16):
            o = (n % 2) * 64
            wbd[:, a_, n // 2, o:o + 64, o:o + 64] = w_[:, n]
    pp = np.zeros((2, 128, NPP), np.float32)
    pr = np.zeros((2, 128, NPR), np.float32)
    col = lambda v, n: f(v).reshape(2, n, 128).transpose(0, 2, 1)
    pp[:, :, P_NG:P_NG + 8] = col(norm_g, 8)
    pp[:, :, P_MG:P_MG + 8] = col(f(ml_norm_g).reshape(2, 1024), 8)
    pp[:, :, P_SG:P_SG + 8] = col(ssd_norm_g, 8)
    pp[:, :, P_LCW:P_LCW + 32] = f(lru_conv_w).reshape(2, 4, 8, 128).transpose(0, 3, 2, 1).reshape(2, 128, 32)
    pp[:, :, P_LCB:P_LCB + 8] = col(lru_conv_b, 8)
    pp[:, :, P_LAM:P_LAM + 8] = col(lru_lambda, 8)
    pp[:, :, P_SCW:P_SCW + 64] = f(ssd_conv_w).reshape(2, 4, 16, 128).transpose(0, 3, 2, 1).reshape(2, 128, 64)
    pp[:, :, P_SCB:P_SCB + 16] = col(ssd_conv_b, 16)
    pp[:, :, P_LBA:P_LBA + 8] = col(lru_b_a, 8)
    pp[:, :, P_LBX:P_LBX + 8] = col(lru_b_x, 8)
    pr[:, :, R_FB:R_FB + 4] = f(ml_f_bias)[:, None, :]
    pr[:, :, R_DTB:R_DTB + 16] = f(ssd_dt_bias)[:, None, :]
    pr[:, :, R_ALOG:R_ALOG + 16] = f(ssd_a_log)[:, None, :]
    pr[:, :, R_DD:R_DD + 16] = f(ssd_d)[:, None, :]
    fgr = np.ascontiguousarray(np.broadcast_to(f(final_norm_g)[None, :], (128, 1024)))
    cfa, cba = _consts()
    meta = f(meta_tokens)
    smc, smn, smm = f(state_mlstm_c), f(state_mlstm_n), f(state_mlstm_m)
    shl, scl, shs, scs = f(state_rglru_h), f(state_rglru_conv), f(state_ssd_h), f(state_ssd_conv)
    in_maps = []
    for c in range(8):
        sl = slice(16 * c, 16 * c + 16)
        in_maps.append({
            "xp": x_prompt[c], "meta": meta, "xs": x_sample[sl].reshape(128, 1024),
            "st_c": np.ascontiguousarray(smc[:, sl]), "st_n": np.ascontiguousarray(smn[:, sl]).reshape(2, 64, 128),
            "st_m": np.ascontiguousarray(smm[:, sl]), "st_hl": np.ascontiguousarray(shl[:, sl]),
            "st_cl": np.ascontiguousarray(scl[:, sl]).reshape(2, 48, 1024),
            "st_hs": np.ascontiguousarray(shs[:, sl]).reshape(2, 16, 1024, 128),
            "st_cs": np.ascontiguousarray(scs[:, sl]).reshape(2, 48, 2048),
            "w_in": w_in_, "w_br": w_br_, "w_out": w_out_, "wbd": wbd, "pp": pp, "pr": pr, "fg": fgr, "cf": cfa, "cb": cba,
        })
    key = repr(_cfg)
    if key not in _PROG:
        _PROG[key] = build_program(_cfg)
    nc = _PROG[key]
    res = run_bass_kernel_spmd(nc, in_maps, core_ids=list(range(8)))
    R = res.results
    cat = lambda n, ax: np.concatenate([np.asarray(r[n]) for r in R], axis=ax)
    y_prompt = np.stack([np.asarray(r["y_p"]) for r in R], axis=0)
    y_sample = cat("y_s", 0).reshape(128, 8, 1024)
    stk = lambda n: np.stack([np.asarray(r[n]) for r in R], axis=1)
    p_c = stk("p_c"); p_n = stk("p_n"); p_m = stk("p_m"); p_hl = stk("p_hl"); p_cl = stk("p_cl")
    p_hs = stk("p_hs").reshape(2, 8, 16, 64, 128); p_cs = stk("p_cs")
    s_c = cat("s_c", 1); s_n = cat("s_n", 1).reshape(2, 128, 4, 128); s_m = cat("s_m", 1)
    s_hl = cat("s_hl", 1); s_cl = cat("s_cl", 1); s_hs = cat("s_hs", 1).reshape(2, 128, 16, 64, 128); s_cs = cat("s_cs", 1)
    outs = (y_prompt, y_sample, p_c, p_n, p_m, p_hl, p_cl, p_hs, p_cs, s_c, s_n, s_m, s_hl, s_cl, s_hs, s_cs)
    return tuple(np.ascontiguousarray(o, dtype=np.float32) for o in outs)
```

```python
import numpy as np
import ml_dtypes
from contextlib import ExitStack
import concourse.bass as bass
import concourse.mybir as mybir
from concourse.bass_utils import run_bass_kernel_spmd

F32 = mybir.dt.float32
BF16 = mybir.dt.bfloat16
AF = mybir.ActivationFunctionType
ALU = mybir.AluOpType
AX = mybir.AxisListType

EPS = 1e-6
NIN = 12312
OQ, OK_, OV, OI, OO, OZ = 0, 512, 1024, 2048, 2056, 3080
OLX, OLZ = 4104, 5128
OSZ, OXBC, ODT, OG = 6152, 7176, 9224, 9240
TILES = [(i * 128, 128) for i in range(16)] + [(2048, 16), (2064, 128)]
NTOK = 2192
SAMPLE = 17
NSLOT = 12
P_NG, P_MG, P_SG, P_LCW, P_LCB, P_LAM, P_SCW, P_SCB, P_LBA, P_LBX, NPP = 0, 8, 16, 24, 56, 64, 72, 136, 152, 160, 168
R_FB, R_DTB, R_ALOG, R_DD, NPR = 0, 4, 20, 36, 52
C_IDF, C_TRIP, C_SGTP, C_ONES, C_TRIS, C_SGTS, C_SAMES, C_ROWSEL, C_HALF0, C_HALF1, C_EPS, NCF = 0, 128, 256, 384, 512, 640, 768, 896, 912, 1040, 1168, 1172
B_IDB, B_COLSEL, NCB = 0, 128, 128 + 2048


class Sem:
    def __init__(self, h, name):
        self.h = h
        self.name = name
        self.val = 0


class Buf:
    __slots__ = ("name", "excl", "w", "rd", "sem")

    def __init__(self, name, excl=False, sem=None):
        self.name = name
        self.excl = excl
        self.w = None
        self.rd = {}
        self.sem = sem


class Eng:
    def __init__(self, name, h, sem, is_pe=False):
        self.name = name
        self.h = h
        self.sem = sem
        self.is_pe = is_pe
        self.waited = {}
        self.nwait = 0
        self.nins = 0


class K:
    def __init__(self, nc, stack):
        self.nc = nc
        self.stack = stack
        self.nsem = 0
        self.pe = Eng("pe", nc.tensor, self.new_sem("s_pe"), is_pe=True)
        self.act = Eng("act", nc.scalar, self.new_sem("s_act"))
        self.dve = Eng("dve", nc.vector, self.new_sem("s_dve"))
        self.pool = Eng("pool", nc.gpsimd, self.new_sem("s_pool"))
        self.sp = Eng("sp", nc.sync, self.new_sem("s_sp"))
        self.dram_out_sems = {}

    def new_sem(self, name):
        self.nsem += 1
        return Sem(self.stack.enter_context(self.nc.semaphore(name)), name)

    def _deps(self, engid, is_pe, R, W):
        deps = []
        for b in R:
            if b.w is not None:
                e, s, v = b.w
                if not (e == engid and is_pe):
                    deps.append((s, v))
            if b.excl:
                for (e, s), v in b.rd.items():
                    if e != engid:
                        deps.append((s, v))
        for b in W:
            if b.w is not None:
                e, s, v = b.w
                if e != engid or engid == "dma":
                    deps.append((s, v))
            for (e, s), v in b.rd.items():
                if e != engid or engid == "dma":
                    deps.append((s, v))
        return deps

    def _wait(self, eng, deps):
        best = {}
        for s, v in deps:
            if v > best.get(s, 0):
                best[s] = v
        for s, v in best.items():
            if eng.waited.get(s, 0) >= v:
                continue
            if v > s.val:
                raise RuntimeError(f"wait on un-emitted milestone {s.name} {v}>{s.val} from {eng.name}")
            eng.h.wait_ge(s.h, v)
            eng.waited[s] = v
            eng.nwait += 1

    def _mark(self, engid, ev_sem, ev_val, R, W):
        for b in R:
            b.rd[(engid, ev_sem)] = ev_val
        for b in W:
            b.w = (engid, ev_sem, ev_val)
            b.rd = {}

    def op(self, eng, fn, R=(), W=(), inc=True):
        deps = self._deps(eng.name, eng.is_pe, R, W)
        self._wait(eng, deps)
        ins = fn(eng.h)
        eng.nins += 1
        if inc:
            eng.sem.val += 1
            ins.then_inc(eng.sem.h, 1)
            val = eng.sem.val
        else:
            val = eng.sem.val + 1
        self._mark(eng.name, eng.sem, val, R, W)
        return ins

    def dma(self, eng, out, in_, R=(), W=(), sem=None, is_out=False, **kw):
        deps = self._deps("dma", False, R, W)
        self._wait(eng, deps)
        ins = eng.h.dma_start(out=out, in_=in_, **kw)
        eng.nins += 1
        sem.val += 16
        ins.then_inc(sem.h, 16)
        self._mark("dma", sem, sem.val, R, W)
        if is_out:
            self.dram_out_sems[sem] = sem.val
        return ins

    def barrier(self, engs):
        for e in engs:
            deps = [(o.sem, o.sem.val) for o in engs if o is not e]
            self._wait(e, deps)


def ap(t, p0, npart, f0, dims):
    F = 1
    for s in t.shape[1:]:
        F *= s
    return bass.AP(t, p0 * F + f0, [[F, npart]] + [list(d) for d in dims])


def dap(t, off, dims):
    return bass.AP(t, off, [list(d) for d in dims])


def build_program(cfg=None):
    cfg = cfg or {}
    nlayers = cfg.get("nlayers", 2)
    nc = bass.Bass("TRN2", target_bir_lowering=False)
    di = lambda n, s, dt=F32: nc.dram_tensor(n, list(s), dt, kind="ExternalInput")
    do = lambda n, s: nc.dram_tensor(n, list(s), F32, kind="ExternalOutput")
    dint = lambda n, s, dt=F32: nc.dram_tensor(n, list(s), dt, kind="Internal")
    xp = di("xp", [2048, 1024]); meta = di("meta", [16, 1024]); xs = di("xs", [128, 1024])
    st_c = di("st_c", [2, 16, 4, 128, 256]); st_n = di("st_n", [2, 64, 128]); st_m = di("st_m", [2, 16, 4])
    st_hl = di("st_hl", [2, 16, 1024]); st_cl = di("st_cl", [2, 48, 1024])
    st_hs = di("st_hs", [2, 16, 1024, 128]); st_cs = di("st_cs", [2, 48, 2048])
    w_in = di("w_in", [2, 1024, NIN]); w_br = di("w_br", [2, 3, 1024, 1024]); w_out = di("w_out", [2, 1024, 1024])
    wbd_d = di("wbd", [2, 2, 8, 128, 128])
    pp_d = di("pp", [2, 128, NPP]); pr_d = di("pr", [2, 128, NPR]); fg_d = di("fg", [128, 1024])
    cf_d = di("cf", [128, NCF]); cb_d = di("cb", [128, NCB], BF16)
    y_p = do("y_p", [2048, 1024]); y_s = do("y_s", [128, 1024])
    p_c = do("p_c", [2, 4, 128, 256]); p_n = do("p_n", [2, 4, 128]); p_m = do("p_m", [2, 4])
    p_hl = do("p_hl", [2, 1024]); p_cl = do("p_cl", [2, 3, 1024]); p_hs = do("p_hs", [2, 1024, 128]); p_cs = do("p_cs", [2, 3, 2048])
    s_c = do("s_c", [2, 16, 4, 128, 256]); s_n = do("s_n", [2, 64, 128]); s_m = do("s_m", [2, 16, 4])
    s_hl = do("s_hl", [2, 16, 1024]); s_cl = do("s_cl", [2, 16, 3, 1024]); s_hs = do("s_hs", [2, 16, 1024, 128]); s_cs = do("s_cs", [2, 16, 3, 2048])
    xscr = dint("xscr", [NTOK, 1024]); mscr = dint("mscr", [NTOK, 1024]); xnscr = dint("xnscr", [18, 128, 8, 128], BF16)

    with ExitStack() as st:
        k = K(nc, st)
        pe, act, dve, pool, sp = k.pe, k.act, k.dve, k.pool, k.sp

        def sbt(name, shape, dt=F32, sem=False, stack=st):
            t = stack.enter_context(nc.sbuf_tensor("sb_" + name, list(shape), dt))
            b = Buf(name, sem=(k.new_sem("d_" + name) if sem else None))
            return t, b

        slots = [sbt(f"slot{i}", [128, 8, 512], BF16, sem=True) for i in range(NSLOT)]
        XT = [sbt(f"xt{i}", [128, 1024], F32, sem=True) for i in range(4)]
        XN = [sbt(f"xn{i}", [128, 8, 128], BF16, sem=True) for i in range(3)]
        cf, cfB = sbt("cf", [128, NCF], F32, sem=True)
        cb, cbB = sbt("cb", [128, 128], BF16, sem=True)
        fg, fgB = sbt("fg", [128, 1024], F32, sem=True)
        ppt, ppB = sbt("ppt", [128, 2, NPP], F32, sem=True)
        prt, prB = sbt("prt", [128, 2, NPR], F32, sem=True)
        wif, wifB = sbt("wif", [128, 8, 8], BF16, sem=True)
        wdt, wdtB = sbt("wdt", [128, 8, 16], BF16, sem=True)
        ps = []
        for i in range(8):
            t = st.enter_context(nc.psum_tensor(f"ps{i}", [128, 512], F32))
            ps.append((t, t.bitcast(BF16), Buf(f"ps{i}", excl=True)))
        bank_free = {"F": [0, 1, 2], "B": [3, 4, 5, 6, 7]}

        def bk(pool="B"):
            if not bank_free[pool]:
                raise RuntimeError("out of PSUM banks in pool " + pool)
            return ps[bank_free[pool].pop(0)]

        def rel(*bs):
            for b in bs:
                i = [x[2] for x in ps].index(b[2])
                p = "F" if i < 3 else "B"
                assert i not in bank_free[p]
                bank_free[p].append(i)

        def run(gens):
            if cfg.get("seq"):
                for g in gens:
                    for _ in g:
                        pass
                return
            act_ = list(gens)
            while act_:
                for g in list(act_):
                    try:
                        next(g)
                    except StopIteration:
                        act_.remove(g)

        XS = [Buf(f"xscr{i}") for i in range(18)]
        MS = [Buf(f"mscr{i}") for i in range(18)]
        XNS = [Buf(f"xnscr{i}") for i in range(18)]

        def mm(out, lhsT, rhs, start, stop, R, W, inc):
            k.op(pe, lambda e: e.matmul(out, lhsT=lhsT, rhs=rhs, start=start, stop=stop), R, W, inc)

        def tr(out, in_, ident, R, W, inc):
            k.op(pe, lambda e: e.transpose(out=out, in_=in_, identity=ident), R, W, inc)

        def A(out, in_, func, R, W, scale=None, bias=None, accum=None):
            kw = {}
            if scale is not None:
                kw["scale"] = scale
            if bias is not None:
                kw["bias"] = bias
            if accum is not None:
                kw["accum_out"] = accum
            k.op(act, lambda e: e.activation(out=out, in_=in_, func=func, **kw), R, W)

        def tt(out, in0, in1, op, R, W, eng=None):
            k.op(eng or dve, lambda e: e.tensor_tensor(out=out, in0=in0, in1=in1, op=op), R, W)

        def ts(out, in0, s1, s2, op0, op1, R, W, eng=None):
            if op1 is None:
                k.op(eng or dve, lambda e: e.tensor_scalar(out=out, in0=in0, scalar1=s1, scalar2=None, op0=op0), R, W)
            else:
                k.op(eng or dve, lambda e: e.tensor_scalar(out=out, in0=in0, scalar1=s1, scalar2=s2, op0=op0, op1=op1), R, W)

        def stt(out, in0, scalar, in1, op0, op1, R, W):
            k.op(dve, lambda e: e.scalar_tensor_tensor(out=out, in0=in0, scalar=scalar, in1=in1, op0=op0, op1=op1), R, W)

        def cp(out, in_, R, W, eng=None):
            k.op(eng or dve, lambda e: e.tensor_copy(out=out, in_=in_), R, W)

        def rcp(out, in_, R, W):
            k.op(dve, lambda e: e.reciprocal(out=out, in_=in_), R, W)

        def mset(t_ap, val, W, eng=None):
            k.op(eng or dve, lambda e: e.memset(t_ap, val), (), W)

        def dma_in(dst_ap, src_ap, dstbuf, R=(), eng=None, **kw):
            k.dma(eng or sp, dst_ap, src_ap, R=R, W=[dstbuf], sem=dstbuf.sem, **kw)

        def dma_out(dst_ap, src_ap, srcbuf, W=(), final=False, eng=None, **kw):
            k.dma(eng or sp, dst_ap, src_ap, R=[srcbuf], W=W, sem=srcbuf.sem, is_out=final, **kw)

        identf = cf[:, C_IDF:C_IDF + 128]
        identb = cb[:, B_IDB:B_IDB + 128]

        def cfm(c0, L):
            return cf[:L, c0:c0 + L]

        slot_free = list(range(NSLOT))
        loaded = {}
        loaded_done = {}

        def wsrc(spec):
            kind = spec[0]
            if kind == "in":
                _, l, c0 = spec
                return dap(w_in, l * 1024 * NIN + c0, [[NIN, 128], [128 * NIN, 8], [1, 512]])
            if kind == "br":
                _, l, b, h = spec
                return dap(w_br, (l * 3 + b) * 1024 * 1024 + h * 512, [[1024, 128], [128 * 1024, 8], [1, 512]])
            _, l, h = spec
            return dap(w_out, l * 1024 * 1024 + h * 512, [[1024, 128], [128 * 1024, 8], [1, 512]])

        def phase_set(l, ph):
            if ph == "A":
                d = {"q": ("in", l, OQ), "k": ("in", l, OK_), "v0": ("in", l, OV), "v1": ("in", l, OV + 512),
                     "o0": ("in", l, OO), "o1": ("in", l, OO + 512), "z0": ("in", l, OZ), "z1": ("in", l, OZ + 512),
                     "g0": ("in", l, OG), "g1": ("in", l, OG + 512), "br0": ("br", l, 0, 0), "br1": ("br", l, 0, 1)}
            elif ph == "B":
                d = {"lx0": ("in", l, OLX), "lx1": ("in", l, OLX + 512), "lz0": ("in", l, OLZ), "lz1": ("in", l, OLZ + 512),
                     "g0": ("in", l, OG + 1024), "g1": ("in", l, OG + 1536), "br0": ("br", l, 1, 0), "br1": ("br", l, 1, 1)}
            else:
                d = {"sz0": ("in", l, OSZ), "sz1": ("in", l, OSZ + 512),
                     "xb0": ("in", l, OXBC), "xb1": ("in", l, OXBC + 512), "xb2": ("in", l, OXBC + 1024), "xb3": ("in", l, OXBC + 1536),
                     "g0": ("in", l, OG + 2048), "g1": ("in", l, OG + 2560), "br0": ("br", l, 2, 0), "br1": ("br", l, 2, 1),
                     "wo0": ("out", l, 0), "wo1": ("out", l, 1)}
            return d

        def load_weights(l, ph, only_free=True):
            key = (l, ph)
            d = phase_set(l, ph)
            have = loaded.setdefault(key, {})
            done = loaded_done.setdefault(key, set())
            for name, spec in d.items():
                if name in done:
                    continue
                if not slot_free:
                    return False
                si = slot_free.pop(0)
                t, b = slots[si]
                k.dma(pool, t[:], wsrc(spec), W=[b], sem=b.sem)
                have[name] = si
                done.add(name)
            return True

        def release_weights(l, ph, names=None):
            d = loaded[(l, ph)]
            for name in list(d.keys()):
                if names is None or name in names:
                    slot_free.append(d.pop(name))
            if names is None:
                loaded.pop((l, ph))

        def Wt(l, ph, name):
            si = loaded[(l, ph)][name]
            return slots[si]

        with nc.Block() as block:
            dma_in(cf[:], cf_d[:, :], cfB)
            dma_in(cb[:], cb_d[:, 0:128], cbB)
            dma_in(fg[:], fg_d[:, :], fgB)
            for l_ in range(2):
                dma_in(ppt[:, l_, :], pp_d[l_, :, :], ppB)
                dma_in(prt[:, l_, :], pr_d[l_, :, :], prB)
            phases = [(l, ph) for l in range(nlayers) for ph in "ABC" if cfg.get("ph" + ph, True)]
            if phases:
                load_weights(*phases[0])

            for l in range(nlayers):
                last_layer = (l == nlayers - 1)
                k.dma(pool, wif[:], dap(w_in, l * 1024 * NIN + OI, [[NIN, 128], [128 * NIN, 8], [1, 8]]), W=[wifB], sem=wifB.sem)
                k.dma(pool, wdt[:], dap(w_in, l * 1024 * NIN + ODT, [[NIN, 128], [128 * NIN, 8], [1, 16]]), W=[wdtB], sem=wdtB.sem)

                def ppc(c0, n=1):
                    return ppt[:, l, c0:c0 + n]

                def ppbc(c0, n, L):
                    return ap(ppt, 0, 128, l * NPP + c0, [[1, n], [0, L]])

                def prc(c0, n, L):
                    return prt[:L, l, c0:c0 + n]

                with ExitStack() as pst:
                    P = lambda n, s, dt=F32, sem=False: sbt(f"p0_{n}_{l}", s, dt, sem=sem, stack=pst)
                    xnb2 = [P(f"xnb{i_}", [128, 1024], BF16) for i_ in range(2)]
                    jk2 = [P(f"jk{i_}", [128, 1024], BF16) for i_ in range(2)]
                    sm2 = [P(f"sm{i_}", [128, 4], F32) for i_ in range(2)]

                    def p0_load(i):
                        t0, L = TILES[i]
                        xt, xb = XT[i % 4]
                        if l == 0:
                            if i == 0:
                                dma_in(xt[0:16, :], meta[:, :], xb)
                                dma_in(xt[16:128, :], xp[0:112, :], xb)
                            elif i < 16:
                                dma_in(xt[:, :], xp[128 * i - 16:128 * i + 112, :], xb)
                            elif i == 16:
                                dma_in(xt[0:16, :], xp[2032:2048, :], xb)
                            else:
                                dma_in(xt[:, :], xs[:, :], xb)
                        else:
                            dma_in(xt[:L, :], xscr[t0:t0 + L, :], xb, R=[XS[i]])

                    def p0(i):
                        t0, L = TILES[i]
                        xt, xb = XT[i % 4]
                        xnb, xnbB = xnb2[i % 2]; jk, jkB = jk2[i % 2]; sm, smB = sm2[i % 2]
                        if i + 2 < 18:
                            p0_load(i + 2)
                        if l == 0:
                            dma_out(xscr[t0:t0 + L, :], xt[:L, :], xb, W=[XS[i]])
                        A(jk[:L, :], xt[:L, :], AF.Square, [xb], [jkB, smB], accum=sm[:L, 0:1])
                        yield
                        A(sm[:L, 1:2], sm[:L, 0:1], AF.Ln, [smB, cfB], [smB], scale=1.0 / 1024, bias=cf[:L, C_EPS:C_EPS + 1])
                        A(sm[:L, 3:4], sm[:L, 1:2], AF.Exp, [smB], [smB], scale=-0.5)
                        A(xnb[:L, :], xt[:L, :], AF.Identity, [xb, smB], [xnbB], scale=sm[:L, 3:4])
                        yield
                        pt = bk("F" if i % 2 else "B")
                        for j in range(8):
                            tr(pt[1][:, j * 128:j * 128 + L], xnb[:L, j * 128:(j + 1) * 128], identb[:L, :L], [xnbB, cbB], [pt[2]], j == 7)
                        xn, xnB = XN[i % 3]
                        tt(xn[:, :, :L], ap(pt[1], 0, 128, 0, [[128, 8], [1, L]]), ppbc(P_NG, 8, L), ALU.mult, [pt[2], ppB], [xnB])
                        rel(pt)
                        dma_out(xnscr[i, :, :, :L], xn[:, :, :L], xnB, W=[XNS[i]])
                        yield

                    p0_load(0)
                    p0_load(1)
                    for i in range(0, 18, 2):
                        run([p0(i), p0(i + 1)])
                    k.barrier([pe, act, dve])

                def tail_prefetch(i, mode):
                    t0, L = TILES[i]
                    if mode != "first":
                        mt_, mb_ = XT[i % 2]
                        dma_in(mt_[:L, :], mscr[t0:t0 + L, :], mb_, R=[MS[i]])
                    if mode == "last":
                        xt_, xb_ = XT[2 + i % 2]
                        dma_in(xt_[:L, :], xscr[t0:t0 + L, :], xb_, R=[XS[i]])

                def xn_load(i):
                    xn, xnB = XN[i % 3]
                    L = TILES[i][1]
                    dma_in(xn[:, :, :L], xnscr[i, :, :, :L], xnB, R=[XNS[i]])

                def tail(i, ph, mode, yT, yTB, xn, xnB, T):
                    t0, L = TILES[i]
                    sg, sgB = T["sg"]
                    mt_, mb_ = XT[i % 2]
                    zb = [bk(), bk()]
                    for n in range(2):
                        wt_, wB = Wt(l, ph, f"br{n}")
                        for kk in range(8):
                            mm(zb[n][0][:L, :], yT[:, kk, :L], wt_[:, kk, :], kk == 0, kk == 7, [yTB, wB], [zb[n][2]], kk == 7)
                        yield
                    gb = [bk(), bk()]
                    for n in range(2):
                        wt_, wB = Wt(l, ph, f"g{n}")
                        for kk in range(8):
                            mm(gb[n][0][:L, :], xn[:, kk, :L], wt_[:, kk, :], kk == 0, kk == 7, [xnB, wB], [gb[n][2]], kk == 7)
                        yield
                    for n in range(2):
                        A(sg[:L, n * 512:(n + 1) * 512], gb[n][0][:L, :], AF.Tanh, [gb[n][2]], [sgB], scale=0.5)
                    rel(*gb)
                    if mode == "first":
                        for n in range(2):
                            stt(mt_[:L, n * 512:(n + 1) * 512], sg[:L, n * 512:(n + 1) * 512], 1.0, zb[n][0][:L, :], ALU.add, ALU.mult, [zb[n][2], sgB], [mb_])
                        rel(*zb)
                        dma_out(mscr[t0:t0 + L, :], mt_[:L, :], mb_, W=[MS[i]])
                        return
                    for n in range(2):
                        stt(sg[:L, n * 512:(n + 1) * 512], sg[:L, n * 512:(n + 1) * 512], 1.0, zb[n][0][:L, :], ALU.add, ALU.mult, [zb[n][2], sgB], [sgB])
                    rel(*zb)
                    if mode == "mid":
                        tt(mt_[:L, :], mt_[:L, :], sg[:L, :], ALU.add, [mb_, sgB], [mb_])
                        dma_out(mscr[t0:t0 + L, :], mt_[:L, :], mb_, W=[MS[i]])
                        return
                    mrg, mrgB = T["mrg"]
                    mT, mTB = T["mT"]
                    tt(mrg[:L, :], mt_[:L, :], sg[:L, :], ALU.add, [mb_, sgB], [mrgB])
                    pt = bk()
                    for j in range(8):
                        tr(pt[1][:, j * 128:j * 128 + L], mrg[:L, j * 128:(j + 1) * 128], identb[:L, :L], [mrgB, cbB], [pt[2]], j == 7)
                    cp(mT[:, :, :L], ap(pt[1], 0, 128, 0, [[128, 8], [1, L]]), [pt[2]], [mTB])
                    rel(pt)
                    yield
                    ob = [bk(), bk()]
                    for n in range(2):
                        wt_, wB = Wt(l, ph, f"wo{n}")
                        for kk in range(8):
                            mm(ob[n][0][:L, :], mT[:, kk, :L], wt_[:, kk, :], kk == 0, kk == 7, [mTB, wB], [ob[n][2]], kk == 7)
                        yield
                    xt_, xb_ = XT[2 + i % 2]
                    for n in range(2):
                        stt(xt_[:L, n * 512:(n + 1) * 512], ob[n][0][:L, :], 0.5, xt_[:L, n * 512:(n + 1) * 512], ALU.mult, ALU.add, [xb_, ob[n][2]], [xb_])
                    rel(*ob)
                    if not last_layer:
                        dma_out(xscr[t0:t0 + L, :], xt_[:L, :], xb_, W=[XS[i]])
                        return
                    sm, smB = T["fsm"]
                    A(mrg[:L, :], xt_[:L, :], AF.Square, [xb_], [mrgB, smB], accum=sm[:L, 0:1])
                    A(sm[:L, 1:2], sm[:L, 0:1], AF.Ln, [smB, cfB], [smB], scale=1.0 / 1024, bias=cf[:L, C_EPS:C_EPS + 1])
                    A(sm[:L, 3:4], sm[:L, 1:2], AF.Exp, [smB], [smB], scale=-0.5)
                    stt(mt_[:L, :], xt_[:L, :], sm[:L, 3:4], fg[:L, :], ALU.mult, ALU.mult, [xb_, smB, fgB], [mb_])
                    if i == 0:
                        dma_out(y_p[0:112, :], mt_[16:128, :], mb_, final=True)
                    elif i < 16:
                        dma_out(y_p[128 * i - 16:128 * i + 112, :], mt_[:, :], mb_, final=True)
                    elif i == 16:
                        dma_out(y_p[2032:2048, :], mt_[0:16, :], mb_, final=True)
                    else:
                        dma_out(y_s[:, :], mt_[:, :], mb_, final=True)

                def repl4(val_ap, ncol, dstR, dstRB, out_t, out_B):
                    tt(ap(dstR, 0, 4, 0, [[4, ncol], [1, 4]]), ap(val_ap[0], 0, 4, val_ap[1], [[1, ncol], [0, 4]]),
                       ap(cf, 0, 4, C_IDF, [[0, ncol], [1, 4]]), ALU.mult, [val_ap[2], cfB], [dstRB])
                    b = bk()
                    mm(b[0][:, 0:ncol * 4], cf[0:4, C_ONES:C_ONES + 128], dstR[0:4, 0:ncol * 4], True, True, [cfB, dstRB], [b[2]], True)
                    cp(out_t[:, 0:ncol * 4], b[0][:, 0:ncol * 4], [b[2]], [out_B])
                    rel(b)

                def drive(front, back, ph, front_only, nxt_phase):
                    xn_load(0)
                    run([front(0)])
                    for i in range(18):
                        gs = [back(i)]
                        if i + 1 < 18:
                            gs.append(front(i + 1))
                        run(gs)
                        if i + 1 == 17:
                            release_weights(l, ph, front_only)
                            if nxt_phase is not None:
                                load_weights(*nxt_phase)
                    k.barrier([pe, act, dve])

                def phaseA(nxt_phase):
                    with ExitStack() as pst:
                        P = lambda n, s, dt=F32, sem=False: sbt(f"a_{n}_{l}", s, dt, sem=sem, stack=pst)
                        P2 = lambda n, s, dt=F32: [P(f"{n}{i_}", s, dt) for i_ in range(2)]
                        qT2 = P2("qT", [128, 4, 128], BF16); kT, kTB = P("kT", [128, 4, 128], BF16)
                        ktok2 = P2("ktok", [128, 512], BF16)
                        ve2 = P2("ve", [128, 4, 260], BF16); pmT2 = P2("pmT", [128, 4, 128], BF16)
                        og2 = P2("og", [128, 1024]); sgo, sgoB = P("sgo", [128, 1024]); sg, sgB = P("sg", [128, 1024])
                        yml, ymlB = P("yml", [128, 1024], BF16); yT, yTB = P("yT", [128, 8, 128], BF16)
                        CN, CNB = P("CN", [128, 4, 260]); CNb, CNbB = P("CNb", [128, 4, 260], BF16)
                        jk, jkB = P("jk", [128, 256], BF16)
                        gif, gifB = P("gif", [128, 8]); gx, gxB = P("gx", [128, 8]); gnlf, gnlfB = P("gnlf", [128, 4])
                        ga2 = P2("ga", [128, 4]); ge, geB = P("ge", [128, 4]); gfl2 = P2("gfl", [128, 4])
                        gebl2 = P2("gebl", [128, 4]); gnbl2 = P2("gnbl", [128, 4])
                        gden, gdenB = P("gden", [128, 8]); gss, gssB = P("gss", [128, 4]); gt, gtB = P("gt", [128, 12]); gsc, gscB = P("gsc", [128, 4])
                        mst, mstB = P("mst", [128, 96], F32, sem=True); rr, rrB = P("rr", [128, 64]); emr, emrB = P("emr", [128, 64])
                        co = [P(f"co{i_}", [128, 4, 260], F32, sem=True) for i_ in range(2)]
                        T = {"sg": (sg, sgB)}
                        DQS = float(128 ** -0.5)
                        csel, cselB = P("csel", [128, 2048], BF16, sem=True)
                        dma_in(csel[:], cb_d[:, B_COLSEL:B_COLSEL + 2048], cselB)
                        mset(CN[:], 0.0, [CNB]); mset(CNb[:], 0.0, [CNbB]); mset(mst[:], 0.0, [mstB])

                        def front(i):
                            t0, L = TILES[i]
                            smp = (i == SAMPLE)
                            par = i % 2
                            xn, xnB = XN[i % 3]
                            if i + 1 < 18:
                                xn_load(i + 1)
                            qT, qTB = qT2[par]; ktok, ktokB = ktok2[par]; ve, veB = ve2[par]; pmT, pmTB = pmT2[par]; og, ogB = og2[par]
                            ga, gaB = ga2[par]; gfl, gflB = gfl2[par]; gebl, geblB = gebl2[par]; gnbl, gnblB = gnbl2[par]
                            C_TRI = C_TRIS if smp else C_TRIP
                            C_ONE = C_SAMES if smp else C_ONES
                            for (wn, dst, dstB, scl) in (("q", qT, qTB, 1.0), ("k", kT, kTB, DQS)):
                                wt_, wB = Wt(l, "A", wn)
                                b = bk("F")
                                for h in range(4):
                                    for kk in range(8):
                                        mm(b[0][:, h * 128:h * 128 + L], wt_[:, kk, h * 128:(h + 1) * 128], xn[:, kk, :L], kk == 0, kk == 7,
                                           [wB, xnB], [b[2]], kk == 7)
                                    if h % 2 == 1 and h < 3:
                                        yield
                                A(dst[:, :, :L], ap(b[0], 0, 128, 0, [[128, 4], [1, L]]), AF.Identity, [b[2]], [dstB], scale=scl)
                                rel(b)
                                yield
                            wt_, wB = Wt(l, "A", "k")
                            b = bk("F")
                            for kk in range(8):
                                mm(b[0][:L, :], xn[:, kk, :L], wt_[:, kk, :], kk == 0, kk == 7, [xnB, wB], [b[2]], kk == 7)
                            A(ktok[:L, :], b[0][:L, :], AF.Identity, [b[2]], [ktokB], scale=DQS)
                            rel(b)
                            bg = bk("F")
                            for kk in range(8):
                                mm(bg[0][:L, 0:8], xn[:, kk, :L], wif[:, kk, :], kk == 0, kk == 7, [xnB, wifB], [bg[2]], kk == 7)
                            cp(gif[:L, :], bg[0][:L, 0:8], [bg[2]], [gifB])
                            tt(gx[:L, 0:4], gif[:L, 4:8], prc(R_FB, 4, L), ALU.add, [gifB, prB], [gxB])
                            A(gx[:L, 4:8], gx[:L, 0:4], AF.Exp, [gxB], [gxB], scale=-1.0)
                            A(gnlf[:L, :], gx[:L, 4:8], AF.Ln, [gxB, cfB], [gnlfB], bias=cf[:L, C_ONES:C_ONES + 1])
                            yield
                            bv = [bk("F"), bk("F")]
                            for n in range(2):
                                wt_, wB = Wt(l, "A", f"v{n}")
                                for kk in range(8):
                                    mm(bv[n][0][:L, :], xn[:, kk, :L], wt_[:, kk, :], kk == 0, kk == 7, [xnB, wB], [bv[n][2]], kk == 7)
                                yield
                            mm(bg[0][:L, 8:12], cfm(C_TRI, L), gnlf[:L, :], True, True, [cfB, gnlfB], [bg[2]], False)
                            mm(bg[0][:, 12:16], cf[:L, C_ONE:C_ONE + 128], gnlf[:L, :], True, True, [cfB, gnlfB], [bg[2]], True)
                            tt(ga[:L, :], gif[:L, 0:4], bg[0][:L, 8:12], ALU.add, [gifB, bg[2]], [gaB])
                            A(ge[:L, :], ga[:L, :], AF.Exp, [gaB], [geB])
                            A(gfl[:L, :], bg[0][:L, 8:12], AF.Exp, [bg[2]], [gflB])
                            A(gebl[:, :], bg[0][:, 12:16], AF.Exp, [bg[2]], [geblB], scale=-1.0)
                            cp(gnbl[:, :], bg[0][:, 12:16], [bg[2]], [gnblB])
                            rel(bg)
                            yield
                            for n in range(2):
                                tt(ve[:L, 2 * n:2 * n + 2, 0:256], ap(bv[n][0], 0, L, 0, [[256, 2], [1, 256]]),
                                   ap(ge, 0, L, 2 * n, [[1, 2], [0, 256]]), ALU.mult, [bv[n][2], geB], [veB])
                            rel(*bv)
                            cp(ap(ve, 0, L, 256, [[260, 4], [1, 1]]), ap(ge, 0, L, 0, [[1, 4], [1, 1]]), [geB], [veB])
                            bs = bk("F")
                            for h in range(4):
                                mm(bs[0][:L, h * 128:h * 128 + L], kT[:, h, :L], qT[:, h, :L], True, True, [kTB, qTB], [bs[2]], h == 3)
                            tt(pmT[:L, :, :L], ap(bs[0], 0, L, 0, [[128, 4], [1, L]]), ap(cf, 0, L, C_TRI, [[0, 4], [1, L]]), ALU.mult, [bs[2], cfB], [pmTB])
                            rel(bs)
                            yield
                            for (wn, dst, dstB, fn, fsc) in (("o", sgo, sgoB, AF.Tanh, 0.5), ("z", og, ogB, AF.Silu, 1.0)):
                                bb = [bk("F"), bk("F")]
                                for n in range(2):
                                    wt_, wB = Wt(l, "A", f"{wn}{n}")
                                    for kk in range(8):
                                        mm(bb[n][0][:L, :], xn[:, kk, :L], wt_[:, kk, :], kk == 0, kk == 7, [xnB, wB], [bb[n][2]], kk == 7)
                                    yield
                                for n in range(2):
                                    A(dst[:L, n * 512:(n + 1) * 512], bb[n][0][:L, :], fn, [bb[n][2]], [dstB], scale=fsc)
                                rel(*bb)
                            stt(og[:L, :], sgo[:L, :], 1.0, og[:L, :], ALU.add, ALU.mult, [ogB, sgoB], [ogB])
                            yield

                        def back(i):
                            t0, L = TILES[i]
                            smp = (i == SAMPLE)
                            par = i % 2
                            xn, xnB = XN[i % 3]
                            qT, qTB = qT2[par]; ktok, ktokB = ktok2[par]; ve, veB = ve2[par]; pmT, pmTB = pmT2[par]; og, ogB = og2[par]
                            ga, gaB = ga2[par]; gfl, gflB = gfl2[par]; gebl, geblB = gebl2[par]; gnbl, gnblB = gnbl2[par]
                            if not smp:
                                bn = [bk() for _ in range(4)]
                                for h in range(4):
                                    mm(bn[h][0][:L, 0:257], pmT[:L, h, :L], ve[:L, h, 0:257], True, False, [pmTB, veB], [bn[h][2]], False)
                                    mm(bn[h][0][:L, 0:257], qT[:, h, :L], CNb[:, h, 0:257], False, True, [qTB, CNbB], [bn[h][2]], True)
                                yield
                                for h in range(4):
                                    b = bk()
                                    mm(b[0][:, 0:257], ktok[:L, h * 128:(h + 1) * 128], ve[:L, h, 0:257], True, True, [ktokB, veB], [b[2]], True)
                                    tt(CN[:, h, 0:257], b[0][:, 0:257], CN[:, h, 0:257], ALU.add, [b[2], CNB], [CNB])
                                    ts(CN[:, h, 0:257], CN[:, h, 0:257], gebl[:, h:h + 1], None, ALU.mult, None, [CNB, geblB], [CNB])
                                    rel(b)
                                    if h % 2 == 1:
                                        yield
                                A(CNb[:, :, 0:257], CN[:, :, 0:257], AF.Copy, [CNB], [CNbB])
                                bm = bk()
                                tr(bm[0][0:4, 0:L], ga[:L, 0:4], identf[:L, :L], [gaB, cfB], [bm[2]], False)
                                tr(bm[0][0:4, 128:256], gnbl[:, 0:4], identf[:, :], [gnblB, cfB], [bm[2]], True)
                                k.op(dve, lambda e: e.reduce_max(out=mst[0:4, 1:2], in_=bm[0][0:4, 0:L], axis=AX.X), [bm[2]], [mstB])
                                tt(mst[0:4, 2:3], mst[0:4, 0:1], mst[0:4, 1:2], ALU.max, [mstB], [mstB])
                                tt(mst[0:4, 0:1], mst[0:4, 2:3], bm[0][0:4, 128:129], ALU.subtract, [mstB, bm[2]], [mstB])
                                rel(bm)
                                yield
                            else:
                                n0t, n0tB = XT[3]
                                dma_in(n0t[0:64, 0:128], st_n[l, :, :], n0tB)
                                dma_in(ap(mst, 0, 4, 16, [[1, 16]]), dap(st_m, l * 64, [[1, 4], [4, 16]]), mstB, allow_slow_non_contiguous=True)
                                b = bk()
                                tr(b[0][:, 0:64], n0t[0:64, 0:128], identf[0:64, 0:64], [n0tB, cfB], [b[2]], True)
                                n0T, n0TB = P("n0T", [128, 64])
                                cp(n0T[:, :], b[0][:, 0:64], [b[2]], [n0TB])
                                rel(b)
                                nout, noutB = P("nout", [128, 64])
                                A(mst[0:4, 32:48], mst[0:4, 16:32], AF.Exp, [mstB], [mstB])
                                rr5, rr5B = P("rr5", [128, 512])
                                tt(ap(rr5, 0, 4, 0, [[128, 4], [8, 16], [1, 8]]), ap(cf, 0, 4, C_IDF, [[1, 4], [0, 16], [0, 8]]), ap(mst, 0, 4, 32, [[0, 4], [1, 16], [0, 8]]),
                                   ALU.mult, [cfB, mstB], [rr5B])
                                b = bk()
                                mm(b[0][:, 0:512], cf[0:4, C_ONES:C_ONES + 128], rr5[0:4, 0:512], True, True, [cfB, rr5B], [b[2]], True)
                                qs, qsB = P("qs", [128, 4, 128], BF16)
                                tt(qs[:].rearrange("p a b -> p (a b)"), qT[:].rearrange("p a b -> p (a b)"), b[0][:, 0:512], ALU.mult, [qTB, b[2]], [qsB])
                                rel(b)
                                yield
                                bm = bk()
                                tr(bm[0][0:4, 0:128], ga[:, 0:4], identf[:, :], [gaB, cfB], [bm[2]], False)
                                tr(bm[0][0:4, 128:256], gnbl[:, 0:4], identf[:, :], [gnblB, cfB], [bm[2]], True)
                                k.op(dve, lambda e: e.tensor_reduce(out=mst[0:4, 64:80], in_=ap(bm[0], 0, 4, 0, [[8, 16], [1, 8]]), axis=AX.X, op=ALU.max), [bm[2]], [mstB])
                                tt(mst[0:4, 64:80], mst[0:4, 64:80], mst[0:4, 16:32], ALU.max, [mstB], [mstB])
                                nblv = ap(bm[0], 0, 4, 128, [[8, 16]])
                                tt(mst[0:4, 48:64], mst[0:4, 64:80], nblv, ALU.subtract, [mstB, bm[2]], [mstB])
                                tt(mst[0:4, 64:80], mst[0:4, 16:32], mst[0:4, 48:64], ALU.subtract, [mstB], [mstB])
                                tt(mst[0:4, 64:80], mst[0:4, 64:80], nblv, ALU.subtract, [mstB, bm[2]], [mstB])
                                A(mst[0:4, 64:80], mst[0:4, 64:80], AF.Exp, [mstB], [mstB])
                                w1r, w1rB = P("w1r", [128, 64]); w2r, w2rB = P("w2r", [128, 64])
                                repl4((mst, 64, mstB), 16, rr, rrB, w1r, w1rB)
                                ts(mst[0:4, 80:96], mst[0:4, 48:64], -1.0, None, ALU.mult, None, [mstB], [mstB])
                                tt(mst[0:4, 80:96], mst[0:4, 80:96], nblv, ALU.subtract, [mstB, bm[2]], [mstB])
                                A(mst[0:4, 80:96], mst[0:4, 80:96], AF.Exp, [mstB], [mstB])
                                repl4((mst, 80, mstB), 16, rr, rrB, w2r, w2rB)
                                rel(bm)
                                dma_out(dap(s_m, l * 64, [[1, 4], [4, 16]]), ap(mst, 0, 4, 48, [[1, 16]]), mstB, final=True, allow_slow_non_contiguous=True)
                                yield
                                bn = [bk() for _ in range(4)]
                                for h in range(4):
                                    mm(bn[h][0][:, 0:257], pmT[:, h, :], ve[:, h, 0:257], True, False, [pmTB, veB], [bn[h][2]], False)
                                qz = [P(f"qz{i_}", [128, 4, 128], BF16) for i_ in range(2)]
                                vej = [P(f"vej{i_}", [128, 4, 260], BF16) for i_ in range(2)]
                                CNb2 = [(CNb, CNbB), P("CNb2", [128, 4, 260], BF16)]
                                tmpc, tmpcB = P("tmpc", [128, 4, 256])
                                ndl, ndlB = P("ndl", [128, 64])
                                stg = [XT[0], XT[2], XT[3]]

                                def ld(j):
                                    cs_, csB = stg[j % 3]
                                    dma_in(ap(cs_, 0, 128, 0, [[256, 4], [1, 256]]), dap(st_c, ((l * 16 + j) * 4) * 128 * 256, [[256, 128], [128 * 256, 4], [1, 256]]), csB)

                                ld(0)
                                ld(1)
                                for j in range(16):
                                    if j + 2 < 16:
                                        ld(j + 2)
                                    cs_, csB = stg[j % 3]
                                    cn, cnB = CNb2[j % 2]
                                    A(cn[:, :, 0:256], ap(cs_, 0, 128, 0, [[256, 4], [1, 256]]), AF.Copy, [csB], [cnB])
                                    A(ap(cn, 0, 128, 256, [[260, 4], [1, 1]]), ap(n0T, 0, 128, j * 4, [[1, 4], [1, 1]]), AF.Copy, [n0TB], [cnB])
                                    qz_, qzB = qz[j % 2]
                                    tt(qz_[:, :, :], qs[:, :, :], ap(csel, 0, 128, j * 128, [[0, 4], [1, 128]]), ALU.mult, [qsB, cselB], [qzB])
                                    for h in range(4):
                                        mm(bn[h][0][:, 0:257], qz_[:, h, :], cn[:, h, 0:257], False, j == 15, [qzB, cnB], [bn[h][2]], True)
                                    vj, vjB = vej[j % 2]
                                    A(vj[:].rearrange("p a b -> p (a b)"), ve[:].rearrange("p a b -> p (a b)"), AF.Identity, [veB, cfB], [vjB], scale=cf[:, C_ROWSEL + j:C_ROWSEL + j + 1])
                                    co_, coB = co[j % 2]
                                    for h in range(4):
                                        b = bk("F")
                                        col = j * 4 + h
                                        mm(b[0][:, 0:257], ktok[:, h * 128:(h + 1) * 128], vj[:, h, 0:257], True, True, [ktokB, vjB], [b[2]], True)
                                        A(tmpc[:, h, :], b[0][:, 0:256], AF.Identity, [b[2], w2rB], [tmpcB], scale=w2r[:, col:col + 1])
                                        A(ndl[:, col:col + 1], b[0][:, 256:257], AF.Copy, [b[2]], [ndlB])
                                        rel(b)
                                        stt(co_[:, h, 0:256], ap(cs_, 0, 128, h * 256, [[1, 256]]), w1r[:, col:col + 1], tmpc[:, h, :], ALU.mult, ALU.add, [csB, w1rB, tmpcB], [coB])
                                    dma_out(dap(s_c, ((l * 16 + j) * 4) * 128 * 256, [[256, 128], [128 * 256, 4], [1, 256]]), co_[:, :, 0:256], coB, final=True)
                                    yield
                                tt(nout[:, :], ndl[:, :], w2r[:, :], ALU.mult, [ndlB, w2rB], [noutB])
                                tt(ndl[:, :], n0T[:, :], w1r[:, :], ALU.mult, [n0TB, w1rB], [ndlB])
                                tt(nout[:, :], nout[:, :], ndl[:, :], ALU.add, [noutB, ndlB], [noutB])
                                b = bk()
                                tr(b[0][0:64, 0:128], nout[:, 0:64], identf[:, :], [noutB, cfB], [b[2]], True)
                                no2, no2B = P("no2", [64, 128], F32, sem=True)
                                cp(no2[:, :], b[0][0:64, 0:128], [b[2]], [no2B])
                                rel(b)
                                dma_out(s_n[l, :, :], no2[:, :], no2B, final=True)
                            for h in range(4):
                                cp(gden[:L, h:h + 1], bn[h][0][:L, 256:257], [bn[h][2]], [gdenB])
                            A(gden[:L, 4:8], gden[:L, 0:4], AF.Abs, [gdenB], [gdenB])
                            tt(gden[:L, 4:8], gden[:L, 4:8], gfl[:L, :], ALU.max, [gdenB, gflB], [gdenB])
                            rcp(gt[:L, 0:4], gden[:L, 4:8], [gdenB], [gtB])
                            for h in range(4):
                                A(jk[:L, :], bn[h][0][:L, 0:256], AF.Square, [bn[h][2]], [jkB, gssB], accum=gss[:L, h:h + 1])
                            yield
                            tt(gt[:L, 4:8], gt[:L, 0:4], gt[:L, 0:4], ALU.mult, [gtB], [gtB])
                            tt(gt[:L, 4:8], gt[:L, 4:8], gss[:L, :], ALU.mult, [gtB, gssB], [gtB])
                            A(gt[:L, 4:8], gt[:L, 4:8], AF.Ln, [gtB, cfB], [gtB], scale=1.0 / 256, bias=cf[:L, C_EPS:C_EPS + 1])
                            A(gt[:L, 8:12], gt[:L, 4:8], AF.Exp, [gtB], [gtB], scale=-0.5)
                            stt(gsc[:L, :], gt[:L, 0:4], 0.5, gt[:L, 8:12], ALU.mult, ALU.mult, [gtB], [gscB])
                            for h in range(4):
                                stt(yml[:L, h * 256:(h + 1) * 256], bn[h][0][:L, 0:256], gsc[:L, h:h + 1], og[:L, h * 256:(h + 1) * 256], ALU.mult, ALU.mult,
                                    [bn[h][2], gscB, ogB], [ymlB])
                            rel(*bn)
                            yield
                            pt = bk()
                            for j in range(8):
                                tr(pt[1][:, j * 128:j * 128 + L], yml[:L, j * 128:(j + 1) * 128], identb[:L, :L], [ymlB, cbB], [pt[2]], j == 7)
                            tt(yT[:, :, :L], ap(pt[1], 0, 128, 0, [[128, 8], [1, L]]), ppbc(P_MG, 8, L), ALU.mult, [pt[2], ppB], [yTB])
                            rel(pt)
                            yield
                            yield from tail(i, "A", "first", yT, yTB, xn, xnB, T)
                            if i == 16:
                                A(mst[0:4, 3:4], mst[0:4, 0:1], AF.Exp, [mstB], [mstB], scale=-1.0)
                                repl4((mst, 3, mstB), 1, rr, rrB, emr, emrB)
                                co_, coB = co[0]
                                for h in range(4):
                                    ts(co_[:, h, 0:257], CN[:, h, 0:257], emr[:, h:h + 1], None, ALU.mult, None, [CNB, emrB], [coB])
                                dma_out(dap(p_c, l * 4 * 128 * 256, [[256, 128], [128 * 256, 4], [1, 256]]), co_[:, :, 0:256], coB, final=True)
                                dma_out(dap(p_n, l * 512, [[1, 128], [128, 4], [1, 1]]), ap(co_, 0, 128, 256, [[260, 4], [1, 1]]), coB, final=True, allow_slow_non_contiguous=True)
                                dma_out(dap(p_m, l * 4, [[1, 4], [1, 1]]), mst[0:4, 0:1], mstB, final=True)

                        drive(front, back, "A", ["q", "k", "v0", "v1", "o0", "o1", "z0", "z1"], nxt_phase)

                def phaseB(nxt_phase):
                    with ExitStack() as pst:
                        P = lambda n, s, dt=F32, sem=False: sbt(f"b_{n}_{l}", s, dt, sem=sem, stack=pst)
                        xf, xfB = P("xf", [128, 1408], BF16)
                        wbd, wbdB = P("wbd", [128, 2, 8, 128], BF16, sem=True)
                        for a_ in range(2):
                            k.dma(pool, wbd[:, a_, :, :], dap(wbd_d, (l * 2 + a_) * 8 * 128 * 128, [[128, 128], [128 * 128, 8], [1, 128]]), W=[wbdB], sem=wbdB.sem)
                        dgl, dglB = P("dgl", [128, 32, 128], BF16)
                        tt(dgl[:, :, :], ap(cb, 0, 128, 0, [[0, 32], [1, 128]]), ap(ppt, 0, 128, l * NPP + P_LCW, [[1, 32], [0, 128]]), ALU.mult, [cbB, ppB], [dglB])
                        xc, xcB = P("xc", [128, 8, 128]); xcb, xcbB = P("xcb", [128, 8, 128], BF16)
                        Rr, RrB = P("R", [128, 8, 128]); Ii, IiB = P("I", [128, 8, 128]); Tt, TtB = P("T", [128, 8, 128]); Hh, HhB = P("H", [128, 8, 128])
                        yT2 = [P(f"yT{i_}", [128, 8, 128], BF16) for i_ in range(2)]
                        sg, sgB = P("sg", [128, 1024])
                        hc, hcB = P("hc", [128, 8]); cA, cAB = P("cA", [128, 8]); tq, tqB = P("tq", [128, 8, 16])
                        xcBk = [Buf(f"xc{kk}_{l}") for kk in range(8)]
                        ctmp = [P(f"ctmp{i_}", [128, 128]) for i_ in range(2)]
                        ho, hoB = P("ho", [128, 1024], F32, sem=True)
                        T = {"sg": (sg, sgB)}
                        A(cA[:, :], ppc(P_LAM, 8), AF.Exp, [ppB], [cAB], scale=-1.0)
                        A(cA[:, :], cA[:, :], AF.Ln, [cAB, cfB], [cAB], bias=cf[:, C_ONES:C_ONES + 1])
                        ts(cA[:, :], cA[:, :], -4.0, None, ALU.mult, None, [cAB], [cAB])
                        hb, hbB = P("hb", [128, 16])
                        ts(hb[:, :], ppc(P_LBA, 16), 0.5, None, ALU.mult, None, [ppB], [hbB])
                        mset(xf[:], 0.0, [xfB]); mset(hc[:], 0.0, [hcB])

                        def front(i):
                            t0, L = TILES[i]
                            smp = (i == SAMPLE)
                            xn, xnB = XN[i % 3]
                            yT, yTB = yT2[i % 2]
                            if i + 1 < 18:
                                xn_load(i + 1)
                            if smp:
                                xwin = lambda kk, j: ap(xf, 0, 128, kk * 176 + j, [[11, 16], [1, 8]])
                                xcv = lambda kk: ap(xc, 0, 128, kk * 128, [[8, 16], [1, 8]])
                                s48, s48B = XT[3]
                                dma_in(s48[0:48, :], st_cl[l, :, :], s48B)
                                b = bk("F")
                                for kk in range(8):
                                    tr(b[0][:, kk * 48:(kk + 1) * 48], s48[0:48, kk * 128:(kk + 1) * 128], identf[0:48, 0:48], [s48B, cfB], [b[2]], kk == 7)
                                cp(ap(xf, 0, 128, 0, [[176, 8], [11, 16], [1, 3]]), ap(b[0], 0, 128, 0, [[48, 8], [3, 16], [1, 3]]), [b[2]], [xfB])
                                rel(b)
                            else:
                                xwin = lambda kk, j: ap(xf, 0, 128, kk * 131 + j, [[1, L]])
                                xcv = lambda kk: xc[:, kk, :L]
                            for n in range(2):
                                wt_, wB = Wt(l, "B", f"lx{n}")
                                b = bk("F")
                                for m in range(4):
                                    for kk in range(8):
                                        mm(b[0][:, m * 128:m * 128 + L], wt_[:, kk, m * 128:(m + 1) * 128], xn[:, kk, :L], kk == 0, kk == 7, [wB, xnB], [b[2]], kk == 7)
                                    if m == 1:
                                        yield
                                if smp:
                                    A(ap(xf, 0, 128, n * 4 * 176 + 3, [[176, 4], [11, 16], [1, 8]]), ap(b[0], 0, 128, 0, [[128, 4], [8, 16], [1, 8]]), AF.Copy, [b[2]], [xfB])
                                else:
                                    A(ap(xf, 0, 128, n * 4 * 131 + 3, [[131, 4], [1, L]]), ap(b[0], 0, 128, 0, [[128, 4], [1, L]]), AF.Copy, [b[2]], [xfB])
                                rel(b)
                                yield
                            if smp or i == 16:
                                lt, ltB = XT[3]
                                bt_ = [bk("F"), bk("F")]
                                for n in range(2):
                                    wt_, wB = Wt(l, "B", f"lx{n}")
                                    for kk in range(8):
                                        mm(bt_[n][0][:L, :], xn[:, kk, :L], wt_[:, kk, :], kk == 0, kk == 7, [xnB, wB], [bt_[n][2]], kk == 7)
                                for n in range(2):
                                    A(lt[:L, n * 512:(n + 1) * 512], bt_[n][0][:L, :], AF.Copy, [bt_[n][2]], [ltB])
                                rel(*bt_)
                                if smp:
                                    for r in range(3):
                                        dma_out(dap(s_cl, l * 16 * 3 * 1024 + r * 1024, [[3 * 1024, 16], [1, 1024]]), bass.AP(lt, (5 + r) * 1024, [[8 * 1024, 16], [1, 1024]]), ltB, final=True)
                                else:
                                    dma_out(p_cl[l, :, :], lt[13:16, :], ltB, final=True)
                                yield
                            for g4 in range(2):
                                b = bk("F")
                                ov = (lambda q: ap(b[0], 0, 128, q * 128, [[8, 16], [1, 8]])) if smp else (lambda q: b[0][:, q * 128:q * 128 + L])
                                for q in range(4):
                                    kk = 4 * g4 + q
                                    for j in range(4):
                                        mm(ov(q), dgl[:, kk * 4 + j, :], xwin(kk, j), j == 0, j == 3, [dglB, xfB], [b[2]], j == 3 and q == 3)
                                for q in range(4):
                                    kk = 4 * g4 + q
                                    A(xcv(kk), ov(q), AF.Identity, [b[2], ppB], [xcBk[kk]], bias=ppc(P_LCB + kk))
                                rel(b)
                                yield
                            if not smp:
                                cp(ap(xf, 0, 128, 0, [[131, 8], [1, 3]]), ap(xf, 0, 128, L, [[131, 8], [1, 3]]), [xfB], [xfB])
                            A(xcb[:, :, :L], xc[:, :, :L], AF.Copy, xcBk, [xcbB])
                            for (a_, dst, dstB, bcol) in ((0, Rr, RrB, P_LBA), (1, Ii, IiB, P_LBX)):
                                bb = [bk("F"), bk("F")]
                                for kk in range(8):
                                    mm(bb[kk // 4][0][:, (kk % 4) * 128:(kk % 4) * 128 + L], wbd[:, a_, kk, :], xcb[:, kk, :L], True, True, [wbdB, xcbB], [bb[kk // 4][2]], kk % 4 == 3)
                                for kk in range(8):
                                    A(dst[:, kk, :L], bb[kk // 4][0][:, (kk % 4) * 128:(kk % 4) * 128 + L], AF.Tanh, [bb[kk // 4][2], hbB], [dstB], scale=0.5, bias=hb[:, bcol - P_LBA + kk:bcol - P_LBA + kk + 1])
                                rel(*bb)
                                yield
                            bz_ = [bk("F"), bk("F")]
                            for n in range(2):
                                wt_, wB = Wt(l, "B", f"lz{n}")
                                for m in range(4):
                                    for kk in range(8):
                                        mm(bz_[n][0][:, m * 128:m * 128 + L], wt_[:, kk, m * 128:(m + 1) * 128], xn[:, kk, :L], kk == 0, kk == 7, [wB, xnB], [bz_[n][2]], kk == 7)
                                    if m == 1:
                                        yield
                                yield
                            stt(Rr[:, :, :L], Rr[:, :, :L], 1.0, ap(cA, 0, 128, 0, [[1, 8], [0, L]]), ALU.add, ALU.mult, [RrB, cAB], [RrB])
                            A(Rr[:, :, :L], Rr[:, :, :L], AF.Exp, [RrB], [RrB])
                            tt(Tt[:, :, :L], Rr[:, :, :L], Rr[:, :, :L], ALU.mult, [RrB], [TtB])
                            A(Tt[:, :, :L], Tt[:, :, :L], AF.Ln, [TtB, cfB], [TtB], scale=-1.0, bias=cf[:, C_ONES:C_ONES + 1])
                            A(Tt[:, :, :L], Tt[:, :, :L], AF.Exp, [TtB], [TtB], scale=0.5)
                            yield
                            stt(Ii[:, :, :L], Ii[:, :, :L], 1.0, xc[:, :, :L], ALU.add, ALU.mult, [IiB] + xcBk, [IiB])
                            stt(Ii[:, :, :L], Ii[:, :, :L], 0.5, Tt[:, :, :L], ALU.mult, ALU.mult, [IiB, TtB], [IiB])
                            yield
                            if smp:
                                h0, h0B = XT[2]
                                dma_in(h0[0:16, :], st_hl[l, :, :], h0B)
                                b = bk("F")
                                for kk in range(8):
                                    tr(b[0][:, kk * 16:(kk + 1) * 16], h0[0:16, kk * 128:(kk + 1) * 128], identf[0:16, 0:16], [h0B, cfB], [b[2]], kk == 7)
                                a0 = ap(Rr, 0, 128, 0, [[128, 8], [8, 16]])
                                u0 = ap(Ii, 0, 128, 0, [[128, 8], [8, 16]])
                                tt(tq[:, :, :], a0, ap(b[0], 0, 128, 0, [[16, 8], [1, 16]]), ALU.mult, [RrB, b[2]], [tqB])
                                rel(b)
                                tt(u0, u0, tq[:, :, :], ALU.add, [IiB, tqB], [IiB])
                                mset(a0, 0.0, [RrB])
                                for kk in range(8):
                                    k.op(dve, lambda e: e.tensor_tensor_scan(out=Hh[:, kk, :], data0=Rr[:, kk, :], data1=Ii[:, kk, :], initial=0.0, op0=ALU.mult, op1=ALU.add),
                                         [RrB, IiB], [HhB])
                                cp(tq[:, :, :], ap(Hh, 0, 128, 7, [[128, 8], [8, 16]]), [HhB], [tqB])
                                for n in range(2):
                                    b2 = bk("F")
                                    for kk in range(4):
                                        tr(b2[0][0:16, kk * 128:(kk + 1) * 128], tq[:, n * 4 + kk, :], identf[:, :], [tqB, cfB], [b2[2]], kk == 3)
                                    cp(ho[0:16, n * 512:(n + 1) * 512], b2[0][0:16, :], [b2[2]], [hoB])
                                    rel(b2)
                                dma_out(s_hl[l, :, :], ho[0:16, :], hoB, final=True)
                            else:
                                for kk in range(8):
                                    k.op(dve, lambda e: e.tensor_tensor_scan(out=Hh[:, kk, :L], data0=Rr[:, kk, :L], data1=Ii[:, kk, :L], initial=hc[:, kk:kk + 1], op0=ALU.mult, op1=ALU.add),
                                         [RrB, IiB, hcB], [HhB])
                                cp(hc[:, :], ap(Hh, 0, 128, L - 1, [[128, 8]]), [HhB], [hcB])
                                if i == 16:
                                    b = bk("F")
                                    tr(b[0][0:8, 0:128], hc[:, 0:8], identf[:, :], [hcB, cfB], [b[2]], True)
                                    cp(ho[0:8, 0:128], b[0][0:8, 0:128], [b[2]], [hoB])
                                    rel(b)
                                    dma_out(dap(p_hl, l * 1024, [[128, 8], [1, 128]]), ho[0:8, 0:128], hoB, final=True)
                            yield
                            for n in range(2):
                                A(Tt[:, 4 * n:4 * n + 4, :L], ap(bz_[n][0], 0, 128, 0, [[128, 4], [1, L]]), AF.Silu, [bz_[n][2]], [TtB])
                            rel(*bz_)
                            tt(yT[:, :, :L], Hh[:, :, :L], Tt[:, :, :L], ALU.mult, [HhB, TtB], [yTB])
                            yield

                        def back(i):
                            xn, xnB = XN[i % 3]
                            yT, yTB = yT2[i % 2]
                            if i + 1 < 18:
                                tail_prefetch(i + 1, "mid")
                            yield from tail(i, "B", "mid", yT, yTB, xn, xnB, T)

                        tail_prefetch(0, "mid")
                        drive(front, back, "B", ["lx0", "lx1", "lz0", "lz1"], nxt_phase)

                def phaseC(nxt_phase):
                    with ExitStack() as pst:
                        P = lambda n, s, dt=F32, sem=False: sbt(f"c_{n}_{l}", s, dt, sem=sem, stack=pst)
                        P2 = lambda n, s, dt=F32: [P(f"{n}{i_}", s, dt) for i_ in range(2)]
                        xf, xfB = P("xf", [128, 2096], BF16)
                        dg, dgB = P("dg", [128, 64, 128], BF16)
                        tt(dg[:, :, :], ap(cb, 0, 128, 0, [[0, 64], [1, 128]]), ap(ppt, 0, 128, l * NPP + P_SCW, [[1, 64], [0, 128]]), ALU.mult, [cbB, ppB], [dgB])
                        xbcb2 = P2("xbcb", [128, 16, 128], BF16)
                        xdt2 = P2("xdt", [128, 1024], BF16); xw2 = P2("xw", [128, 1024], BF16); xsD2 = P2("xsD", [128, 1024], BF16)
                        btok2 = P2("btok", [128, 512], BF16)
                        Xq, XqB = P("Xq", [128, 4, 128]); Eq, EqB = P("Eq", [128, 4, 128])
                        Wq = [P(f"Wq{i_}", [128, 4, 128], BF16) for i_ in range(2)]
                        Gm2 = P2("Gm", [128, 4, 128], BF16)
                        ST, STB = P("ST", [128, 1024], F32, sem=True); STb, STbB = P("STb", [128, 1024], BF16)
                        yc, ycB = P("yc", [128, 1024], F32, sem=True); sg, sgB = P("sg", [128, 1024])
                        ltb, ltbB = P("ltb", [128, 1024], F32, sem=True)
                        ytok, ytokB = P("ytok", [128, 1024], BF16); yT, yTB = P("yT", [128, 8, 128], BF16)
                        mT, mTB = P("mT", [128, 8, 128], BF16); mT2 = mT.reshape([128, 1024])
                        d0, d0B = P("d0", [128, 32]); dtt, dttB = P("dt", [128, 16]); dta2 = P2("dta", [128, 16])
                        cum, cumB = P("cum", [128, 16]); ecum2 = P2("ecum", [128, 16]); wsx, wsxB = P("wsx", [128, 16]); ecl2 = P2("ecl", [128, 16])
                        aneg, anegB = P("aneg", [128, 16]); ss, ssB = P("ss", [128, 4]); fsm, fsmB = P("fsm", [128, 4])
                        T = {"sg": (sg, sgB), "mrg": (ytok, ytokB), "mT": (mT, mTB), "fsm": (fsm, fsmB)}
                        A(aneg[:, :], prt[:, l, R_ALOG:R_ALOG + 16], AF.Exp, [prB], [anegB])
                        ts(aneg[:, :], aneg[:, :], -1.0, None, ALU.mult, None, [anegB], [anegB])
                        mset(xf[:], 0.0, [xfB]); mset(ST[:], 0.0, [STB]); mset(STb[:], 0.0, [STbB])

                        def front(i):
                            t0, L = TILES[i]
                            smp = (i == SAMPLE)
                            par = i % 2
                            xn, xnB = XN[i % 3]
                            if i + 1 < 18:
                                xn_load(i + 1)
                            xbcb, xbcbB = xbcb2[par]; xdt, xdtB = xdt2[par]; xw, xwB = xw2[par]; xsD, xsDB = xsD2[par]
                            btok, btokB = btok2[par]; Gm, GmB = Gm2[par]; dta, dtaB = dta2[par]; ecum, ecumB = ecum2[par]; ecl, eclB = ecl2[par]
                            C_TRI = C_TRIS if smp else C_TRIP
                            C_ONE = C_SAMES if smp else C_ONES
                            def conv_chunks(ms, xwin, accv):
                                ms = list(ms)
                                for g0 in range(0, len(ms), 4):
                                    grp = ms[g0:g0 + 4]
                                    b = bk("F")
                                    ov = (lambda q: ap(b[0], 0, 128, q * 128, [[8, 16], [1, 8]])) if smp else (lambda q: b[0][:, q * 128:q * 128 + L])
                                    for q, m in enumerate(grp):
                                        for j in range(4):
                                            mm(ov(q), dg[:, m * 4 + j, :], xwin(m, j), j == 0, j == 3, [dgB, xfB], [b[2]], j == 3 and q == len(grp) - 1)
                                    for q, m in enumerate(grp):
                                        A(xbcb[:, m, :L], b[0][:, q * 128:q * 128 + L], AF.Silu, [b[2], ppB], [xbcbB], bias=ppc(P_SCB + m))
                                    rel(b)
                                    yield

                            if smp:
                                accv = lambda a_: ap(a_, 0, 128, 0, [[8, 16], [1, 8]])
                                s48, s48B = ltb, ltbB
                                for hf in range(2):
                                    xwin = lambda m, j, hf=hf: ap(xf, 0, 128, (m - 8 * hf) * 176 + j, [[11, 16], [1, 8]])
                                    dma_in(s48[0:48, :], st_cs[l, :, hf * 1024:(hf + 1) * 1024], s48B)
                                    b = bk("F")
                                    for kk in range(8):
                                        tr(b[0][:, kk * 48:(kk + 1) * 48], s48[0:48, kk * 128:(kk + 1) * 128], identf[0:48, 0:48], [s48B, cfB], [b[2]], kk == 7)
                                    cp(ap(xf, 0, 128, 0, [[176, 8], [11, 16], [1, 3]]), ap(b[0], 0, 128, 0, [[48, 8], [3, 16], [1, 3]]), [b[2]], [xfB])
                                    rel(b)
                                    for n in range(2):
                                        wt_, wB = Wt(l, "C", f"xb{2 * hf + n}")
                                        b = bk("F")
                                        for m in range(4):
                                            for kk in range(8):
                                                mm(b[0][:, m * 128:m * 128 + L], wt_[:, kk, m * 128:(m + 1) * 128], xn[:, kk, :L], kk == 0, kk == 7, [wB, xnB], [b[2]], kk == 7 and m == 3)
                                        A(ap(xf, 0, 128, n * 4 * 176 + 3, [[176, 4], [11, 16], [1, 8]]), ap(b[0], 0, 128, 0, [[128, 4], [8, 16], [1, 8]]), AF.Copy, [b[2]], [xfB])
                                        rel(b)
                                        yield
                                    yield from conv_chunks(range(8 * hf, 8 * hf + 8), xwin, accv)
                            else:
                                xwin = lambda m, j: ap(xf, 0, 128, m * 131 + j, [[1, L]])
                                accv = lambda a_: a_[:, :L]
                                for n in range(4):
                                    wt_, wB = Wt(l, "C", f"xb{n}")
                                    b = bk("F")
                                    for m in range(4):
                                        for kk in range(8):
                                            mm(b[0][:, m * 128:m * 128 + L], wt_[:, kk, m * 128:(m + 1) * 128], xn[:, kk, :L], kk == 0, kk == 7, [wB, xnB], [b[2]], kk == 7)
                                        if m == 1:
                                            yield
                                    A(ap(xf, 0, 128, n * 4 * 131 + 3, [[131, 4], [1, L]]), ap(b[0], 0, 128, 0, [[128, 4], [1, L]]), AF.Copy, [b[2]], [xfB])
                                    rel(b)
                                    yield
                            if smp or i == 16:
                                for hf in range(2):
                                    bt_ = [bk("F"), bk("F")]
                                    for n in range(2):
                                        wt_, wB = Wt(l, "C", f"xb{hf * 2 + n}")
                                        for kk in range(8):
                                            mm(bt_[n][0][:L, :], xn[:, kk, :L], wt_[:, kk, :], kk == 0, kk == 7, [xnB, wB], [bt_[n][2]], kk == 7)
                                    for n in range(2):
                                        A(ltb[:L, n * 512:(n + 1) * 512], bt_[n][0][:L, :], AF.Copy, [bt_[n][2]], [ltbB])
                                    rel(*bt_)
                                    if smp:
                                        for r in range(3):
                                            dma_out(dap(s_cs, l * 16 * 3 * 2048 + r * 2048 + hf * 1024, [[3 * 2048, 16], [1, 1024]]), bass.AP(ltb, (5 + r) * 1024, [[8 * 1024, 16], [1, 1024]]), ltbB, final=True)
                                    else:
                                        dma_out(p_cs[l, :, hf * 1024:(hf + 1) * 1024], ltb[13:16, :], ltbB, final=True)
                                    yield
                            bd = bk("F")
                            for kk in range(8):
                                mm(bd[0][:L, 0:16], xn[:, kk, :L], wdt[:, kk, :], kk == 0, kk == 7, [xnB, wdtB], [bd[2]], kk == 7)
                            tt(d0[:L, 0:16], bd[0][:L, 0:16], prc(R_DTB, 16, L), ALU.add, [bd[2], prB], [d0B])
                            A(d0[:L, 16:32], d0[:L, 0:16], AF.Exp, [d0B], [d0B])
                            A(dtt[:L, :], d0[:L, 16:32], AF.Ln, [d0B, cfB], [dttB], bias=cf[:L, C_ONES:C_ONES + 1])
                            tt(dta[:L, :], dtt[:L, :], aneg[:L, :], ALU.mult, [dttB, anegB], [dtaB])
                            yield
                            if not smp:
                                yield from conv_chunks(range(16), xwin, accv)
                                cp(ap(xf, 0, 128, 0, [[131, 16], [1, 3]]), ap(xf, 0, 128, L, [[131, 16], [1, 3]]), [xfB], [xfB])
                            mm(bd[0][:L, 16:32], cfm(C_TRI, L), dta[:L, :], True, True, [cfB, dtaB], [bd[2]], False)
                            mm(bd[0][:, 32:48], cf[:L, C_ONE:C_ONE + 128], dta[:L, :], True, True, [cfB, dtaB], [bd[2]], True)
                            cp(cum[:L, :], bd[0][:L, 16:32], [bd[2]], [cumB])
                            A(ecum[:L, :], cum[:L, :], AF.Exp, [cumB], [ecumB])
                            tt(wsx[:L, :], bd[0][:L, 32:48], cum[:L, :], ALU.subtract, [bd[2], cumB], [wsxB])
                            A(wsx[:L, :], wsx[:L, :], AF.Exp, [wsxB], [wsxB])
                            tt(wsx[:L, :], wsx[:L, :], dtt[:L, :], ALU.mult, [wsxB, dttB], [wsxB])
                            A(ecl[:, :], bd[0][:, 32:48], AF.Exp, [bd[2]], [eclB])
                            rel(bd)
                            yield
                            pxs = bk("F")
                            for m in range(8):
                                tr(pxs[1][:L, m * 128:(m + 1) * 128], xbcb[:, m, :L], identb[:, :], [xbcbB, cbB], [pxs[2]], m == 7)
                            pB_ = bk("F")
                            for g in range(4):
                                tr(pB_[1][:L, g * 128:(g + 1) * 128], xbcb[:, 8 + g, :L], identb[:, :], [xbcbB, cbB], [pB_[2]], g == 3)
                            cp(btok[:L, :], pB_[1][:L, 0:512], [pB_[2]], [btokB])
                            rel(pB_)
                            xsv = ap(pxs[1], 0, L, 0, [[64, 16], [1, 64]])
                            o3 = lambda t_: ap(t_, 0, L, 0, [[64, 16], [1, 64]])
                            tt(o3(xdt), xsv, ap(dtt, 0, L, 0, [[1, 16], [0, 64]]), ALU.mult, [pxs[2], dttB], [xdtB])
                            yield
                            tt(o3(xw), xsv, ap(wsx, 0, L, 0, [[1, 16], [0, 64]]), ALU.mult, [pxs[2], wsxB], [xwB])
                            tt(o3(xsD), xsv, ap(prt, 0, L, l * NPR + R_DD, [[1, 16], [0, 64]]), ALU.mult, [pxs[2], prB], [xsDB])
                            rel(pxs)
                            bgm = bk("F")
                            for g in range(4):
                                mm(bgm[0][:L, g * 128:g * 128 + L], xbcb[:, 8 + g, :L], xbcb[:, 12 + g, :L], True, True, [xbcbB], [bgm[2]], g == 3)
                            tt(Gm[:L, :, :L], ap(bgm[0], 0, L, 0, [[128, 4], [1, L]]), ap(cf, 0, L, C_TRI, [[0, 4], [1, L]]), ALU.mult, [bgm[2], cfB], [GmB])
                            rel(bgm)
                            yield

                        def back(i):
                            t0, L = TILES[i]
                            smp = (i == SAMPLE)
                            par = i % 2
                            xn, xnB = XN[i % 3]
                            if i + 1 < 18:
                                tail_prefetch(i + 1, "last")
                            xbcb, xbcbB = xbcb2[par]; xdt, xdtB = xdt2[par]; xw, xwB = xw2[par]; xsD, xsDB = xsD2[par]
                            btok, btokB = btok2[par]; Gm, GmB = Gm2[par]; dta, dtaB = dta2[par]; ecum, ecumB = ecum2[par]; ecl, eclB = ecl2[par]
                            C_TRI = C_TRIS if smp else C_TRIP
                            C_SGT = C_SGTS if smp else C_SGTP
                            if not smp:
                                bi = [bk(), bk()]
                                for g in range(4):
                                    mm(bi[g // 2][0][:L, (g % 2) * 256:(g % 2 + 1) * 256], xbcb[:, 12 + g, :L], STb[:, g * 256:(g + 1) * 256], True, True,
                                       [xbcbB, STbB], [bi[g // 2][2]], g % 2 == 1)
                            else:
                                rb = [P(f"rb{b_}", [128, 16, 8]) for b_ in range(2)]
                                eclP, eclPB = P("eclP", [128, 128])
                                for b_ in range(2):
                                    tt(rb[b_][0][:, :, :], ap(cf, 0, 128, C_ROWSEL, [[1, 16], [0, 8]]), ap(dta, 0, 128, b_, [[0, 16], [2, 8]]), ALU.mult, [cfB, dtaB], [rb[b_][1]])
                                b = bk()
                                mm(b[0][:, 0:128], cf[:, C_HALF0:C_HALF0 + 128], rb[0][0][:].rearrange("p a b -> p (a b)"), True, False, [cfB, rb[0][1]], [b[2]], False)
                                mm(b[0][:, 0:128], cf[:, C_HALF1:C_HALF1 + 128], rb[1][0][:].rearrange("p a b -> p (a b)"), False, True, [cfB, rb[1][1]], [b[2]], True)
                                A(eclP[:, :], b[0][:, 0:128], AF.Exp, [b[2]], [eclPB])
                                rel(b)
                                xwj = [(ytok, ytokB), (mT2, mTB)]
                                so = [(ST, STB), (yc, ycB)]
                                byT = [bk(), bk()]
                                stg = [XT[0], XT[2], (ltb, ltbB)]

                                def ld(j):
                                    s_, sB_ = stg[j % 3]
                                    dma_in(ap(s_, 0, 128, 0, [[128, 8], [1, 128]]), dap(st_hs, (l * 16 + j) * 1024 * 128, [[128, 128], [128 * 128, 8], [1, 128]]), sB_)

                                ld(0)
                                ld(1)
                                for j in range(16):
                                    if j + 2 < 16:
                                        ld(j + 2)
                                    sg_, sgB_ = stg[j % 3]
                                    b2 = [bk("F"), bk("F")]
                                    for c in range(8):
                                        tr(b2[c // 4][0][:, (c % 4) * 128:(c % 4 + 1) * 128], sg_[:, c * 128:(c + 1) * 128], identf[:, :], [sgB_, cfB], [b2[c // 4][2]], c % 4 == 3)
                                    for n in range(2):
                                        A(STb[:, n * 512:(n + 1) * 512], b2[n][0][:, :], AF.Copy, [b2[n][2]], [STbB])
                                    rel(*b2)
                                    for c in range(8):
                                        mm(byT[c // 4][0][:, (c % 4) * 128 + 8 * j:(c % 4) * 128 + 8 * j + 8], STb[:, c * 128:(c + 1) * 128], xbcb[:, 12 + c // 2, 8 * j:8 * j + 8],
                                           True, True, [STbB, xbcbB], [byT[c // 4][2]], c % 4 == 3)
                                    xj, xjB = xwj[j % 2]
                                    A(xj[:, :], xw[:, :], AF.Identity, [xwB, cfB], [xjB], scale=cf[:, C_ROWSEL + j:C_ROWSEL + j + 1])
                                    b3 = [bk(), bk()]
                                    for c in range(8):
                                        g = c // 2
                                        mm(b3[c // 4][0][:, (c % 4) * 128:(c % 4 + 1) * 128], xj[:, c * 128:(c + 1) * 128], btok[:, g * 128:(g + 1) * 128], True, True,
                                           [xjB, btokB], [b3[c // 4][2]], c % 4 == 3)
                                    so_, soB = so[j % 2]
                                    tt(ap(so_, 0, 128, 0, [[128, 8], [1, 128]]), ap(sg_, 0, 128, 0, [[128, 8], [1, 128]]), ap(eclP, 0, 128, j * 8, [[1, 8], [0, 128]]), ALU.mult,
                                       [sgB_, eclPB], [soB])
                                    for n in range(2):
                                        tt(so_[:, n * 512:(n + 1) * 512], so_[:, n * 512:(n + 1) * 512], b3[n][0][:, :], ALU.add, [soB, b3[n][2]], [soB])
                                    rel(*b3)
                                    dma_out(dap(s_hs, (l * 16 + j) * 1024 * 128, [[128, 128], [128 * 128, 8], [1, 128]]), ap(so_, 0, 128, 0, [[128, 8], [1, 128]]), soB, final=True)
                                    yield
                                for n in range(2):
                                    A(yc[:, n * 512:(n + 1) * 512], byT[n][0][:, :], AF.Copy, [byT[n][2]], [ycB])
                                rel(*byT)
                                bi = [bk(), bk()]
                                for c in range(8):
                                    tr(bi[c // 4][0][:, (c % 4) * 128:(c % 4 + 1) * 128], yc[:, c * 128:(c + 1) * 128], identf[:, :], [ycB, cfB], [bi[c // 4][2]], c % 4 == 3)
                            for n in range(2):
                                tt(ap(yc, 0, L, n * 512, [[64, 8], [1, 64]]), ap(bi[n][0], 0, L, 0, [[64, 8], [1, 64]]), ap(ecum, 0, L, n * 8, [[1, 8], [0, 64]]), ALU.mult,
                                   [bi[n][2], ecumB], [ycB])
                            rel(*bi)
                            yield
                            by = [bk(), bk()]
                            for q in range(4):
                                tt(Xq[:L, :, :L], ap(cf, 0, L, C_TRI, [[0, 4], [1, L]]), ap(dta, 0, L, 4 * q, [[1, 4], [0, L]]), ALU.mult, [cfB, dtaB], [XqB])
                                bsg = bk()
                                mm(ap(bsg[0], 0, L, 0, [[128, 4], [1, L]]), cfm(C_SGT, L), Xq[:L, :, :L], True, True, [cfB, XqB], [bsg[2]], True)
                                A(Eq[:L, :, :L], ap(bsg[0], 0, L, 0, [[128, 4], [1, L]]), AF.Exp, [bsg[2]], [EqB])
                                rel(bsg)
                                wq_, wqB = Wq[q % 2]
                                tt(wq_[:L, :, :L], Eq[:L, :, :L], ap(Gm, 0, L, q * 128, [[0, 4], [1, L]]), ALU.mult, [EqB, GmB], [wqB])
                                for e_ in range(4):
                                    h = 4 * q + e_
                                    o_ = by[h // 8][0][:L, (h % 8) * 64:(h % 8 + 1) * 64]
                                    mm(o_, wq_[:L, e_, :L], xdt[:L, h * 64:(h + 1) * 64], True, False, [wqB, xdtB], [by[h // 8][2]], False)
                                    mm(o_, identb[:L, :L], xsD[:L, h * 64:(h + 1) * 64], False, True, [cbB, xsDB], [by[h // 8][2]], True)
                                yield
                            for n in range(2):
                                tt(yc[:L, n * 512:(n + 1) * 512], yc[:L, n * 512:(n + 1) * 512], by[n][0][:L, :], ALU.add, [ycB, by[n][2]], [ycB])
                            rel(*by)
                            bz = [bk(), bk()]
                            for n in range(2):
                                wt_, wB = Wt(l, "C", f"sz{n}")
                                for kk in range(8):
                                    mm(bz[n][0][:L, :], xn[:, kk, :L], wt_[:, kk, :], kk == 0, kk == 7, [xnB, wB], [bz[n][2]], kk == 7)
                                yield
                            for n in range(2):
                                A(sg[:L, n * 512:(n + 1) * 512], bz[n][0][:L, :], AF.Silu, [bz[n][2]], [sgB])
                            rel(*bz)
                            yield
                            tt(yc[:L, :], yc[:L, :], sg[:L, :], ALU.mult, [ycB, sgB], [ycB])
                            A(ytok[:L, :], yc[:L, :], AF.Square, [ycB], [ytokB, ssB], accum=ss[:L, 0:1])
                            A(ss[:L, 1:2], ss[:L, 0:1], AF.Ln, [ssB, cfB], [ssB], scale=1.0 / 1024, bias=cf[:L, C_EPS:C_EPS + 1])
                            A(ss[:L, 3:4], ss[:L, 1:2], AF.Exp, [ssB], [ssB], scale=-0.5)
                            A(ytok[:L, :], yc[:L, :], AF.Identity, [ycB, ssB], [ytokB], scale=ss[:L, 3:4])
                            yield
                            pt = bk()
                            for j in range(8):
                                tr(pt[1][:, j * 128:j * 128 + L], ytok[:L, j * 128:(j + 1) * 128], identb[:L, :L], [ytokB, cbB], [pt[2]], j == 7)
                            tt(yT[:, :, :L], ap(pt[1], 0, 128, 0, [[128, 8], [1, L]]), ppbc(P_SG, 8, L), ALU.mult, [pt[2], ppB], [yTB])
                            rel(pt)
                            if not smp:
                                bu = [bk(), bk()]
                                for g in range(4):
                                    mm(bu[g // 2][0][:, (g % 2) * 256:(g % 2 + 1) * 256], btok[:L, g * 128:(g + 1) * 128], xw[:L, g * 256:(g + 1) * 256], True, True,
                                       [btokB, xwB], [bu[g // 2][2]], g % 2 == 1)
                                tt(ap(ST, 0, 128, 0, [[64, 16], [1, 64]]), ap(ST, 0, 128, 0, [[64, 16], [1, 64]]), ap(ecl, 0, 128, 0, [[1, 16], [0, 64]]), ALU.mult, [STB, eclB], [STB])
                                for n in range(2):
                                    tt(ST[:, n * 512:(n + 1) * 512], ST[:, n * 512:(n + 1) * 512], bu[n][0][:, :], ALU.add, [STB, bu[n][2]], [STB])
                                rel(*bu)
                                A(STb[:, :], ST[:, :], AF.Copy, [STB], [STbB])
                            yield
                            yield from tail(i, "C", "last", yT, yTB, xn, xnB, T)
                            if i == 16:
                                so_, soB = yc, ycB
                                b2 = [bk(), bk()]
                                for c in range(8):
                                    tr(b2[c // 4][0][:, (c % 4) * 128:(c % 4 + 1) * 128], ST[:, c * 128:(c + 1) * 128], identf[:, :], [STB, cfB], [b2[c // 4][2]], c % 4 == 3)
                                for n in range(2):
                                    cp(so_[:, n * 512:(n + 1) * 512], b2[n][0][:, :], [b2[n][2]], [soB])
                                rel(*b2)
                                dma_out(dap(p_hs, l * 1024 * 128, [[128, 128], [128 * 128, 8], [1, 128]]), ap(so_, 0, 128, 0, [[128, 8], [1, 128]]), soB, final=True)

                        tail_prefetch(0, "last")
                        drive(front, back, "C", ["xb0", "xb1", "xb2", "xb3"], nxt_phase)

                PH = {"A": phaseA, "B": phaseB, "C": phaseC}
                for ph in "ABC":
                    if (l, ph) not in phases:
                        continue
                    ok = load_weights(l, ph)
                    assert ok, "not enough weight slots"
                    nxt = phases.index((l, ph)) + 1
                    if nxt < len(phases):
                        load_weights(*phases[nxt])
                    PH[ph](phases[nxt] if nxt < len(phases) else None)
                    release_weights(l, ph)

            for s in k.dram_out_sems:
                sp.h.wait_ge(s.h, s.val)
        print("instr counts", {e.name: (e.nins, e.nwait) for e in (pe, act, dve, pool, sp)}, "sems", k.nsem)
    return nc


def _consts():
    r = np.arange(128)
    seq = r // 8
    same = (seq[:, None] == seq[None, :]).astype(np.float32)
    cfa = np.zeros((128, NCF), np.float32)
    cfa[:, C_IDF:C_IDF + 128] = np.eye(128)
    cfa[:, C_TRIP:C_TRIP + 128] = (r[:, None] <= r[None, :])
    cfa[:, C_SGTP:C_SGTP + 128] = (r[:, None] > r[None, :])
    cfa[:, C_ONES:C_ONES + 128] = 1.0
    cfa[:, C_TRIS:C_TRIS + 128] = (r[:, None] <= r[None, :]) * same
    cfa[:, C_SGTS:C_SGTS + 128] = (r[:, None] > r[None, :]) * same
    cfa[:, C_SAMES:C_SAMES + 128] = same
    cfa[:, C_ROWSEL:C_ROWSEL + 16] = (seq[:, None] == np.arange(16)[None, :])
    cfa[:, C_HALF0:C_HALF0 + 128] = (r[None, :] // 64 == 0)
    cfa[:, C_HALF1:C_HALF1 + 128] = (r[None, :] // 64 == 1)
    cfa[:, C_EPS] = EPS
    cba = np.zeros((128, NCB), np.float32)
    cba[:, B_IDB:B_IDB + 128] = np.eye(128)
    colsel = (np.arange(16)[:, None] == seq[None, :]).astype(np.float32).reshape(1, 2048)
    cba[:, B_COLSEL:B_COLSEL + 2048] = colsel
    return cfa, cba.astype(ml_dtypes.bfloat16)


_PROG = {}


def kernel(x_prompt, x_sample, state_mlstm_c, state_mlstm_n, state_mlstm_m, state_rglru_h,
           state_rglru_conv, state_ssd_h, state_ssd_conv, meta_tokens, w_in, norm_g, ml_f_bias,
           ml_norm_g, lru_conv_w, lru_conv_b, lru_w_a, lru_b_a, lru_w_x, lru_b_x, lru_lambda,
           ssd_conv_w, ssd_conv_b, ssd_dt_bias, ssd_a_log, ssd_d, ssd_norm_g,
           w_br_ml, w_br_lru, w_br_ssd, w_out, final_norm_g, _cfg=None):
    f = lambda a: np.ascontiguousarray(np.asarray(a, dtype=np.float32))
    x_prompt, x_sample = f(x_prompt), f(x_sample)
    w_in_, w_out_ = f(w_in), f(w_out)
    w_br_ = np.ascontiguousarray(np.stack([f(w_br_ml), f(w_br_lru), f(w_br_ssd)], axis=1))
    wbd = np.zeros((2, 2, 8, 128, 128), np.float32)
    for a_, w_ in enumerate((f(lru_w_a), f(lru_w_x))):
        for n in range(16):
            o = (n % 2) * 64
            wbd[:, a_, n // 2, o:o + 64, o:o + 64] = w_[:, n]
    pp = np.zeros((2, 128, NPP), np.float32)
    pr = np.zeros((2, 128, NPR), np.float32)
    col = lambda v, n: f(v).reshape(2, n, 128).transpose(0, 2, 1)
    pp[:, :, P_NG:P_NG + 8] = col(norm_g, 8)
    pp[:, :, P_MG:P_MG + 8] = col(f(ml_norm_g).reshape(2, 1024), 8)
    pp[:, :, P_SG:P_SG + 8] = col(ssd_norm_g, 8)
    pp[:, :, P_LCW:P_LCW + 32] = f(lru_conv_w).reshape(2, 4, 8, 128).transpose(0, 3, 2, 1).reshape(2, 128, 32)
    pp[:, :, P_LCB:P_LCB + 8] = col(lru_conv_b, 8)
    pp[:, :, P_LAM:P_LAM + 8] = col(lru_lambda, 8)
    pp[:, :, P_SCW:P_SCW + 64] = f(ssd_conv_w).reshape(2, 4, 16, 128).transpose(0, 3, 2, 1).reshape(2, 128, 64)
    pp[:, :, P_SCB:P_SCB + 16] = col(ssd_conv_b, 16)
    pp[:, :, P_LBA:P_LBA + 8] = col(lru_b_a, 8)
    pp[:, :, P_LBX:P_LBX + 8] = col(lru_b_x, 8)
    pr[:, :, R_FB:R_FB + 4] = f(ml_f_bias)[:, None, :]
    pr[:, :, R_DTB:R_DTB + 16] = f(ssd_dt_bias)[:, None, :]
    pr[:, :, R_ALOG:R_ALOG + 16] = f(ssd_a_log)[:, None, :]
    pr[:, :, R_DD:R_DD + 16] = f(ssd_d)[:, None, :]
    fgr = np.ascontiguousarray(np.broadcast_to(f(final_norm_g)[None, :], (128, 1024)))
    cfa, cba = _consts()
    meta = f(meta_tokens)
    smc, smn, smm = f(state_mlstm_c), f(state_mlstm_n), f(state_mlstm_m)
    shl, scl, shs, scs = f(state_rglru_h), f(state_rglru_conv), f(state_ssd_h), f(state_ssd_conv)
    in_maps = []
    for c in range(8):
        sl = slice(16 * c, 16 * c + 16)
        in_maps.append({
            "xp": x_prompt[c], "meta": meta, "xs": x_sample[sl].reshape(128, 1024),
            "st_c": np.ascontiguousarray(smc[:, sl]), "st_n": np.ascontiguousarray(smn[:, sl]).reshape(2, 64, 128),
            "st_m": np.ascontiguousarray(smm[:, sl]), "st_hl": np.ascontiguousarray(shl[:, sl]),
            "st_cl": np.ascontiguousarray(scl[:, sl]).reshape(2, 48, 1024),
            "st_hs": np.ascontiguousarray(shs[:, sl]).reshape(2, 16, 1024, 128),
            "st_cs": np.ascontiguousarray(scs[:, sl]).reshape(2, 48, 2048),
            "w_in": w_in_, "w_br": w_br_, "w_out": w_out_, "wbd": wbd, "pp": pp, "pr": pr, "fg": fgr, "cf": cfa, "cb": cba,
        })
    key = repr(_cfg)
    if key not in _PROG:
        _PROG[key] = build_program(_cfg)
    nc = _PROG[key]
    res = run_bass_kernel_spmd(nc, in_maps, core_ids=list(range(8)))
    R = res.results
    cat = lambda n, ax: np.concatenate([np.asarray(r[n]) for r in R], axis=ax)
    y_prompt = np.stack([np.asarray(r["y_p"]) for r in R], axis=0)
    y_sample = cat("y_s", 0).reshape(128, 8, 1024)
    stk = lambda n: np.stack([np.asarray(r[n]) for r in R], axis=1)
    p_c = stk("p_c"); p_n = stk("p_n"); p_m = stk("p_m"); p_hl = stk("p_hl"); p_cl = stk("p_cl")
    p_hs = stk("p_hs").reshape(2, 8, 16, 64, 128); p_cs = stk("p_cs")
    s_c = cat("s_c", 1); s_n = cat("s_n", 1).reshape(2, 128, 4, 128); s_m = cat("s_m", 1)
    s_hl = cat("s_hl", 1); s_cl = cat("s_cl", 1); s_hs = cat("s_hs", 1).reshape(2, 128, 16, 64, 128); s_cs = cat("s_cs", 1)
    outs = (y_prompt, y_sample, p_c, p_n, p_m, p_hl, p_cl, p_hs, p_cs, s_c, s_n, s_m, s_hl, s_cl, s_hs, s_cs)
    return tuple(np.ascontiguousarray(o, dtype=np.float32) for o in outs)
```

```python
import numpy as np
import ml_dtypes
from contextlib import ExitStack
import concourse.bass as bass
import concourse.mybir as mybir
from concourse.bass_utils import run_bass_kernel_spmd

F32 = mybir.dt.float32
BF16 = mybir.dt.bfloat16
AF = mybir.ActivationFunctionType
ALU = mybir.AluOpType
AX = mybir.AxisListType

EPS = 1e-6
NIN = 12312
OQ, OK_, OV, OI, OO, OZ = 0, 512, 1024, 2048, 2056, 3080
OLX, OLZ = 4104, 5128
OSZ, OXBC, ODT, OG = 6152, 7176, 9224, 9240
TILES = [(i * 128, 128) for i in range(16)] + [(2048, 16), (2064, 128)]
NTOK = 2192
SAMPLE = 17
NSLOT = 12
P_NG, P_MG, P_SG, P_LCW, P_LCB, P_LAM, P_SCW, P_SCB, P_LBA, P_LBX, NPP = 0, 8, 16, 24, 56, 64, 72, 136, 152, 160, 168
R_FB, R_DTB, R_ALOG, R_DD, NPR = 0, 4, 20, 36, 52
C_IDF, C_TRIP, C_SGTP, C_ONES, C_TRIS, C_SGTS, C_SAMES, C_ROWSEL, C_HALF0, C_HALF1, C_EPS, NCF = 0, 128, 256, 384, 512, 640, 768, 896, 912, 1040, 1168, 1172
B_IDB, B_COLSEL, NCB = 0, 128, 128 + 2048


class Sem:
    def __init__(self, h, name):
        self.h = h
        self.name = name
        self.val = 0


class Buf:
    __slots__ = ("name", "excl", "w", "rd", "sem")

    def __init__(self, name, excl=False, sem=None):
        self.name = name
        self.excl = excl
        self.w = None
        self.rd = {}
        self.sem = sem


class Eng:
    def __init__(self, name, h, sem, is_pe=False):
        self.name = name
        self.h = h
        self.sem = sem
        self.is_pe = is_pe
        self.waited = {}
        self.nwait = 0
        self.nins = 0


class K:
    def __init__(self, nc, stack):
        self.nc = nc
        self.stack = stack
        self.nsem = 0
        self.pe = Eng("pe", nc.tensor, self.new_sem("s_pe"), is_pe=True)
        self.act = Eng("act", nc.scalar, self.new_sem("s_act"))
        self.dve = Eng("dve", nc.vector, self.new_sem("s_dve"))
        self.pool = Eng("pool", nc.gpsimd, self.new_sem("s_pool"))
        self.sp = Eng("sp", nc.sync, self.new_sem("s_sp"))
        self.dram_out_sems = {}

    def new_sem(self, name):
        self.nsem += 1
        return Sem(self.stack.enter_context(self.nc.semaphore(name)), name)

    def _deps(self, engid, is_pe, R, W):
        deps = []
        for b in R:
            if b.w is not None:
                e, s, v = b.w
                if not (e == engid and is_pe):
                    deps.append((s, v))
            if b.excl:
                for (e, s), v in b.rd.items():
                    if e != engid:
                        deps.append((s, v))
        for b in W:
            if b.w is not None:
                e, s, v = b.w
                if e != engid or engid == "dma":
                    deps.append((s, v))
            for (e, s), v in b.rd.items():
                if e != engid or engid == "dma":
                    deps.append((s, v))
        return deps

    def _wait(self, eng, deps):
        best = {}
        for s, v in deps:
            if v > best.get(s, 0):
                best[s] = v
        for s, v in best.items():
            if eng.waited.get(s, 0) >= v:
                continue
            if v > s.val:
                raise RuntimeError(f"wait on un-emitted milestone {s.name} {v}>{s.val} from {eng.name}")
            eng.h.wait_ge(s.h, v)
            eng.waited[s] = v
            eng.nwait += 1

    def _mark(self, engid, ev_sem, ev_val, R, W):
        for b in R:
            b.rd[(engid, ev_sem)] = ev_val
        for b in W:
            b.w = (engid, ev_sem, ev_val)
            b.rd = {}

    def op(self, eng, fn, R=(), W=(), inc=True):
        deps = self._deps(eng.name, eng.is_pe, R, W)
        self._wait(eng, deps)
        ins = fn(eng.h)
        eng.nins += 1
        if inc:
            eng.sem.val += 1
            ins.then_inc(eng.sem.h, 1)
            val = eng.sem.val
        else:
            val = eng.sem.val + 1
        self._mark(eng.name, eng.sem, val, R, W)
        return ins

    def dma(self, eng, out, in_, R=(), W=(), sem=None, is_out=False, **kw):
        deps = self._deps("dma", False, R, W)
        self._wait(eng, deps)
        ins = eng.h.dma_start(out=out, in_=in_, **kw)
        eng.nins += 1
        sem.val += 16
        ins.then_inc(sem.h, 16)
        self._mark("dma", sem, sem.val, R, W)
        if is_out:
            self.dram_out_sems[sem] = sem.val
        return ins

    def barrier(self, engs):
        for e in engs:
            deps = [(o.sem, o.sem.val) for o in engs if o is not e]
            self._wait(e, deps)


def ap(t, p0, npart, f0, dims):
    F = 1
    for s in t.shape[1:]:
        F *= s
    return bass.AP(t, p0 * F + f0, [[F, npart]] + [list(d) for d in dims])


def dap(t, off, dims):
    return bass.AP(t, off, [list(d) for d in dims])


def build_program(cfg=None):
    cfg = cfg or {}
    nlayers = cfg.get("nlayers", 2)
    nc = bass.Bass("TRN2", target_bir_lowering=False)
    di = lambda n, s, dt=F32: nc.dram_tensor(n, list(s), dt, kind="ExternalInput")
    do = lambda n, s: nc.dram_tensor(n, list(s), F32, kind="ExternalOutput")
    dint = lambda n, s, dt=F32: nc.dram_tensor(n, list(s), dt, kind="Internal")
    xp = di("xp", [2048, 1024]); meta = di("meta", [16, 1024]); xs = di("xs", [128, 1024])
    st_c = di("st_c", [2, 16, 4, 128, 256]); st_n = di("st_n", [2, 64, 128]); st_m = di("st_m", [2, 16, 4])
    st_hl = di("st_hl", [2, 16, 1024]); st_cl = di("st_cl", [2, 48, 1024])
    st_hs = di("st_hs", [2, 16, 1024, 128]); st_cs = di("st_cs", [2, 48, 2048])
    w_in = di("w_in", [2, 1024, NIN]); w_br = di("w_br", [2, 3, 1024, 1024]); w_out = di("w_out", [2, 1024, 1024])
    wbd_d = di("wbd", [2, 2, 8, 128, 128])
    pp_d = di("pp", [2, 128, NPP]); pr_d = di("pr", [2, 128, NPR]); fg_d = di("fg", [128, 1024])
    cf_d = di("cf", [128, NCF]); cb_d = di("cb", [128, NCB], BF16)
    y_p = do("y_p", [2048, 1024]); y_s = do("y_s", [128, 1024])
    p_c = do("p_c", [2, 4, 128, 256]); p_n = do("p_n", [2, 4, 128]); p_m = do("p_m", [2, 4])
    p_hl = do("p_hl", [2, 1024]); p_cl = do("p_cl", [2, 3, 1024]); p_hs = do("p_hs", [2, 1024, 128]); p_cs = do("p_cs", [2, 3, 2048])
    s_c = do("s_c", [2, 16, 4, 128, 256]); s_n = do("s_n", [2, 64, 128]); s_m = do("s_m", [2, 16, 4])
    s_hl = do("s_hl", [2, 16, 1024]); s_cl = do("s_cl", [2, 16, 3, 1024]); s_hs = do("s_hs", [2, 16, 1024, 128]); s_cs = do("s_cs", [2, 16, 3, 2048])
    xscr = dint("xscr", [NTOK, 1024]); mscr = dint("mscr", [NTOK, 1024]); xnscr = dint("xnscr", [18, 128, 8, 128], BF16)

    with ExitStack() as st:
        k = K(nc, st)
        pe, act, dve, pool, sp = k.pe, k.act, k.dve, k.pool, k.sp

        def sbt(name, shape, dt=F32, sem=False, stack=st):
            t = stack.enter_context(nc.sbuf_tensor("sb_" + name, list(shape), dt))
            b = Buf(name, sem=(k.new_sem("d_" + name) if sem else None))
            return t, b

        slots = [sbt(f"slot{i}", [128, 8, 512], BF16, sem=True) for i in range(NSLOT)]
        XT = [sbt(f"xt{i}", [128, 1024], F32, sem=True) for i in range(4)]
        XN = [sbt(f"xn{i}", [128, 8, 128], BF16, sem=True) for i in range(3)]
        cf, cfB = sbt("cf", [128, NCF], F32, sem=True)
        cb, cbB = sbt("cb", [128, 128], BF16, sem=True)
        fg, fgB = sbt("fg", [128, 1024], F32, sem=True)
        ppt, ppB = sbt("ppt", [128, 2, NPP], F32, sem=True)
        prt, prB = sbt("prt", [128, 2, NPR], F32, sem=True)
        wif, wifB = sbt("wif", [128, 8, 8], BF16, sem=True)
        wdt, wdtB = sbt("wdt", [128, 8, 16], BF16, sem=True)
        ps = []
        for i in range(8):
            t = st.enter_context(nc.psum_tensor(f"ps{i}", [128, 512], F32))
            ps.append((t, t.bitcast(BF16), Buf(f"ps{i}", excl=True)))
        bank_free = {"F": [0, 1, 2], "B": [3, 4, 5, 6, 7]}

        def bk(pool="B"):
            if not bank_free[pool]:
                raise RuntimeError("out of PSUM banks in pool " + pool)
            return ps[bank_free[pool].pop(0)]

        def rel(*bs):
            for b in bs:
                i = [x[2] for x in ps].index(b[2])
                p = "F" if i < 3 else "B"
                assert i not in bank_free[p]
                bank_free[p].append(i)

        def run(gens):
            if cfg.get("seq"):
                for g in gens:
                    for _ in g:
                        pass
                return
            act_ = list(gens)
            while act_:
                for g in list(act_):
                    try:
                        next(g)
                    except StopIteration:
                        act_.remove(g)

        XS = [Buf(f"xscr{i}") for i in range(18)]
        MS = [Buf(f"mscr{i}") for i in range(18)]
        XNS = [Buf(f"xnscr{i}") for i in range(18)]

        def mm(out, lhsT, rhs, start, stop, R, W, inc):
            k.op(pe, lambda e: e.matmul(out, lhsT=lhsT, rhs=rhs, start=start, stop=stop), R, W, inc)

        def tr(out, in_, ident, R, W, inc):
            k.op(pe, lambda e: e.transpose(out=out, in_=in_, identity=ident), R, W, inc)

        def A(out, in_, func, R, W, scale=None, bias=None, accum=None):
            kw = {}
            if scale is not None:
                kw["scale"] = scale
            if bias is not None:
                kw["bias"] = bias
            if accum is not None:
                kw["accum_out"] = accum
            k.op(act, lambda e: e.activation(out=out, in_=in_, func=func, **kw), R, W)

        def tt(out, in0, in1, op, R, W, eng=None):
            k.op(eng or dve, lambda e: e.tensor_tensor(out=out, in0=in0, in1=in1, op=op), R, W)

        def ts(out, in0, s1, s2, op0, op1, R, W, eng=None):
            if op1 is None:
                k.op(eng or dve, lambda e: e.tensor_scalar(out=out, in0=in0, scalar1=s1, scalar2=None, op0=op0), R, W)
            else:
                k.op(eng or dve, lambda e: e.tensor_scalar(out=out, in0=in0, scalar1=s1, scalar2=s2, op0=op0, op1=op1), R, W)

        def stt(out, in0, scalar, in1, op0, op1, R, W):
            k.op(dve, lambda e: e.scalar_tensor_tensor(out=out, in0=in0, scalar=scalar, in1=in1, op0=op0, op1=op1), R, W)

        def cp(out, in_, R, W, eng=None):
            k.op(eng or dve, lambda e: e.tensor_copy(out=out, in_=in_), R, W)

        def rcp(out, in_, R, W):
            k.op(dve, lambda e: e.reciprocal(out=out, in_=in_), R, W)

        def mset(t_ap, val, W, eng=None):
            k.op(eng or dve, lambda e: e.memset(t_ap, val), (), W)

        def dma_in(dst_ap, src_ap, dstbuf, R=(), eng=None, **kw):
            k.dma(eng or sp, dst_ap, src_ap, R=R, W=[dstbuf], sem=dstbuf.sem, **kw)

        def dma_out(dst_ap, src_ap, srcbuf, W=(), final=False, eng=None, **kw):
            k.dma(eng or sp, dst_ap, src_ap, R=[srcbuf], W=W, sem=srcbuf.sem, is_out=final, **kw)

        identf = cf[:, C_IDF:C_IDF + 128]
        identb = cb[:, B_IDB:B_IDB + 128]

        def cfm(c0, L):
            return cf[:L, c0:c0 + L]

        slot_free = list(range(NSLOT))
        loaded = {}
        loaded_done = {}

        def wsrc(spec):
            kind = spec[0]
            if kind == "in":
                _, l, c0 = spec
                return dap(w_in, l * 1024 * NIN + c0, [[NIN, 128], [128 * NIN, 8], [1, 512]])
            if kind == "br":
                _, l, b, h = spec
                return dap(w_br, (l * 3 + b) * 1024 * 1024 + h * 512, [[1024, 128], [128 * 1024, 8], [1, 512]])
            _, l, h = spec
            return dap(w_out, l * 1024 * 1024 + h * 512, [[1024, 128], [128 * 1024, 8], [1, 512]])

        def phase_set(l, ph):
            if ph == "A":
                d = {"q": ("in", l, OQ), "k": ("in", l, OK_), "v0": ("in", l, OV), "v1": ("in", l, OV + 512),
                     "o0": ("in", l, OO), "o1": ("in", l, OO + 512), "z0": ("in", l, OZ), "z1": ("in", l, OZ + 512),
                     "g0": ("in", l, OG), "g1": ("in", l, OG + 512), "br0": ("br", l, 0, 0), "br1": ("br", l, 0, 1)}
            elif ph == "B":
                d = {"lx0": ("in", l, OLX), "lx1": ("in", l, OLX + 512), "lz0": ("in", l, OLZ), "lz1": ("in", l, OLZ + 512),
                     "g0": ("in", l, OG + 1024), "g1": ("in", l, OG + 1536), "br0": ("br", l, 1, 0), "br1": ("br", l, 1, 1)}
            else:
                d = {"sz0": ("in", l, OSZ), "sz1": ("in", l, OSZ + 512),
                     "xb0": ("in", l, OXBC), "xb1": ("in", l, OXBC + 512), "xb2": ("in", l, OXBC + 1024), "xb3": ("in", l, OXBC + 1536),
                     "g0": ("in", l, OG + 2048), "g1": ("in", l, OG + 2560), "br0": ("br", l, 2, 0), "br1": ("br", l, 2, 1),
                     "wo0": ("out", l, 0), "wo1": ("out", l, 1)}
            return d

        def load_weights(l, ph, only_free=True):
            key = (l, ph)
            d = phase_set(l, ph)
            have = loaded.setdefault(key, {})
            done = loaded_done.setdefault(key, set())
            for name, spec in d.items():
                if name in done:
                    continue
                if not slot_free:
                    return False
                si = slot_free.pop(0)
                t, b = slots[si]
                k.dma(pool, t[:], wsrc(spec), W=[b], sem=b.sem)
                have[name] = si
                done.add(name)
            return True

        def release_weights(l, ph, names=None):
            d = loaded[(l, ph)]
            for name in list(d.keys()):
                if names is None or name in names:
                    slot_free.append(d.pop(name))
            if names is None:
                loaded.pop((l, ph))

        def Wt(l, ph, name):
            si = loaded[(l, ph)][name]
            return slots[si]

        with nc.Block() as block:
            dma_in(cf[:], cf_d[:, :], cfB)
            dma_in(cb[:], cb_d[:, 0:128], cbB)
            dma_in(fg[:], fg_d[:, :], fgB)
            for l_ in range(2):
                dma_in(ppt[:, l_, :], pp_d[l_, :, :], ppB)
                dma_in(prt[:, l_, :], pr_d[l_, :, :], prB)
            phases = [(l, ph) for l in range(nlayers) for ph in "ABC" if cfg.get("ph" + ph, True)]
            if phases:
                load_weights(*phases[0])

            for l in range(nlayers):
                last_layer = (l == nlayers - 1)
                k.dma(pool, wif[:], dap(w_in, l * 1024 * NIN + OI, [[NIN, 128], [128 * NIN, 8], [1, 8]]), W=[wifB], sem=wifB.sem)
                k.dma(pool, wdt[:], dap(w_in, l * 1024 * NIN + ODT, [[NIN, 128], [128 * NIN, 8], [1, 16]]), W=[wdtB], sem=wdtB.sem)

                def ppc(c0, n=1):
                    return ppt[:, l, c0:c0 + n]

                def ppbc(c0, n, L):
                    return ap(ppt, 0, 128, l * NPP + c0, [[1, n], [0, L]])

                def prc(c0, n, L):
                    return prt[:L, l, c0:c0 + n]

                with ExitStack() as pst:
                    P = lambda n, s, dt=F32, sem=False: sbt(f"p0_{n}_{l}", s, dt, sem=sem, stack=pst)
                    xnb2 = [P(f"xnb{i_}", [128, 1024], BF16) for i_ in range(2)]
                    jk2 = [P(f"jk{i_}", [128, 1024], BF16) for i_ in range(2)]
                    sm2 = [P(f"sm{i_}", [128, 4], F32) for i_ in range(2)]

                    def p0_load(i):
                        t0, L = TILES[i]
                        xt, xb = XT[i % 4]
                        if l == 0:
                            if i == 0:
                                dma_in(xt[0:16, :], meta[:, :], xb)
                                dma_in(xt[16:128, :], xp[0:112, :], xb)
                            elif i < 16:
                                dma_in(xt[:, :], xp[128 * i - 16:128 * i + 112, :], xb)
                            elif i == 16:
                                dma_in(xt[0:16, :], xp[2032:2048, :], xb)
                            else:
                                dma_in(xt[:, :], xs[:, :], xb)
                        else:
                            dma_in(xt[:L, :], xscr[t0:t0 + L, :], xb, R=[XS[i]])

                    def p0(i):
                        t0, L = TILES[i]
                        xt, xb = XT[i % 4]
                        xnb, xnbB = xnb2[i % 2]; jk, jkB = jk2[i % 2]; sm, smB = sm2[i % 2]
                        if i + 2 < 18:
                            p0_load(i + 2)
                        if l == 0:
                            dma_out(xscr[t0:t0 + L, :], xt[:L, :], xb, W=[XS[i]])
                        A(jk[:L, :], xt[:L, :], AF.Square, [xb], [jkB, smB], accum=sm[:L, 0:1])
                        yield
                        A(sm[:L, 1:2], sm[:L, 0:1], AF.Ln, [smB, cfB], [smB], scale=1.0 / 1024, bias=cf[:L, C_EPS:C_EPS + 1])
                        A(sm[:L, 3:4], sm[:L, 1:2], AF.Exp, [smB], [smB], scale=-0.5)
                        A(xnb[:L, :], xt[:L, :], AF.Identity, [xb, smB], [xnbB], scale=sm[:L, 3:4])
                        yield
                        pt = bk("F" if i % 2 else "B")
                        for j in range(8):
                            tr(pt[1][:, j * 128:j * 128 + L], xnb[:L, j * 128:(j + 1) * 128], identb[:L, :L], [xnbB, cbB], [pt[2]], j == 7)
                        xn, xnB = XN[i % 3]
                        tt(xn[:, :, :L], ap(pt[1], 0, 128, 0, [[128, 8], [1, L]]), ppbc(P_NG, 8, L), ALU.mult, [pt[2], ppB], [xnB])
                        rel(pt)
                        dma_out(xnscr[i, :, :, :L], xn[:, :, :L], xnB, W=[XNS[i]])
                        yield

                    p0_load(0)
                    p0_load(1)
                    for i in range(0, 18, 2):
                        run([p0(i), p0(i + 1)])
                    k.barrier([pe, act, dve])

                def tail_prefetch(i, mode):
                    t0, L = TILES[i]
                    if mode != "first":
                        mt_, mb_ = XT[i % 2]
                        dma_in(mt_[:L, :], mscr[t0:t0 + L, :], mb_, R=[MS[i]])
                    if mode == "last":
                        xt_, xb_ = XT[2 + i % 2]
                        dma_in(xt_[:L, :], xscr[t0:t0 + L, :], xb_, R=[XS[i]])

                def xn_load(i):
                    xn, xnB = XN[i % 3]
                    L = TILES[i][1]
                    dma_in(xn[:, :, :L], xnscr[i, :, :, :L], xnB, R=[XNS[i]])

                def tail(i, ph, mode, yT, yTB, xn, xnB, T):
                    t0, L = TILES[i]
                    sg, sgB = T["sg"]
                    mt_, mb_ = XT[i % 2]
                    zb = [bk(), bk()]
                    for n in range(2):
                        wt_, wB = Wt(l, ph, f"br{n}")
                        for kk in range(8):
                            mm(zb[n][0][:L, :], yT[:, kk, :L], wt_[:, kk, :], kk == 0, kk == 7, [yTB, wB], [zb[n][2]], kk == 7)
                        yield
                    gb = [bk(), bk()]
                    for n in range(2):
                        wt_, wB = Wt(l, ph, f"g{n}")
                        for kk in range(8):
                            mm(gb[n][0][:L, :], xn[:, kk, :L], wt_[:, kk, :], kk == 0, kk == 7, [xnB, wB], [gb[n][2]], kk == 7)
                        yield
                    for n in range(2):
                        A(sg[:L, n * 512:(n + 1) * 512], gb[n][0][:L, :], AF.Tanh, [gb[n][2]], [sgB], scale=0.5)
                    rel(*gb)
                    if mode == "first":
                        for n in range(2):
                            stt(mt_[:L, n * 512:(n + 1) * 512], sg[:L, n * 512:(n + 1) * 512], 1.0, zb[n][0][:L, :], ALU.add, ALU.mult, [zb[n][2], sgB], [mb_])
                        rel(*zb)
                        dma_out(mscr[t0:t0 + L, :], mt_[:L, :], mb_, W=[MS[i]])
                        return
                    for n in range(2):
                        stt(sg[:L, n * 512:(n + 1) * 512], sg[:L, n * 512:(n + 1) * 512], 1.0, zb[n][0][:L, :], ALU.add, ALU.mult, [zb[n][2], sgB], [sgB])
                    rel(*zb)
                    if mode == "mid":
                        tt(mt_[:L, :], mt_[:L, :], sg[:L, :], ALU.add, [mb_, sgB], [mb_])
                        dma_out(mscr[t0:t0 + L, :], mt_[:L, :], mb_, W=[MS[i]])
                        return
                    mrg, mrgB = T["mrg"]
                    mT, mTB = T["mT"]
                    tt(mrg[:L, :], mt_[:L, :], sg[:L, :], ALU.add, [mb_, sgB], [mrgB])
                    pt = bk()
                    for j in range(8):
                        tr(pt[1][:, j * 128:j * 128 + L], mrg[:L, j * 128:(j + 1) * 128], identb[:L, :L], [mrgB, cbB], [pt[2]], j == 7)
                    cp(mT[:, :, :L], ap(pt[1], 0, 128, 0, [[128, 8], [1, L]]), [pt[2]], [mTB])
                    rel(pt)
                    yield
                    ob = [bk(), bk()]
                    for n in range(2):
                        wt_, wB = Wt(l, ph, f"wo{n}")
                        for kk in range(8):
                            mm(ob[n][0][:L, :], mT[:, kk, :L], wt_[:, kk, :], kk == 0, kk == 7, [mTB, wB], [ob[n][2]], kk == 7)
                        yield
                    xt_, xb_ = XT[2 + i % 2]
                    for n in range(2):
                        stt(xt_[:L, n * 512:(n + 1) * 512], ob[n][0][:L, :], 0.5, xt_[:L, n * 512:(n + 1) * 512], ALU.mult, ALU.add, [xb_, ob[n][2]], [xb_])
                    rel(*ob)
                    if not last_layer:
                        dma_out(xscr[t0:t0 + L, :], xt_[:L, :], xb_, W=[XS[i]])
                        return
                    sm, smB = T["fsm"]
                    A(mrg[:L, :], xt_[:L, :], AF.Square, [xb_], [mrgB, smB], accum=sm[:L, 0:1])
                    A(sm[:L, 1:2], sm[:L, 0:1], AF.Ln, [smB, cfB], [smB], scale=1.0 / 1024, bias=cf[:L, C_EPS:C_EPS + 1])
                    A(sm[:L, 3:4], sm[:L, 1:2], AF.Exp, [smB], [smB], scale=-0.5)
                    stt(mt_[:L, :], xt_[:L, :], sm[:L, 3:4], fg[:L, :], ALU.mult, ALU.mult, [xb_, smB, fgB], [mb_])
                    if i == 0:
                        dma_out(y_p[0:112, :], mt_[16:128, :], mb_, final=True)
                    elif i < 16:
                        dma_out(y_p[128 * i - 16:128 * i + 112, :], mt_[:, :], mb_, final=True)
                    elif i == 16:
                        dma_out(y_p[2032:2048, :], mt_[0:16, :], mb_, final=True)
                    else:
                        dma_out(y_s[:, :], mt_[:, :], mb_, final=True)

                def repl4(val_ap, ncol, dstR, dstRB, out_t, out_B):
                    tt(ap(dstR, 0, 4, 0, [[4, ncol], [1, 4]]), ap(val_ap[0], 0, 4, val_ap[1], [[1, ncol], [0, 4]]),
                       ap(cf, 0, 4, C_IDF, [[0, ncol], [1, 4]]), ALU.mult, [val_ap[2], cfB], [dstRB])
                    b = bk()
                    mm(b[0][:, 0:ncol * 4], cf[0:4, C_ONES:C_ONES + 128], dstR[0:4, 0:ncol * 4], True, True, [cfB, dstRB], [b[2]], True)
                    cp(out_t[:, 0:ncol * 4], b[0][:, 0:ncol * 4], [b[2]], [out_B])
                    rel(b)

                def drive(front, back, ph, front_only, nxt_phase):
                    xn_load(0)
                    run([front(0)])
                    for i in range(18):
                        gs = [back(i)]
                        if i + 1 < 18:
                            gs.append(front(i + 1))
                        run(gs)
                        if i + 1 == 17:
                            release_weights(l, ph, front_only)
                            if nxt_phase is not None:
                                load_weights(*nxt_phase)
                    k.barrier([pe, act, dve])

                def phaseA(nxt_phase):
                    with ExitStack() as pst:
                        P = lambda n, s, dt=F32, sem=False: sbt(f"a_{n}_{l}", s, dt, sem=sem, stack=pst)
                        P2 = lambda n, s, dt=F32: [P(f"{n}{i_}", s, dt) for i_ in range(2)]
                        qT2 = P2("qT", [128, 4, 128], BF16); kT, kTB = P("kT", [128, 4, 128], BF16)
                        ktok2 = P2("ktok", [128, 512], BF16)
                        ve2 = P2("ve", [128, 4, 260], BF16); pmT2 = P2("pmT", [128, 4, 128], BF16)
                        og2 = P2("og", [128, 1024]); sgo, sgoB = P("sgo", [128, 1024]); sg, sgB = P("sg", [128, 1024])
                        yml, ymlB = P("yml", [128, 1024], BF16); yT, yTB = P("yT", [128, 8, 128], BF16)
                        CN, CNB = P("CN", [128, 4, 260]); CNb, CNbB = P("CNb", [128, 4, 260], BF16)
                        jk, jkB = P("jk", [128, 256], BF16)
                        gif, gifB = P("gif", [128, 8]); gx, gxB = P("gx", [128, 8]); gnlf, gnlfB = P("gnlf", [128, 4])
                        ga2 = P2("ga", [128, 4]); ge, geB = P("ge", [128, 4]); gfl2 = P2("gfl", [128, 4])
                        gebl2 = P2("gebl", [128, 4]); gnbl2 = P2("gnbl", [128, 4])
                        gden, gdenB = P("gden", [128, 8]); gss, gssB = P("gss", [128, 4]); gt, gtB = P("gt", [128, 12]); gsc, gscB = P("gsc", [128, 4])
                        mst, mstB = P("mst", [128, 96], F32, sem=True); rr, rrB = P("rr", [128, 64]); emr, emrB = P("emr", [128, 64])
                        co = [P(f"co{i_}", [128, 4, 260], F32, sem=True) for i_ in range(2)]
                        T = {"sg": (sg, sgB)}
                        DQS = float(128 ** -0.5)
                        csel, cselB = P("csel", [128, 2048], BF16, sem=True)
                        dma_in(csel[:], cb_d[:, B_COLSEL:B_COLSEL + 2048], cselB)
                        mset(CN[:], 0.0, [CNB]); mset(CNb[:], 0.0, [CNbB]); mset(mst[:], 0.0, [mstB])

                        def front(i):
                            t0, L = TILES[i]
                            smp = (i == SAMPLE)
                            par = i % 2
                            xn, xnB = XN[i % 3]
                            if i + 1 < 18:
                                xn_load(i + 1)
                            qT, qTB = qT2[par]; ktok, ktokB = ktok2[par]; ve, veB = ve2[par]; pmT, pmTB = pmT2[par]; og, ogB = og2[par]
                            ga, gaB = ga2[par]; gfl, gflB = gfl2[par]; gebl, geblB = gebl2[par]; gnbl, gnblB = gnbl2[par]
                            C_TRI = C_TRIS if smp else C_TRIP
                            C_ONE = C_SAMES if smp else C_ONES
                            for (wn, dst, dstB, scl) in (("q", qT, qTB, 1.0), ("k", kT, kTB, DQS)):
                                wt_, wB = Wt(l, "A", wn)
                                b = bk("F")
                                for h in range(4):
                                    for kk in range(8):
                                        mm(b[0][:, h * 128:h * 128 + L], wt_[:, kk, h * 128:(h + 1) * 128], xn[:, kk, :L], kk == 0, kk == 7,
                                           [wB, xnB], [b[2]], kk == 7 and h == 3)
                                A(dst[:, :, :L], ap(b[0], 0, 128, 0, [[128, 4], [1, L]]), AF.Identity, [b[2]], [dstB], scale=scl)
                                rel(b)
                                yield
                            wt_, wB = Wt(l, "A", "k")
                            b = bk("F")
                            for kk in range(8):
                                mm(b[0][:L, :], xn[:, kk, :L], wt_[:, kk, :], kk == 0, kk == 7, [xnB, wB], [b[2]], kk == 7)
                            A(ktok[:L, :], b[0][:L, :], AF.Identity, [b[2]], [ktokB], scale=DQS)
                            rel(b)
                            bg = bk("F")
                            for kk in range(8):
                                mm(bg[0][:L, 0:8], xn[:, kk, :L], wif[:, kk, :], kk == 0, kk == 7, [xnB, wifB], [bg[2]], kk == 7)
                            cp(gif[:L, :], bg[0][:L, 0:8], [bg[2]], [gifB])
                            tt(gx[:L, 0:4], gif[:L, 4:8], prc(R_FB, 4, L), ALU.add, [gifB, prB], [gxB])
                            A(gx[:L, 4:8], gx[:L, 0:4], AF.Exp, [gxB], [gxB], scale=-1.0)
                            A(gnlf[:L, :], gx[:L, 4:8], AF.Ln, [gxB, cfB], [gnlfB], bias=cf[:L, C_ONES:C_ONES + 1])
                            yield
                            bv = [bk("F"), bk("F")]
                            for n in range(2):
                                wt_, wB = Wt(l, "A", f"v{n}")
                                for kk in range(8):
                                    mm(bv[n][0][:L, :], xn[:, kk, :L], wt_[:, kk, :], kk == 0, kk == 7, [xnB, wB], [bv[n][2]], kk == 7)
                            yield
                            mm(bg[0][:L, 8:12], cfm(C_TRI, L), gnlf[:L, :], True, True, [cfB, gnlfB], [bg[2]], False)
                            mm(bg[0][:, 12:16], cf[:L, C_ONE:C_ONE + 128], gnlf[:L, :], True, True, [cfB, gnlfB], [bg[2]], True)
                            tt(ga[:L, :], gif[:L, 0:4], bg[0][:L, 8:12], ALU.add, [gifB, bg[2]], [gaB])
                            A(ge[:L, :], ga[:L, :], AF.Exp, [gaB], [geB])
                            A(gfl[:L, :], bg[0][:L, 8:12], AF.Exp, [bg[2]], [gflB])
                            A(gebl[:, :], bg[0][:, 12:16], AF.Exp, [bg[2]], [geblB], scale=-1.0)
                            cp(gnbl[:, :], bg[0][:, 12:16], [bg[2]], [gnblB])
                            rel(bg)
                            yield
                            for n in range(2):
                                tt(ve[:L, 2 * n:2 * n + 2, 0:256], ap(bv[n][0], 0, L, 0, [[256, 2], [1, 256]]),
                                   ap(ge, 0, L, 2 * n, [[1, 2], [0, 256]]), ALU.mult, [bv[n][2], geB], [veB])
                            rel(*bv)
                            cp(ap(ve, 0, L, 256, [[260, 4], [1, 1]]), ap(ge, 0, L, 0, [[1, 4], [1, 1]]), [geB], [veB])
                            bs = bk("F")
                            for h in range(4):
                                mm(bs[0][:L, h * 128:h * 128 + L], kT[:, h, :L], qT[:, h, :L], True, True, [kTB, qTB], [bs[2]], h == 3)
                            tt(pmT[:L, :, :L], ap(bs[0], 0, L, 0, [[128, 4], [1, L]]), ap(cf, 0, L, C_TRI, [[0, 4], [1, L]]), ALU.mult, [bs[2], cfB], [pmTB])
                            rel(bs)
                            yield
                            for (wn, dst, dstB, fn, fsc) in (("o", sgo, sgoB, AF.Tanh, 0.5), ("z", og, ogB, AF.Silu, 1.0)):
                                bb = [bk("F"), bk("F")]
                                for n in range(2):
                                    wt_, wB = Wt(l, "A", f"{wn}{n}")
                                    for kk in range(8):
                                        mm(bb[n][0][:L, :], xn[:, kk, :L], wt_[:, kk, :], kk == 0, kk == 7, [xnB, wB], [bb[n][2]], kk == 7)
                                    yield
                                for n in range(2):
                                    A(dst[:L, n * 512:(n + 1) * 512], bb[n][0][:L, :], fn, [bb[n][2]], [dstB], scale=fsc)
                                rel(*bb)
                            stt(og[:L, :], sgo[:L, :], 1.0, og[:L, :], ALU.add, ALU.mult, [ogB, sgoB], [ogB])
                            yield

                        def back(i):
                            t0, L = TILES[i]
                            smp = (i == SAMPLE)
                            par = i % 2
                            xn, xnB = XN[i % 3]
                            qT, qTB = qT2[par]; ktok, ktokB = ktok2[par]; ve, veB = ve2[par]; pmT, pmTB = pmT2[par]; og, ogB = og2[par]
                            ga, gaB = ga2[par]; gfl, gflB = gfl2[par]; gebl, geblB = gebl2[par]; gnbl, gnblB = gnbl2[par]
                            if not smp:
                                bn = [bk() for _ in range(4)]
                                for h in range(4):
                                    mm(bn[h][0][:L, 0:257], pmT[:L, h, :L], ve[:L, h, 0:257], True, False, [pmTB, veB], [bn[h][2]], False)
                                    mm(bn[h][0][:L, 0:257], qT[:, h, :L], CNb[:, h, 0:257], False, True, [qTB, CNbB], [bn[h][2]], True)
                                yield
                                for h in range(4):
                                    b = bk()
                                    mm(b[0][:, 0:257], ktok[:L, h * 128:(h + 1) * 128], ve[:L, h, 0:257], True, True, [ktokB, veB], [b[2]], True)
                                    tt(CN[:, h, 0:257], b[0][:, 0:257], CN[:, h, 0:257], ALU.add, [b[2], CNB], [CNB])
                                    ts(CN[:, h, 0:257], CN[:, h, 0:257], gebl[:, h:h + 1], None, ALU.mult, None, [CNB, geblB], [CNB])
                                    rel(b)
                                    if h % 2 == 1:
                                        yield
                                A(CNb[:, :, 0:257], CN[:, :, 0:257], AF.Copy, [CNB], [CNbB])
                                bm = bk()
                                tr(bm[0][0:4, 0:L], ga[:L, 0:4], identf[:L, :L], [gaB, cfB], [bm[2]], False)
                                tr(bm[0][0:4, 128:256], gnbl[:, 0:4], identf[:, :], [gnblB, cfB], [bm[2]], True)
                                k.op(dve, lambda e: e.reduce_max(out=mst[0:4, 1:2], in_=bm[0][0:4, 0:L], axis=AX.X), [bm[2]], [mstB])
                                tt(mst[0:4, 2:3], mst[0:4, 0:1], mst[0:4, 1:2], ALU.max, [mstB], [mstB])
                                tt(mst[0:4, 0:1], mst[0:4, 2:3], bm[0][0:4, 128:129], ALU.subtract, [mstB, bm[2]], [mstB])
                                rel(bm)
                                yield
                            else:
                                n0t, n0tB = XT[3]
                                dma_in(n0t[0:64, 0:128], st_n[l, :, :], n0tB)
                                dma_in(ap(mst, 0, 4, 16, [[1, 16]]), dap(st_m, l * 64, [[1, 4], [4, 16]]), mstB, allow_slow_non_contiguous=True)
                                b = bk()
                                tr(b[0][:, 0:64], n0t[0:64, 0:128], identf[0:64, 0:64], [n0tB, cfB], [b[2]], True)
                                n0T, n0TB = P("n0T", [128, 64])
                                cp(n0T[:, :], b[0][:, 0:64], [b[2]], [n0TB])
                                rel(b)
                                nout, noutB = P("nout", [128, 64])
                                A(mst[0:4, 32:48], mst[0:4, 16:32], AF.Exp, [mstB], [mstB])
                                rr5, rr5B = P("rr5", [128, 512])
                                tt(ap(rr5, 0, 4, 0, [[128, 4], [8, 16], [1, 8]]), ap(cf, 0, 4, C_IDF, [[1, 4], [0, 16], [0, 8]]), ap(mst, 0, 4, 32, [[0, 4], [1, 16], [0, 8]]),
                                   ALU.mult, [cfB, mstB], [rr5B])
                                b = bk()
                                mm(b[0][:, 0:512], cf[0:4, C_ONES:C_ONES + 128], rr5[0:4, 0:512], True, True, [cfB, rr5B], [b[2]], True)
                                qs, qsB = P("qs", [128, 4, 128], BF16)
                                tt(qs[:].rearrange("p a b -> p (a b)"), qT[:].rearrange("p a b -> p (a b)"), b[0][:, 0:512], ALU.mult, [qTB, b[2]], [qsB])
                                rel(b)
                                yield
                                bm = bk()
                                tr(bm[0][0:4, 0:128], ga[:, 0:4], identf[:, :], [gaB, cfB], [bm[2]], False)
                                tr(bm[0][0:4, 128:256], gnbl[:, 0:4], identf[:, :], [gnblB, cfB], [bm[2]], True)
                                k.op(dve, lambda e: e.tensor_reduce(out=mst[0:4, 64:80], in_=ap(bm[0], 0, 4, 0, [[8, 16], [1, 8]]), axis=AX.X, op=ALU.max), [bm[2]], [mstB])
                                tt(mst[0:4, 64:80], mst[0:4, 64:80], mst[0:4, 16:32], ALU.max, [mstB], [mstB])
                                nblv = ap(bm[0], 0, 4, 128, [[8, 16]])
                                tt(mst[0:4, 48:64], mst[0:4, 64:80], nblv, ALU.subtract, [mstB, bm[2]], [mstB])
                                tt(mst[0:4, 64:80], mst[0:4, 16:32], mst[0:4, 48:64], ALU.subtract, [mstB], [mstB])
                                tt(mst[0:4, 64:80], mst[0:4, 64:80], nblv, ALU.subtract, [mstB, bm[2]], [mstB])
                                A(mst[0:4, 64:80], mst[0:4, 64:80], AF.Exp, [mstB], [mstB])
                                w1r, w1rB = P("w1r", [128, 64]); w2r, w2rB = P("w2r", [128, 64])
                                repl4((mst, 64, mstB), 16, rr, rrB, w1r, w1rB)
                                ts(mst[0:4, 80:96], mst[0:4, 48:64], -1.0, None, ALU.mult, None, [mstB], [mstB])
                                tt(mst[0:4, 80:96], mst[0:4, 80:96], nblv, ALU.subtract, [mstB, bm[2]], [mstB])
                                A(mst[0:4, 80:96], mst[0:4, 80:96], AF.Exp, [mstB], [mstB])
                                repl4((mst, 80, mstB), 16, rr, rrB, w2r, w2rB)
                                rel(bm)
                                dma_out(dap(s_m, l * 64, [[1, 4], [4, 16]]), ap(mst, 0, 4, 48, [[1, 16]]), mstB, final=True, allow_slow_non_contiguous=True)
                                yield
                                bn = [bk() for _ in range(4)]
                                for h in range(4):
                                    mm(bn[h][0][:, 0:257], pmT[:, h, :], ve[:, h, 0:257], True, False, [pmTB, veB], [bn[h][2]], False)
                                qz = [P(f"qz{i_}", [128, 4, 128], BF16) for i_ in range(2)]
                                vej = [P(f"vej{i_}", [128, 4, 260], BF16) for i_ in range(2)]
                                CNb2 = [(CNb, CNbB), P("CNb2", [128, 4, 260], BF16)]
                                tmpc, tmpcB = P("tmpc", [128, 4, 256])
                                ndl, ndlB = P("ndl", [128, 64])
                                stg = [XT[0], XT[2], XT[3]]

                                def ld(j):
                                    cs_, csB = stg[j % 3]
                                    dma_in(ap(cs_, 0, 128, 0, [[256, 4], [1, 256]]), dap(st_c, ((l * 16 + j) * 4) * 128 * 256, [[256, 128], [128 * 256, 4], [1, 256]]), csB)

                                ld(0)
                                ld(1)
                                for j in range(16):
                                    if j + 2 < 16:
                                        ld(j + 2)
                                    cs_, csB = stg[j % 3]
                                    cn, cnB = CNb2[j % 2]
                                    A(cn[:, :, 0:256], ap(cs_, 0, 128, 0, [[256, 4], [1, 256]]), AF.Copy, [csB], [cnB])
                                    A(ap(cn, 0, 128, 256, [[260, 4], [1, 1]]), ap(n0T, 0, 128, j * 4, [[1, 4], [1, 1]]), AF.Copy, [n0TB], [cnB])
                                    qz_, qzB = qz[j % 2]
                                    tt(qz_[:, :, :], qs[:, :, :], ap(csel, 0, 128, j * 128, [[0, 4], [1, 128]]), ALU.mult, [qsB, cselB], [qzB])
                                    for h in range(4):
                                        mm(bn[h][0][:, 0:257], qz_[:, h, :], cn[:, h, 0:257], False, j == 15, [qzB, cnB], [bn[h][2]], True)
                                    vj, vjB = vej[j % 2]
                                    A(vj[:].rearrange("p a b -> p (a b)"), ve[:].rearrange("p a b -> p (a b)"), AF.Identity, [veB, cfB], [vjB], scale=cf[:, C_ROWSEL + j:C_ROWSEL + j + 1])
                                    co_, coB = co[j % 2]
                                    for h in range(4):
                                        b = bk("F")
                                        col = j * 4 + h
                                        mm(b[0][:, 0:257], ktok[:, h * 128:(h + 1) * 128], vj[:, h, 0:257], True, True, [ktokB, vjB], [b[2]], True)
                                        A(tmpc[:, h, :], b[0][:, 0:256], AF.Identity, [b[2], w2rB], [tmpcB], scale=w2r[:, col:col + 1])
                                        A(ndl[:, col:col + 1], b[0][:, 256:257], AF.Copy, [b[2]], [ndlB])
                                        rel(b)
                                        stt(co_[:, h, 0:256], ap(cs_, 0, 128, h * 256, [[1, 256]]), w1r[:, col:col + 1], tmpc[:, h, :], ALU.mult, ALU.add, [csB, w1rB, tmpcB], [coB])
                                    dma_out(dap(s_c, ((l * 16 + j) * 4) * 128 * 256, [[256, 128], [128 * 256, 4], [1, 256]]), co_[:, :, 0:256], coB, final=True)
                                    yield
                                tt(nout[:, :], ndl[:, :], w2r[:, :], ALU.mult, [ndlB, w2rB], [noutB])
                                tt(ndl[:, :], n0T[:, :], w1r[:, :], ALU.mult, [n0TB, w1rB], [ndlB])
                                tt(nout[:, :], nout[:, :], ndl[:, :], ALU.add, [noutB, ndlB], [noutB])
                                b = bk()
                                tr(b[0][0:64, 0:128], nout[:, 0:64], identf[:, :], [noutB, cfB], [b[2]], True)
                                no2, no2B = P("no2", [64, 128], F32, sem=True)
                                cp(no2[:, :], b[0][0:64, 0:128], [b[2]], [no2B])
                                rel(b)
                                dma_out(s_n[l, :, :], no2[:, :], no2B, final=True)
                            for h in range(4):
                                cp(gden[:L, h:h + 1], bn[h][0][:L, 256:257], [bn[h][2]], [gdenB])
                            A(gden[:L, 4:8], gden[:L, 0:4], AF.Abs, [gdenB], [gdenB])
                            tt(gden[:L, 4:8], gden[:L, 4:8], gfl[:L, :], ALU.max, [gdenB, gflB], [gdenB])
                            rcp(gt[:L, 0:4], gden[:L, 4:8], [gdenB], [gtB])
                            for h in range(4):
                                A(jk[:L, :], bn[h][0][:L, 0:256], AF.Square, [bn[h][2]], [jkB, gssB], accum=gss[:L, h:h + 1])
                            yield
                            tt(gt[:L, 4:8], gt[:L, 0:4], gt[:L, 0:4], ALU.mult, [gtB], [gtB])
                            tt(gt[:L, 4:8], gt[:L, 4:8], gss[:L, :], ALU.mult, [gtB, gssB], [gtB])
                            A(gt[:L, 4:8], gt[:L, 4:8], AF.Ln, [gtB, cfB], [gtB], scale=1.0 / 256, bias=cf[:L, C_EPS:C_EPS + 1])
                            A(gt[:L, 8:12], gt[:L, 4:8], AF.Exp, [gtB], [gtB], scale=-0.5)
                            stt(gsc[:L, :], gt[:L, 0:4], 0.5, gt[:L, 8:12], ALU.mult, ALU.mult, [gtB], [gscB])
                            for h in range(4):
                                stt(yml[:L, h * 256:(h + 1) * 256], bn[h][0][:L, 0:256], gsc[:L, h:h + 1], og[:L, h * 256:(h + 1) * 256], ALU.mult, ALU.mult,
                                    [bn[h][2], gscB, ogB], [ymlB])
                            rel(*bn)
                            yield
                            pt = bk()
                            for j in range(8):
                                tr(pt[1][:, j * 128:j * 128 + L], yml[:L, j * 128:(j + 1) * 128], identb[:L, :L], [ymlB, cbB], [pt[2]], j == 7)
                            tt(yT[:, :, :L], ap(pt[1], 0, 128, 0, [[128, 8], [1, L]]), ppbc(P_MG, 8, L), ALU.mult, [pt[2], ppB], [yTB])
                            rel(pt)
                            yield
                            yield from tail(i, "A", "first", yT, yTB, xn, xnB, T)
                            if i == 16:
                                A(mst[0:4, 3:4], mst[0:4, 0:1], AF.Exp, [mstB], [mstB], scale=-1.0)
                                repl4((mst, 3, mstB), 1, rr, rrB, emr, emrB)
                                co_, coB = co[0]
                                for h in range(4):
                                    ts(co_[:, h, 0:257], CN[:, h, 0:257], emr[:, h:h + 1], None, ALU.mult, None, [CNB, emrB], [coB])
                                dma_out(dap(p_c, l * 4 * 128 * 256, [[256, 128], [128 * 256, 4], [1, 256]]), co_[:, :, 0:256], coB, final=True)
                                dma_out(dap(p_n, l * 512, [[1, 128], [128, 4], [1, 1]]), ap(co_, 0, 128, 256, [[260, 4], [1, 1]]), coB, final=True, allow_slow_non_contiguous=True)
                                dma_out(dap(p_m, l * 4, [[1, 4], [1, 1]]), mst[0:4, 0:1], mstB, final=True)

                        drive(front, back, "A", ["q", "k", "v0", "v1", "o0", "o1", "z0", "z1"], nxt_phase)

                def phaseB(nxt_phase):
                    with ExitStack() as pst:
                        P = lambda n, s, dt=F32, sem=False: sbt(f"b_{n}_{l}", s, dt, sem=sem, stack=pst)
                        xf, xfB = P("xf", [128, 1408], BF16)
                        wbd, wbdB = P("wbd", [128, 2, 8, 128], BF16, sem=True)
                        for a_ in range(2):
                            k.dma(pool, wbd[:, a_, :, :], dap(wbd_d, (l * 2 + a_) * 8 * 128 * 128, [[128, 128], [128 * 128, 8], [1, 128]]), W=[wbdB], sem=wbdB.sem)
                        dgl, dglB = P("dgl", [128, 32, 128], BF16)
                        tt(dgl[:, :, :], ap(cb, 0, 128, 0, [[0, 32], [1, 128]]), ap(ppt, 0, 128, l * NPP + P_LCW, [[1, 32], [0, 128]]), ALU.mult, [cbB, ppB], [dglB])
                        xc, xcB = P("xc", [128, 8, 128]); xcb, xcbB = P("xcb", [128, 8, 128], BF16)
                        Rr2 = [P(f"R{i_}", [128, 8, 128]) for i_ in range(2)]; Ii2 = [P(f"I{i_}", [128, 8, 128]) for i_ in range(2)]
                        Sz2 = [P(f"Sz{i_}", [128, 8, 128]) for i_ in range(2)]
                        Tt, TtB = P("T", [128, 8, 128]); Hh, HhB = P("H", [128, 8, 128])
                        yT2 = [P(f"yT{i_}", [128, 8, 128], BF16) for i_ in range(2)]
                        sg, sgB = P("sg", [128, 1024])
                        hc, hcB = P("hc", [128, 8]); cA, cAB = P("cA", [128, 8]); tq, tqB = P("tq", [128, 8, 16])
                        xcBk = [Buf(f"xc{kk}_{l}") for kk in range(8)]
                        ctmp = [P(f"ctmp{i_}", [128, 128]) for i_ in range(2)]
                        ho, hoB = P("ho", [128, 1024], F32, sem=True)
                        T = {"sg": (sg, sgB)}
                        A(cA[:, :], ppc(P_LAM, 8), AF.Exp, [ppB], [cAB], scale=-1.0)
                        A(cA[:, :], cA[:, :], AF.Ln, [cAB, cfB], [cAB], bias=cf[:, C_ONES:C_ONES + 1])
                        ts(cA[:, :], cA[:, :], -4.0, None, ALU.mult, None, [cAB], [cAB])
                        hb, hbB = P("hb", [128, 16])
                        ts(hb[:, :], ppc(P_LBA, 16), 0.5, None, ALU.mult, None, [ppB], [hbB])
                        mset(xf[:], 0.0, [xfB]); mset(hc[:], 0.0, [hcB])

                        def front(i):
                            t0, L = TILES[i]
                            smp = (i == SAMPLE)
                            xn, xnB = XN[i % 3]
                            Rr, RrB = Rr2[i % 2]; Ii, IiB = Ii2[i % 2]; Sz, SzB = Sz2[i % 2]
                            if i + 1 < 18:
                                xn_load(i + 1)
                            if smp:
                                xwin = lambda kk, j: ap(xf, 0, 128, kk * 176 + j, [[11, 16], [1, 8]])
                                xcv = lambda kk: ap(xc, 0, 128, kk * 128, [[8, 16], [1, 8]])
                                s48, s48B = XT[3]
                                dma_in(s48[0:48, :], st_cl[l, :, :], s48B)
                                b = bk("F")
                                for kk in range(8):
                                    tr(b[0][:, kk * 48:(kk + 1) * 48], s48[0:48, kk * 128:(kk + 1) * 128], identf[0:48, 0:48], [s48B, cfB], [b[2]], kk == 7)
                                cp(ap(xf, 0, 128, 0, [[176, 8], [11, 16], [1, 3]]), ap(b[0], 0, 128, 0, [[48, 8], [3, 16], [1, 3]]), [b[2]], [xfB])
                                rel(b)
                            else:
                                xwin = lambda kk, j: ap(xf, 0, 128, kk * 131 + j, [[1, L]])
                                xcv = lambda kk: xc[:, kk, :L]
                            for n in range(2):
                                wt_, wB = Wt(l, "B", f"lx{n}")
                                b = bk("F")
                                for m in range(4):
                                    for kk in range(8):
                                        mm(b[0][:, m * 128:m * 128 + L], wt_[:, kk, m * 128:(m + 1) * 128], xn[:, kk, :L], kk == 0, kk == 7, [wB, xnB], [b[2]], kk == 7 and m == 3)
                                if smp:
                                    A(ap(xf, 0, 128, n * 4 * 176 + 3, [[176, 4], [11, 16], [1, 8]]), ap(b[0], 0, 128, 0, [[128, 4], [8, 16], [1, 8]]), AF.Copy, [b[2]], [xfB])
                                else:
                                    A(ap(xf, 0, 128, n * 4 * 131 + 3, [[131, 4], [1, L]]), ap(b[0], 0, 128, 0, [[128, 4], [1, L]]), AF.Copy, [b[2]], [xfB])
                                rel(b)
                                yield
                            if smp or i == 16:
                                lt, ltB = XT[3]
                                bt_ = [bk("F"), bk("F")]
                                for n in range(2):
                                    wt_, wB = Wt(l, "B", f"lx{n}")
                                    for kk in range(8):
                                        mm(bt_[n][0][:L, :], xn[:, kk, :L], wt_[:, kk, :], kk == 0, kk == 7, [xnB, wB], [bt_[n][2]], kk == 7)
                                for n in range(2):
                                    A(lt[:L, n * 512:(n + 1) * 512], bt_[n][0][:L, :], AF.Copy, [bt_[n][2]], [ltB])
                                rel(*bt_)
                                if smp:
                                    for r in range(3):
                                        dma_out(dap(s_cl, l * 16 * 3 * 1024 + r * 1024, [[3 * 1024, 16], [1, 1024]]), bass.AP(lt, (5 + r) * 1024, [[8 * 1024, 16], [1, 1024]]), ltB, final=True)
                                else:
                                    dma_out(p_cl[l, :, :], lt[13:16, :], ltB, final=True)
                                yield
                            for g4 in range(2):
                                b = bk("F")
                                ov = (lambda q: ap(b[0], 0, 128, q * 128, [[8, 16], [1, 8]])) if smp else (lambda q: b[0][:, q * 128:q * 128 + L])
                                for q in range(4):
                                    kk = 4 * g4 + q
                                    for j in range(4):
                                        mm(ov(q), dgl[:, kk * 4 + j, :], xwin(kk, j), j == 0, j == 3, [dglB, xfB], [b[2]], j == 3 and q == 3)
                                for q in range(4):
                                    kk = 4 * g4 + q
                                    A(xcv(kk), ov(q), AF.Identity, [b[2], ppB], [xcBk[kk]], bias=ppc(P_LCB + kk))
                                rel(b)
                                yield
                            if not smp:
                                cp(ap(xf, 0, 128, 0, [[131, 8], [1, 3]]), ap(xf, 0, 128, L, [[131, 8], [1, 3]]), [xfB], [xfB])
                            A(xcb[:, :, :L], xc[:, :, :L], AF.Copy, xcBk, [xcbB])
                            for (a_, dst, dstB, bcol) in ((0, Rr, RrB, P_LBA), (1, Ii, IiB, P_LBX)):
                                bb = [bk("F"), bk("F")]
                                for kk in range(8):
                                    mm(bb[kk // 4][0][:, (kk % 4) * 128:(kk % 4) * 128 + L], wbd[:, a_, kk, :], xcb[:, kk, :L], True, True, [wbdB, xcbB], [bb[kk // 4][2]], kk % 4 == 3)
                                for kk in range(8):
                                    A(dst[:, kk, :L], bb[kk // 4][0][:, (kk % 4) * 128:(kk % 4) * 128 + L], AF.Tanh, [bb[kk // 4][2], hbB], [dstB], scale=0.5, bias=hb[:, bcol - P_LBA + kk:bcol - P_LBA + kk + 1])
                                rel(*bb)
                                yield
                            bz_ = [bk("F"), bk("F")]
                            for n in range(2):
                                wt_, wB = Wt(l, "B", f"lz{n}")
                                for m in range(4):
                                    for kk in range(8):
                                        mm(bz_[n][0][:, m * 128:m * 128 + L], wt_[:, kk, m * 128:(m + 1) * 128], xn[:, kk, :L], kk == 0, kk == 7, [wB, xnB], [bz_[n][2]], kk == 7 and m == 3)
                                yield
                            for n in range(2):
                                A(Sz[:, 4 * n:4 * n + 4, :L], ap(bz_[n][0], 0, 128, 0, [[128, 4], [1, L]]), AF.Silu, [bz_[n][2]], [SzB])
                            rel(*bz_)
                            stt(Rr[:, :, :L], Rr[:, :, :L], 1.0, ap(cA, 0, 128, 0, [[1, 8], [0, L]]), ALU.add, ALU.mult, [RrB, cAB], [RrB])
                            A(Rr[:, :, :L], Rr[:, :, :L], AF.Exp, [RrB], [RrB])
                            tt(Tt[:, :, :L], Rr[:, :, :L], Rr[:, :, :L], ALU.mult, [RrB], [TtB])
                            A(Tt[:, :, :L], Tt[:, :, :L], AF.Ln, [TtB, cfB], [TtB], scale=-1.0, bias=cf[:, C_ONES:C_ONES + 1])
                            A(Tt[:, :, :L], Tt[:, :, :L], AF.Exp, [TtB], [TtB], scale=0.5)
                            yield
                            stt(Ii[:, :, :L], Ii[:, :, :L], 1.0, xc[:, :, :L], ALU.add, ALU.mult, [IiB] + xcBk, [IiB])
                            stt(Ii[:, :, :L], Ii[:, :, :L], 0.5, Tt[:, :, :L], ALU.mult, ALU.mult, [IiB, TtB], [IiB])
                            yield

                        def back(i):
                            t0, L = TILES[i]
                            smp = (i == SAMPLE)
                            xn, xnB = XN[i % 3]
                            yT, yTB = yT2[i % 2]
                            Rr, RrB = Rr2[i % 2]; Ii, IiB = Ii2[i % 2]; Sz, SzB = Sz2[i % 2]
                            if i + 1 < 18:
                                tail_prefetch(i + 1, "mid")
                            if smp:
                                h0, h0B = XT[2]
                                dma_in(h0[0:16, :], st_hl[l, :, :], h0B)
                                b = bk()
                                for kk in range(8):
                                    tr(b[0][:, kk * 16:(kk + 1) * 16], h0[0:16, kk * 128:(kk + 1) * 128], identf[0:16, 0:16], [h0B, cfB], [b[2]], kk == 7)
                                a0 = ap(Rr, 0, 128, 0, [[128, 8], [8, 16]])
                                u0 = ap(Ii, 0, 128, 0, [[128, 8], [8, 16]])
                                tt(tq[:, :, :], a0, ap(b[0], 0, 128, 0, [[16, 8], [1, 16]]), ALU.mult, [RrB, b[2]], [tqB])
                                rel(b)
                                tt(u0, u0, tq[:, :, :], ALU.add, [IiB, tqB], [IiB])
                                mset(a0, 0.0, [RrB])
                                for kk in range(8):
                                    k.op(dve, lambda e: e.tensor_tensor_scan(out=Hh[:, kk, :], data0=Rr[:, kk, :], data1=Ii[:, kk, :], initial=0.0, op0=ALU.mult, op1=ALU.add),
                                         [RrB, IiB], [HhB])
                                cp(tq[:, :, :], ap(Hh, 0, 128, 7, [[128, 8], [8, 16]]), [HhB], [tqB])
                                for n in range(2):
                                    b2 = bk()
                                    for kk in range(4):
                                        tr(b2[0][0:16, kk * 128:(kk + 1) * 128], tq[:, n * 4 + kk, :], identf[:, :], [tqB, cfB], [b2[2]], kk == 3)
                                    cp(ho[0:16, n * 512:(n + 1) * 512], b2[0][0:16, :], [b2[2]], [hoB])
                                    rel(b2)
                                dma_out(s_hl[l, :, :], ho[0:16, :], hoB, final=True)
                            else:
                                for kk in range(8):
                                    k.op(dve, lambda e: e.tensor_tensor_scan(out=Hh[:, kk, :L], data0=Rr[:, kk, :L], data1=Ii[:, kk, :L], initial=hc[:, kk:kk + 1], op0=ALU.mult, op1=ALU.add),
                                         [RrB, IiB, hcB], [HhB])
                                cp(hc[:, :], ap(Hh, 0, 128, L - 1, [[128, 8]]), [HhB], [hcB])
                                if i == 16:
                                    b = bk()
                                    tr(b[0][0:8, 0:128], hc[:, 0:8], identf[:, :], [hcB, cfB], [b[2]], True)
                                    cp(ho[0:8, 0:128], b[0][0:8, 0:128], [b[2]], [hoB])
                                    rel(b)
                                    dma_out(dap(p_hl, l * 1024, [[128, 8], [1, 128]]), ho[0:8, 0:128], hoB, final=True)
                            yield
                            tt(yT[:, :, :L], Hh[:, :, :L], Sz[:, :, :L], ALU.mult, [HhB, SzB], [yTB])
                            yield

                            yield from tail(i, "B", "mid", yT, yTB, xn, xnB, T)

                        tail_prefetch(0, "mid")
                        drive(front, back, "B", ["lx0", "lx1", "lz0", "lz1"], nxt_phase)

                def phaseC(nxt_phase):
                    with ExitStack() as pst:
                        P = lambda n, s, dt=F32, sem=False: sbt(f"c_{n}_{l}", s, dt, sem=sem, stack=pst)
                        P2 = lambda n, s, dt=F32: [P(f"{n}{i_}", s, dt) for i_ in range(2)]
                        xf, xfB = P("xf", [128, 2096], BF16)
                        dg, dgB = P("dg", [128, 64, 128], BF16)
                        tt(dg[:, :, :], ap(cb, 0, 128, 0, [[0, 64], [1, 128]]), ap(ppt, 0, 128, l * NPP + P_SCW, [[1, 64], [0, 128]]), ALU.mult, [cbB, ppB], [dgB])
                        xbcb2 = P2("xbcb", [128, 16, 128], BF16)
                        xdt2 = P2("xdt", [128, 1024], BF16); xw2 = P2("xw", [128, 1024], BF16); xsD2 = P2("xsD", [128, 1024], BF16)
                        btok2 = P2("btok", [128, 512], BF16)
                        Xq, XqB = P("Xq", [128, 4, 128]); Eq, EqB = P("Eq", [128, 4, 128])
                        Wq = [P(f"Wq{i_}", [128, 4, 128], BF16) for i_ in range(2)]
                        Gm2 = P2("Gm", [128, 4, 128], BF16)
                        ST, STB = P("ST", [128, 1024], F32, sem=True); STb, STbB = P("STb", [128, 1024], BF16)
                        yc, ycB = P("yc", [128, 1024], F32, sem=True); sg, sgB = P("sg", [128, 1024])
                        ltb, ltbB = P("ltb", [128, 1024], F32, sem=True)
                        ytok, ytokB = P("ytok", [128, 1024], BF16); yT, yTB = P("yT", [128, 8, 128], BF16)
                        mT, mTB = P("mT", [128, 8, 128], BF16); mT2 = mT.reshape([128, 1024])
                        d0, d0B = P("d0", [128, 32]); dtt, dttB = P("dt", [128, 16]); dta2 = P2("dta", [128, 16])
                        cum, cumB = P("cum", [128, 16]); ecum2 = P2("ecum", [128, 16]); wsx, wsxB = P("wsx", [128, 16]); ecl2 = P2("ecl", [128, 16])
                        aneg, anegB = P("aneg", [128, 16]); ss, ssB = P("ss", [128, 4]); fsm, fsmB = P("fsm", [128, 4])
                        T = {"sg": (sg, sgB), "mrg": (ytok, ytokB), "mT": (mT, mTB), "fsm": (fsm, fsmB)}
                        A(aneg[:, :], prt[:, l, R_ALOG:R_ALOG + 16], AF.Exp, [prB], [anegB])
                        ts(aneg[:, :], aneg[:, :], -1.0, None, ALU.mult, None, [anegB], [anegB])
                        mset(xf[:], 0.0, [xfB]); mset(ST[:], 0.0, [STB]); mset(STb[:], 0.0, [STbB])

                        def front(i):
                            t0, L = TILES[i]
                            smp = (i == SAMPLE)
                            par = i % 2
                            xn, xnB = XN[i % 3]
                            if i + 1 < 18:
                                xn_load(i + 1)
                            xbcb, xbcbB = xbcb2[par]; xdt, xdtB = xdt2[par]; xw, xwB = xw2[par]; xsD, xsDB = xsD2[par]
                            btok, btokB = btok2[par]; Gm, GmB = Gm2[par]; dta, dtaB = dta2[par]; ecum, ecumB = ecum2[par]; ecl, eclB = ecl2[par]
                            C_TRI = C_TRIS if smp else C_TRIP
                            C_ONE = C_SAMES if smp else C_ONES
                            def conv_chunks(ms, xwin, accv):
                                ms = list(ms)
                                for g0 in range(0, len(ms), 4):
                                    grp = ms[g0:g0 + 4]
                                    b = bk("F")
                                    ov = (lambda q: ap(b[0], 0, 128, q * 128, [[8, 16], [1, 8]])) if smp else (lambda q: b[0][:, q * 128:q * 128 + L])
                                    for q, m in enumerate(grp):
                                        for j in range(4):
                                            mm(ov(q), dg[:, m * 4 + j, :], xwin(m, j), j == 0, j == 3, [dgB, xfB], [b[2]], j == 3 and q == len(grp) - 1)
                                    for q, m in enumerate(grp):
                                        A(xbcb[:, m, :L], b[0][:, q * 128:q * 128 + L], AF.Silu, [b[2], ppB], [xbcbB], bias=ppc(P_SCB + m))
                                    rel(b)
                                    yield

                            if smp:
                                accv = lambda a_: ap(a_, 0, 128, 0, [[8, 16], [1, 8]])
                                s48, s48B = ltb, ltbB
                                for hf in range(2):
                                    xwin = lambda m, j, hf=hf: ap(xf, 0, 128, (m - 8 * hf) * 176 + j, [[11, 16], [1, 8]])
                                    dma_in(s48[0:48, :], st_cs[l, :, hf * 1024:(hf + 1) * 1024], s48B)
                                    b = bk("F")
                                    for kk in range(8):
                                        tr(b[0][:, kk * 48:(kk + 1) * 48], s48[0:48, kk * 128:(kk + 1) * 128], identf[0:48, 0:48], [s48B, cfB], [b[2]], kk == 7)
                                    cp(ap(xf, 0, 128, 0, [[176, 8], [11, 16], [1, 3]]), ap(b[0], 0, 128, 0, [[48, 8], [3, 16], [1, 3]]), [b[2]], [xfB])
                                    rel(b)
                                    for n in range(2):
                                        wt_, wB = Wt(l, "C", f"xb{2 * hf + n}")
                                        b = bk("F")
                                        for m in range(4):
                                            for kk in range(8):
                                                mm(b[0][:, m * 128:m * 128 + L], wt_[:, kk, m * 128:(m + 1) * 128], xn[:, kk, :L], kk == 0, kk == 7, [wB, xnB], [b[2]], kk == 7 and m == 3)
                                        A(ap(xf, 0, 128, n * 4 * 176 + 3, [[176, 4], [11, 16], [1, 8]]), ap(b[0], 0, 128, 0, [[128, 4], [8, 16], [1, 8]]), AF.Copy, [b[2]], [xfB])
                                        rel(b)
                                        yield
                                    yield from conv_chunks(range(8 * hf, 8 * hf + 8), xwin, accv)
                            else:
                                xwin = lambda m, j: ap(xf, 0, 128, m * 131 + j, [[1, L]])
                                accv = lambda a_: a_[:, :L]
                                for n in range(4):
                                    wt_, wB = Wt(l, "C", f"xb{n}")
                                    b = bk("F")
                                    for m in range(4):
                                        for kk in range(8):
                                            mm(b[0][:, m * 128:m * 128 + L], wt_[:, kk, m * 128:(m + 1) * 128], xn[:, kk, :L], kk == 0, kk == 7, [wB, xnB], [b[2]], kk == 7 and m == 3)
                                    A(ap(xf, 0, 128, n * 4 * 131 + 3, [[131, 4], [1, L]]), ap(b[0], 0, 128, 0, [[128, 4], [1, L]]), AF.Copy, [b[2]], [xfB])
                                    rel(b)
                                    yield
                            if smp or i == 16:
                                for hf in range(2):
                                    bt_ = [bk("F"), bk("F")]
                                    for n in range(2):
                                        wt_, wB = Wt(l, "C", f"xb{hf * 2 + n}")
                                        for kk in range(8):
                                            mm(bt_[n][0][:L, :], xn[:, kk, :L], wt_[:, kk, :], kk == 0, kk == 7, [xnB, wB], [bt_[n][2]], kk == 7)
                                    for n in range(2):
                                        A(ltb[:L, n * 512:(n + 1) * 512], bt_[n][0][:L, :], AF.Copy, [bt_[n][2]], [ltbB])
                                    rel(*bt_)
                                    if smp:
                                        for r in range(3):
                                            dma_out(dap(s_cs, l * 16 * 3 * 2048 + r * 2048 + hf * 1024, [[3 * 2048, 16], [1, 1024]]), bass.AP(ltb, (5 + r) * 1024, [[8 * 1024, 16], [1, 1024]]), ltbB, final=True)
                                    else:
                                        dma_out(p_cs[l, :, hf * 1024:(hf + 1) * 1024], ltb[13:16, :], ltbB, final=True)
                                    yield
                            bd = bk("F")
                            for kk in range(8):
                                mm(bd[0][:L, 0:16], xn[:, kk, :L], wdt[:, kk, :], kk == 0, kk == 7, [xnB, wdtB], [bd[2]], kk == 7)
                            tt(d0[:L, 0:16], bd[0][:L, 0:16], prc(R_DTB, 16, L), ALU.add, [bd[2], prB], [d0B])
                            A(d0[:L, 16:32], d0[:L, 0:16], AF.Exp, [d0B], [d0B])
                            A(dtt[:L, :], d0[:L, 16:32], AF.Ln, [d0B, cfB], [dttB], bias=cf[:L, C_ONES:C_ONES + 1])
                            tt(dta[:L, :], dtt[:L, :], aneg[:L, :], ALU.mult, [dttB, anegB], [dtaB])
                            yield
                            if not smp:
                                yield from conv_chunks(range(16), xwin, accv)
                                cp(ap(xf, 0, 128, 0, [[131, 16], [1, 3]]), ap(xf, 0, 128, L, [[131, 16], [1, 3]]), [xfB], [xfB])
                            mm(bd[0][:L, 16:32], cfm(C_TRI, L), dta[:L, :], True, True, [cfB, dtaB], [bd[2]], False)
                            mm(bd[0][:, 32:48], cf[:L, C_ONE:C_ONE + 128], dta[:L, :], True, True, [cfB, dtaB], [bd[2]], True)
                            cp(cum[:L, :], bd[0][:L, 16:32], [bd[2]], [cumB])
                            A(ecum[:L, :], cum[:L, :], AF.Exp, [cumB], [ecumB])
                            tt(wsx[:L, :], bd[0][:L, 32:48], cum[:L, :], ALU.subtract, [bd[2], cumB], [wsxB])
                            A(wsx[:L, :], wsx[:L, :], AF.Exp, [wsxB], [wsxB])
                            tt(wsx[:L, :], wsx[:L, :], dtt[:L, :], ALU.mult, [wsxB, dttB], [wsxB])
                            A(ecl[:, :], bd[0][:, 32:48], AF.Exp, [bd[2]], [eclB])
                            rel(bd)
                            yield
                            pxs = bk("F")
                            for m in range(8):
                                tr(pxs[1][:L, m * 128:(m + 1) * 128], xbcb[:, m, :L], identb[:, :], [xbcbB, cbB], [pxs[2]], m == 7)
                            pB_ = bk("F")
                            for g in range(4):
                                tr(pB_[1][:L, g * 128:(g + 1) * 128], xbcb[:, 8 + g, :L], identb[:, :], [xbcbB, cbB], [pB_[2]], g == 3)
                            cp(btok[:L, :], pB_[1][:L, 0:512], [pB_[2]], [btokB])
                            rel(pB_)
                            xsv = ap(pxs[1], 0, L, 0, [[64, 16], [1, 64]])
                            o3 = lambda t_: ap(t_, 0, L, 0, [[64, 16], [1, 64]])
                            tt(o3(xdt), xsv, ap(dtt, 0, L, 0, [[1, 16], [0, 64]]), ALU.mult, [pxs[2], dttB], [xdtB])
                            yield
                            tt(o3(xw), xsv, ap(wsx, 0, L, 0, [[1, 16], [0, 64]]), ALU.mult, [pxs[2], wsxB], [xwB])
                            tt(o3(xsD), xsv, ap(prt, 0, L, l * NPR + R_DD, [[1, 16], [0, 64]]), ALU.mult, [pxs[2], prB], [xsDB])
                            rel(pxs)
                            bgm = bk("F")
                            for g in range(4):
                                mm(bgm[0][:L, g * 128:g * 128 + L], xbcb[:, 8 + g, :L], xbcb[:, 12 + g, :L], True, True, [xbcbB], [bgm[2]], g == 3)
                            tt(Gm[:L, :, :L], ap(bgm[0], 0, L, 0, [[128, 4], [1, L]]), ap(cf, 0, L, C_TRI, [[0, 4], [1, L]]), ALU.mult, [bgm[2], cfB], [GmB])
                            rel(bgm)
                            yield

                        def back(i):
                            t0, L = TILES[i]
                            smp = (i == SAMPLE)
                            par = i % 2
                            xn, xnB = XN[i % 3]
                            if i + 1 < 18:
                                tail_prefetch(i + 1, "last")
                            xbcb, xbcbB = xbcb2[par]; xdt, xdtB = xdt2[par]; xw, xwB = xw2[par]; xsD, xsDB = xsD2[par]
                            btok, btokB = btok2[par]; Gm, GmB = Gm2[par]; dta, dtaB = dta2[par]; ecum, ecumB = ecum2[par]; ecl, eclB = ecl2[par]
                            C_TRI = C_TRIS if smp else C_TRIP
                            C_SGT = C_SGTS if smp else C_SGTP
                            if not smp:
                                bi = [bk(), bk()]
                                for g in range(4):
                                    mm(bi[g // 2][0][:L, (g % 2) * 256:(g % 2 + 1) * 256], xbcb[:, 12 + g, :L], STb[:, g * 256:(g + 1) * 256], True, True,
                                       [xbcbB, STbB], [bi[g // 2][2]], g % 2 == 1)
                            else:
                                rb = [P(f"rb{b_}", [128, 16, 8]) for b_ in range(2)]
                                eclP, eclPB = P("eclP", [128, 128])
                                for b_ in range(2):
                                    tt(rb[b_][0][:, :, :], ap(cf, 0, 128, C_ROWSEL, [[1, 16], [0, 8]]), ap(dta, 0, 128, b_, [[0, 16], [2, 8]]), ALU.mult, [cfB, dtaB], [rb[b_][1]])
                                b = bk()
                                mm(b[0][:, 0:128], cf[:, C_HALF0:C_HALF0 + 128], rb[0][0][:].rearrange("p a b -> p (a b)"), True, False, [cfB, rb[0][1]], [b[2]], False)
                                mm(b[0][:, 0:128], cf[:, C_HALF1:C_HALF1 + 128], rb[1][0][:].rearrange("p a b -> p (a b)"), False, True, [cfB, rb[1][1]], [b[2]], True)
                                A(eclP[:, :], b[0][:, 0:128], AF.Exp, [b[2]], [eclPB])
                                rel(b)
                                xwj = [(ytok, ytokB), (mT2, mTB)]
                                so = [(ST, STB), (yc, ycB)]
                                byT = [bk(), bk()]
                                stg = [XT[0], XT[2], (ltb, ltbB)]

                                def ld(j):
                                    s_, sB_ = stg[j % 3]
                                    dma_in(ap(s_, 0, 128, 0, [[128, 8], [1, 128]]), dap(st_hs, (l * 16 + j) * 1024 * 128, [[128, 128], [128 * 128, 8], [1, 128]]), sB_)

                                ld(0)
                                ld(1)
                                for j in range(16):
                                    if j + 2 < 16:
                                        ld(j + 2)
                                    sg_, sgB_ = stg[j % 3]
                                    b2 = [bk("F"), bk("F")]
                                    for c in range(8):
                                        tr(b2[c // 4][0][:, (c % 4) * 128:(c % 4 + 1) * 128], sg_[:, c * 128:(c + 1) * 128], identf[:, :], [sgB_, cfB], [b2[c // 4][2]], c % 4 == 3)
                                    for n in range(2):
                                        A(STb[:, n * 512:(n + 1) * 512], b2[n][0][:, :], AF.Copy, [b2[n][2]], [STbB])
                                    rel(*b2)
                                    for c in range(8):
                                        mm(byT[c // 4][0][:, (c % 4) * 128 + 8 * j:(c % 4) * 128 + 8 * j + 8], STb[:, c * 128:(c + 1) * 128], xbcb[:, 12 + c // 2, 8 * j:8 * j + 8],
                                           True, True, [STbB, xbcbB], [byT[c // 4][2]], c % 4 == 3)
                                    xj, xjB = xwj[j % 2]
                                    A(xj[:, :], xw[:, :], AF.Identity, [xwB, cfB], [xjB], scale=cf[:, C_ROWSEL + j:C_ROWSEL + j + 1])
                                    b3 = [bk(), bk()]
                                    for c in range(8):
                                        g = c // 2
                                        mm(b3[c // 4][0][:, (c % 4) * 128:(c % 4 + 1) * 128], xj[:, c * 128:(c + 1) * 128], btok[:, g * 128:(g + 1) * 128], True, True,
                                           [xjB, btokB], [b3[c // 4][2]], c % 4 == 3)
                                    so_, soB = so[j % 2]
                                    tt(ap(so_, 0, 128, 0, [[128, 8], [1, 128]]), ap(sg_, 0, 128, 0, [[128, 8], [1, 128]]), ap(eclP, 0, 128, j * 8, [[1, 8], [0, 128]]), ALU.mult,
                                       [sgB_, eclPB], [soB])
                                    for n in range(2):
                                        tt(so_[:, n * 512:(n + 1) * 512], so_[:, n * 512:(n + 1) * 512], b3[n][0][:, :], ALU.add, [soB, b3[n][2]], [soB])
                                    rel(*b3)
                                    dma_out(dap(s_hs, (l * 16 + j) * 1024 * 128, [[128, 128], [128 * 128, 8], [1, 128]]), ap(so_, 0, 128, 0, [[128, 8], [1, 128]]), soB, final=True)
                                    yield
                                for n in range(2):
                                    A(yc[:, n * 512:(n + 1) * 512], byT[n][0][:, :], AF.Copy, [byT[n][2]], [ycB])
                                rel(*byT)
                                bi = [bk(), bk()]
                                for c in range(8):
                                    tr(bi[c // 4][0][:, (c % 4) * 128:(c % 4 + 1) * 128], yc[:, c * 128:(c + 1) * 128], identf[:, :], [ycB, cfB], [bi[c // 4][2]], c % 4 == 3)
                            for n in range(2):
                                tt(ap(yc, 0, L, n * 512, [[64, 8], [1, 64]]), ap(bi[n][0], 0, L, 0, [[64, 8], [1, 64]]), ap(ecum, 0, L, n * 8, [[1, 8], [0, 64]]), ALU.mult,
                                   [bi[n][2], ecumB], [ycB])
                            rel(*bi)
                            yield
                            by = [bk(), bk()]
                            for q in range(4):
                                tt(Xq[:L, :, :L], ap(cf, 0, L, C_TRI, [[0, 4], [1, L]]), ap(dta, 0, L, 4 * q, [[1, 4], [0, L]]), ALU.mult, [cfB, dtaB], [XqB])
                                bsg = bk()
                                mm(ap(bsg[0], 0, L, 0, [[128, 4], [1, L]]), cfm(C_SGT, L), Xq[:L, :, :L], True, True, [cfB, XqB], [bsg[2]], True)
                                A(Eq[:L, :, :L], ap(bsg[0], 0, L, 0, [[128, 4], [1, L]]), AF.Exp, [bsg[2]], [EqB])
                                rel(bsg)
                                wq_, wqB = Wq[q % 2]
                                tt(wq_[:L, :, :L], Eq[:L, :, :L], ap(Gm, 0, L, q * 128, [[0, 4], [1, L]]), ALU.mult, [EqB, GmB], [wqB])
                                for e_ in range(4):
                                    h = 4 * q + e_
                                    o_ = by[h // 8][0][:L, (h % 8) * 64:(h % 8 + 1) * 64]
                                    mm(o_, wq_[:L, e_, :L], xdt[:L, h * 64:(h + 1) * 64], True, False, [wqB, xdtB], [by[h // 8][2]], False)
                                    mm(o_, identb[:L, :L], xsD[:L, h * 64:(h + 1) * 64], False, True, [cbB, xsDB], [by[h // 8][2]], True)
                                yield
                            for n in range(2):
                                tt(yc[:L, n * 512:(n + 1) * 512], yc[:L, n * 512:(n + 1) * 512], by[n][0][:L, :], ALU.add, [ycB, by[n][2]], [ycB])
                            rel(*by)
                            bz = [bk(), bk()]
                            for n in range(2):
                                wt_, wB = Wt(l, "C", f"sz{n}")
                                for kk in range(8):
                                    mm(bz[n][0][:L, :], xn[:, kk, :L], wt_[:, kk, :], kk == 0, kk == 7, [xnB, wB], [bz[n][2]], kk == 7)
                            for n in range(2):
                                A(sg[:L, n * 512:(n + 1) * 512], bz[n][0][:L, :], AF.Silu, [bz[n][2]], [sgB])
                            rel(*bz)
                            yield
                            tt(yc[:L, :], yc[:L, :], sg[:L, :], ALU.mult, [ycB, sgB], [ycB])
                            A(ytok[:L, :], yc[:L, :], AF.Square, [ycB], [ytokB, ssB], accum=ss[:L, 0:1])
                            A(ss[:L, 1:2], ss[:L, 0:1], AF.Ln, [ssB, cfB], [ssB], scale=1.0 / 1024, bias=cf[:L, C_EPS:C_EPS + 1])
                            A(ss[:L, 3:4], ss[:L, 1:2], AF.Exp, [ssB], [ssB], scale=-0.5)
                            A(ytok[:L, :], yc[:L, :], AF.Identity, [ycB, ssB], [ytokB], scale=ss[:L, 3:4])
                            yield
                            pt = bk()
                            for j in range(8):
                                tr(pt[1][:, j * 128:j * 128 + L], ytok[:L, j * 128:(j + 1) * 128], identb[:L, :L], [ytokB, cbB], [pt[2]], j == 7)
                            tt(yT[:, :, :L], ap(pt[1], 0, 128, 0, [[128, 8], [1, L]]), ppbc(P_SG, 8, L), ALU.mult, [pt[2], ppB], [yTB])
                            rel(pt)
                            if not smp:
                                bu = [bk(), bk()]
                                for g in range(4):
                                    mm(bu[g // 2][0][:, (g % 2) * 256:(g % 2 + 1) * 256], btok[:L, g * 128:(g + 1) * 128], xw[:L, g * 256:(g + 1) * 256], True, True,
                                       [btokB, xwB], [bu[g // 2][2]], g % 2 == 1)
                                tt(ap(ST, 0, 128, 0, [[64, 16], [1, 64]]), ap(ST, 0, 128, 0, [[64, 16], [1, 64]]), ap(ecl, 0, 128, 0, [[1, 16], [0, 64]]), ALU.mult, [STB, eclB], [STB])
                                for n in range(2):
                                    tt(ST[:, n * 512:(n + 1) * 512], ST[:, n * 512:(n + 1) * 512], bu[n][0][:, :], ALU.add, [STB, bu[n][2]], [STB])
                                rel(*bu)
                                A(STb[:, :], ST[:, :], AF.Copy, [STB], [STbB])
                            yield
                            yield from tail(i, "C", "last", yT, yTB, xn, xnB, T)
                            if i == 16:
                                so_, soB = yc, ycB
                                b2 = [bk(), bk()]
                                for c in range(8):
                                    tr(b2[c // 4][0][:, (c % 4) * 128:(c % 4 + 1) * 128], ST[:, c * 128:(c + 1) * 128], identf[:, :], [STB, cfB], [b2[c // 4][2]], c % 4 == 3)
                                for n in range(2):
                                    cp(so_[:, n * 512:(n + 1) * 512], b2[n][0][:, :], [b2[n][2]], [soB])
                                rel(*b2)
                                dma_out(dap(p_hs, l * 1024 * 128, [[128, 128], [128 * 128, 8], [1, 128]]), ap(so_, 0, 128, 0, [[128, 8], [1, 128]]), soB, final=True)

                        tail_prefetch(0, "last")
                        drive(front, back, "C", ["xb0", "xb1", "xb2", "xb3"], nxt_phase)

                PH = {"A": phaseA, "B": phaseB, "C": phaseC}
                for ph in "ABC":
                    if (l, ph) not in phases:
                        continue
                    ok = load_weights(l, ph)
                    assert ok, "not enough weight slots"
                    nxt = phases.index((l, ph)) + 1
                    if nxt < len(phases):
                        load_weights(*phases[nxt])
                    PH[ph](phases[nxt] if nxt < len(phases) else None)
                    release_weights(l, ph)

            for s in k.dram_out_sems:
                sp.h.wait_ge(s.h, s.val)
        print("instr counts", {e.name: (e.nins, e.nwait) for e in (pe, act, dve, pool, sp)}, "sems", k.nsem)
    return nc


def _consts():
    r = np.arange(128)
    seq = r // 8
    same = (seq[:, None] == seq[None, :]).astype(np.float32)
    cfa = np.zeros((128, NCF), np.float32)
    cfa[:, C_IDF:C_IDF + 128] = np.eye(128)
    cfa[:, C_TRIP:C_TRIP + 128] = (r[:, None] <= r[None, :])
    cfa[:, C_SGTP:C_SGTP + 128] = (r[:, None] > r[None, :])
    cfa[:, C_ONES:C_ONES + 128] = 1.0
    cfa[:, C_TRIS:C_TRIS + 128] = (r[:, None] <= r[None, :]) * same
    cfa[:, C_SGTS:C_SGTS + 128] = (r[:, None] > r[None, :]) * same
    cfa[:, C_SAMES:C_SAMES + 128] = same
    cfa[:, C_ROWSEL:C_ROWSEL + 16] = (seq[:, None] == np.arange(16)[None, :])
    cfa[:, C_HALF0:C_HALF0 + 128] = (r[None, :] // 64 == 0)
    cfa[:, C_HALF1:C_HALF1 + 128] = (r[None, :] // 64 == 1)
    cfa[:, C_EPS] = EPS
    cba = np.zeros((128, NCB), np.float32)
    cba[:, B_IDB:B_IDB + 128] = np.eye(128)
    colsel = (np.arange(16)[:, None] == seq[None, :]).astype(np.float32).reshape(1, 2048)
    cba[:, B_COLSEL:B_COLSEL + 2048] = colsel
    return cfa, cba.astype(ml_dtypes.bfloat16)


_PROG = {}


def kernel(x_prompt, x_sample, state_mlstm_c, state_mlstm_n, state_mlstm_m, state_rglru_h,
           state_rglru_conv, state_ssd_h, state_ssd_conv, meta_tokens, w_in, norm_g, ml_f_bias,
           ml_norm_g, lru_conv_w, lru_conv_b, lru_w_a, lru_b_a, lru_w_x, lru_b_x, lru_lambda,
           ssd_conv_w, ssd_conv_b, ssd_dt_bias, ssd_a_log, ssd_d, ssd_norm_g,
           w_br_ml, w_br_lru, w_br_ssd, w_out, final_norm_g, _cfg=None):
    f = lambda a: np.ascontiguousarray(np.asarray(a, dtype=np.float32))
    x_prompt, x_sample = f(x_prompt), f(x_sample)
    w_in_, w_out_ = f(w_in), f(w_out)
    w_br_ = np.ascontiguousarray(np.stack([f(w_br_ml), f(w_br_lru), f(w_br_ssd)], axis=1))
    wbd = np.zeros((2, 2, 8, 128, 128), np.float32)
    for a_, w_ in enumerate((f(lru_w_a), f(lru_w_x))):
        for n in range(16):
            o = (n % 2) * 64
            wbd[:, a_, n // 2, o:o + 64, o:o + 64] = w_[:, n]
    pp = np.zeros((2, 128, NPP), np.float32)
    pr = np.zeros((2, 128, NPR), np.float32)
    col = lambda v, n: f(v).reshape(2, n, 128).transpose(0, 2, 1)
    pp[:, :, P_NG:P_NG + 8] = col(norm_g, 8)
    pp[:, :, P_MG:P_MG + 8] = col(f(ml_norm_g).reshape(2, 1024), 8)
    pp[:, :, P_SG:P_SG + 8] = col(ssd_norm_g, 8)
    pp[:, :, P_LCW:P_LCW + 32] = f(lru_conv_w).reshape(2, 4, 8, 128).transpose(0, 3, 2, 1).reshape(2, 128, 32)
    pp[:, :, P_LCB:P_LCB + 8] = col(lru_conv_b, 8)
    pp[:, :, P_LAM:P_LAM + 8] = col(lru_lambda, 8)
    pp[:, :, P_SCW:P_SCW + 64] = f(ssd_conv_w).reshape(2, 4, 16, 128).transpose(0, 3, 2, 1).reshape(2, 128, 64)
    pp[:, :, P_SCB:P_SCB + 16] = col(ssd_conv_b, 16)
    pp[:, :, P_LBA:P_LBA + 8] = col(lru_b_a, 8)
    pp[:, :, P_LBX:P_LBX + 8] = col(lru_b_x, 8)
    pr[:, :, R_FB:R_FB + 4] = f(ml_f_bias)[:, None, :]
    pr[:, :, R_DTB:R_DTB + 16] = f(ssd_dt_bias)[:, None, :]
    pr[:, :, R_ALOG:R_ALOG + 16] = f(ssd_a_log)[:, None, :]
    pr[:, :, R_DD:R_DD + 16] = f(ssd_d)[:, None, :]
    fgr = np.ascontiguousarray(np.broadcast_to(f(final_norm_g)[None, :], (128, 1024)))
    cfa, cba = _consts()
    meta = f(meta_tokens)
    smc, smn, smm = f(state_mlstm_c), f(state_mlstm_n), f(state_mlstm_m)
    shl, scl, shs, scs = f(state_rglru_h), f(state_rglru_conv), f(state_ssd_h), f(state_ssd_conv)
    in_maps = []
    for c in range(8):
        sl = slice(16 * c, 16 * c + 16)
        in_maps.append({
            "xp": x_prompt[c], "meta": meta, "xs": x_sample[sl].reshape(128, 1024),
            "st_c": np.ascontiguousarray(smc[:, sl]), "st_n": np.ascontiguousarray(smn[:, sl]).reshape(2, 64, 128),
            "st_m": np.ascontiguousarray(smm[:, sl]), "st_hl": np.ascontiguousarray(shl[:, sl]),
            "st_cl": np.ascontiguousarray(scl[:, sl]).reshape(2, 48, 1024),
            "st_hs": np.ascontiguousarray(shs[:, sl]).reshape(2, 16, 1024, 128),
            "st_cs": np.ascontiguousarray(scs[:, sl]).reshape(2, 48, 2048),
            "w_in": w_in_, "w_br": w_br_, "w_out": w_out_, "wbd": wbd, "pp": pp, "pr": pr, "fg": fgr, "cf": cfa, "cb": cba,
        })
    key = repr(_cfg)
    if key not in _PROG:
        _PROG[key] = build_program(_cfg)
    nc = _PROG[key]
    res = run_bass_kernel_spmd(nc, in_maps, core_ids=list(range(8)))
    R = res.results
    cat = lambda n, ax: np.concatenate([np.asarray(r[n]) for r in R], axis=ax)
    y_prompt = np.stack([np.asarray(r["y_p"]) for r in R], axis=0)
    y_sample = cat("y_s", 0).reshape(128, 8, 1024)
    stk = lambda n: np.stack([np.asarray(r[n]) for r in R], axis=1)
    p_c = stk("p_c"); p_n = stk("p_n"); p_m = stk("p_m"); p_hl = stk("p_hl"); p_cl = stk("p_cl")
    p_hs = stk("p_hs").reshape(2, 8, 16, 64, 128); p_cs = stk("p_cs")
    s_c = cat("s_c", 1); s_n = cat("s_n", 1).reshape(2, 128, 4, 128); s_m = cat("s_m", 1)
    s_hl = cat("s_hl", 1); s_cl = cat("s_cl", 1); s_hs = cat("s_hs", 1).reshape(2, 128, 16, 64, 128); s_cs = cat("s_cs", 1)
    outs = (y_prompt, y_sample, p_c, p_n, p_m, p_hl, p_cl, p_hs, p_cs, s_c, s_n, s_m, s_hl, s_cl, s_hs, s_cs)
    return tuple(np.ascontiguousarray(o, dtype=np.float32) for o in outs)
```
